# Optimizing a Trainium2 kernel written in Bass

```python
import math
import jax
import jax.numpy as jnp
from jax import lax
import numpy as np


D_MODEL = 1024
BATCH = 16
SEQ = 4096
DEPTH = 2

HEAD_DIM = 64
SB_HEADS = 6
RET_HEADS = 6
SB_WIDTH = SB_HEADS * HEAD_DIM
RET_WIDTH = RET_HEADS * HEAD_DIM
S5_WIDTH = D_MODEL - SB_WIDTH - RET_WIDTH
S5_GROUP_CH = 16
S5_GROUPS = S5_WIDTH // S5_GROUP_CH
S5_STATE = 64
D_MIX = SB_WIDTH + RET_WIDTH + S5_WIDTH
D_IN = 3 * SB_WIDTH + 4 * RET_WIDTH + S5_WIDTH
D_FF = ((8 * D_MODEL + 3 * 256 - 1) // (3 * 256)) * 256
CHUNK = 128
N_META = 16
PAD_FRONT = CHUNK - N_META
ROPE_BASE = 10000.0
RMS_EPS = 1e-6

kernel_name = "hymba_sb_retention_s5_hybrid"


def rmsnorm(x, g):
    x32 = x.astype(jnp.float32)
    y = x32 * lax.rsqrt(jnp.mean(x32 * x32, axis=-1, keepdims=True) + RMS_EPS)
    return y.astype(x.dtype) * g


def to_heads(t, n_heads):
    b, l, _ = t.shape
    return t.reshape(b, l, n_heads, HEAD_DIM).transpose(0, 2, 1, 3)


def from_heads(t):
    b, h, l, d = t.shape
    return t.transpose(0, 2, 1, 3).reshape(b, l, h * d)


def rotary(t, pos):
    half = HEAD_DIM // 2
    inv = ROPE_BASE ** (-jnp.arange(half, dtype=jnp.float32) / half)
    ang = pos[:, None] * inv[None, :]
    cos = jnp.cos(ang).astype(t.dtype)
    sin = jnp.sin(ang).astype(t.dtype)
    t1, t2 = t[..., :half], t[..., half:]
    return jnp.concatenate([t1 * cos - t2 * sin, t1 * sin + t2 * cos], axis=-1)


def stick_breaking(q, k, v):
    L = q.shape[2]
    scale = HEAD_DIM ** -0.5
    outs = []
    for blk in range(L // CHUNK):
        end = (blk + 1) * CHUNK
        qb = q[:, :, blk * CHUNK:end]
        kb = k[:, :, :end]
        vb = v[:, :, :end]
        z = jnp.einsum('bhqd,bhkd->bhqk', qb, kb).astype(jnp.float32) * scale
        t_idx = blk * CHUNK + jnp.arange(CHUNK)
        s_idx = jnp.arange(end)
        mask = (s_idx[None, :] < t_idx[:, None]) & (s_idx[None, :] >= PAD_FRONT)
        log1m = jnp.where(mask, jax.nn.log_sigmoid(-z), 0.0)
        between = lax.cumsum(log1m, axis=3, reverse=True) - log1m
        w = jnp.where(mask, jnp.exp(jax.nn.log_sigmoid(z) + between), 0.0)
        outs.append(jnp.einsum('bhqk,bhkd->bhqd', w.astype(vb.dtype), vb))
    return jnp.concatenate(outs, axis=2)


def retention(q, k, v, log_gamma):
    b, h, L, d = q.shape
    n = L // CHUNK
    k = k * (d ** -0.5)
    qc = q.reshape(b, h, n, CHUNK, d)
    kc = k.reshape(b, h, n, CHUNK, d)
    vc = v.reshape(b, h, n, CHUNK, d)
    i = jnp.arange(CHUNK, dtype=jnp.float32)
    diff = i[:, None] - i[None, :]
    decay_in = jnp.where(diff >= 0, jnp.exp(log_gamma[:, None, None] * jnp.maximum(diff, 0.0)), 0.0)
    scores = jnp.einsum('bhnid,bhnjd->bhnij', qc, kc) * decay_in[None, :, None]
    inner = jnp.einsum('bhnij,bhnje->bhnie', scores, vc.astype(jnp.float32))
    q_decay = jnp.exp(log_gamma[:, None] * (i + 1.0))
    k_decay = jnp.exp(log_gamma[:, None] * (CHUNK - 1.0 - i))
    chunk_decay = jnp.exp(log_gamma * CHUNK)
    xs = (jnp.moveaxis(qc.astype(jnp.float32) * q_decay[None, :, None, :, None], 2, 0),
          jnp.moveaxis(kc.astype(jnp.float32) * k_decay[None, :, None, :, None], 2, 0),
          jnp.moveaxis(vc.astype(jnp.float32), 2, 0))

    def step(state, inp):
        qn, kn, vn = inp
        cross = jnp.einsum('bhid,bhde->bhie', qn, state)
        state = state * chunk_decay[None, :, None, None] + jnp.einsum('bhjd,bhje->bhde', kn, vn)
        return state, cross

    s0 = jnp.zeros((b, h, d, d), jnp.float32)
    _, cross = lax.scan(step, s0, xs)
    o = inner + jnp.moveaxis(cross, 0, 2)
    return o.reshape(b, h, L, d)


def _complex_affine_combine(e1, e2):
    a1r, a1i, b1r, b1i = e1
    a2r, a2i, b2r, b2i = e2
    return (a2r * a1r - a2i * a1i,
            a2r * a1i + a2i * a1r,
            a2r * b1r - a2i * b1i + b2r,
            a2r * b1i + a2i * b1r + b2i)


def s5_mixer(u, lam_re, lam_im, log_dt, b_re, b_im, c_re, c_im, d_skip, w_glu):
    bsz, L, _ = u.shape
    f32 = jnp.float32
    u32 = u.astype(f32)
    ug = jnp.transpose(u32.reshape(bsz, L, S5_GROUPS, S5_GROUP_CH), (1, 0, 2, 3))
    lr = lam_re.astype(f32)
    li = lam_im.astype(f32)
    dt = jnp.exp(log_dt.astype(f32))[:, None]
    mag = jnp.exp(lr * dt)
    ar = mag * jnp.cos(li * dt)
    ai = mag * jnp.sin(li * dt)
    den = lr * lr + li * li
    fr = ((ar - 1.0) * lr + ai * li) / den
    fi = (ai * lr - (ar - 1.0) * li) / den
    br = b_re.astype(f32)
    bi = b_im.astype(f32)
    bbr = fr[..., None] * br - fi[..., None] * bi
    bbi = fr[..., None] * bi + fi[..., None] * br
    bu_r = jnp.einsum('lbgc,gpc->lbgp', ug, bbr)
    bu_i = jnp.einsum('lbgc,gpc->lbgp', ug, bbi)
    a_r = jnp.broadcast_to(ar[None, None], (L, 1, S5_GROUPS, S5_STATE))
    a_i = jnp.broadcast_to(ai[None, None], (L, 1, S5_GROUPS, S5_STATE))
    _, _, xr, xi = lax.associative_scan(_complex_affine_combine, (a_r, a_i, bu_r, bu_i), axis=0)
    y = (jnp.einsum('lbgp,gcp->lbgc', xr, c_re.astype(f32))
         - jnp.einsum('lbgp,gcp->lbgc', xi, c_im.astype(f32)))
    y = jnp.transpose(y, (1, 0, 2, 3)).reshape(bsz, L, S5_WIDTH) + d_skip.astype(f32) * u32
    y = jax.nn.gelu(y)
    return (y * jax.nn.sigmoid(y @ w_glu.astype(f32))).astype(u.dtype)


def setup_inputs(seed: int = 0) -> dict:
    key = jax.random.key(seed)
    ks = jax.random.split(key, 24)
    f32 = jnp.float32

    def nrm(k, shape, scale):
        return scale * jax.random.normal(k, shape, f32)

    return {
        'x': nrm(ks[0], (BATCH, SEQ, D_MODEL), 1.0),
        'meta_tokens': nrm(ks[1], (N_META, D_MODEL), 1.0),
        'norm1_g': 1.0 + nrm(ks[2], (DEPTH, D_MODEL), 0.02),
        'w_in': nrm(ks[3], (DEPTH, D_MODEL, D_IN), D_MODEL ** -0.5),
        'sb_q_g': 1.0 + nrm(ks[4], (DEPTH, HEAD_DIM), 0.02),
        'sb_k_g': 1.0 + nrm(ks[5], (DEPTH, HEAD_DIM), 0.02),
        'ret_q_g': 1.0 + nrm(ks[6], (DEPTH, HEAD_DIM), 0.02),
        'ret_k_g': 1.0 + nrm(ks[7], (DEPTH, HEAD_DIM), 0.02),
        'ret_out_g': 1.0 + nrm(ks[8], (DEPTH, RET_WIDTH), 0.02),
        's5_lam_re': -0.5 + nrm(ks[9], (DEPTH, S5_GROUPS, S5_STATE), 0.01),
        's5_lam_im': math.pi * jnp.arange(S5_STATE, dtype=f32) + nrm(ks[10], (DEPTH, S5_GROUPS, S5_STATE), 0.01),
        's5_log_dt': jax.random.uniform(ks[11], (DEPTH, S5_GROUPS), f32, math.log(1e-3), math.log(1e-1)),
        's5_b_re': nrm(ks[12], (DEPTH, S5_GROUPS, S5_STATE, S5_GROUP_CH), (2 * S5_GROUP_CH) ** -0.5),
        's5_b_im': nrm(ks[13], (DEPTH, S5_GROUPS, S5_STATE, S5_GROUP_CH), (2 * S5_GROUP_CH) ** -0.5),
        's5_c_re': nrm(ks[14], (DEPTH, S5_GROUPS, S5_GROUP_CH, S5_STATE), (2 * S5_STATE) ** -0.5),
        's5_c_im': nrm(ks[15], (DEPTH, S5_GROUPS, S5_GROUP_CH, S5_STATE), (2 * S5_STATE) ** -0.5),
        's5_d': nrm(ks[16], (DEPTH, S5_WIDTH), 1.0),
        's5_w_glu': nrm(ks[17], (DEPTH, S5_WIDTH, S5_WIDTH), S5_WIDTH ** -0.5),
        'w_out': nrm(ks[18], (DEPTH, D_MIX, D_MODEL), D_MIX ** -0.5),
        'norm2_g': 1.0 + nrm(ks[19], (DEPTH, D_MODEL), 0.02),
        'w_gate': nrm(ks[20], (DEPTH, D_MODEL, D_FF), D_MODEL ** -0.5),
        'w_up': nrm(ks[21], (DEPTH, D_MODEL, D_FF), D_MODEL ** -0.5),
        'w_down': nrm(ks[22], (DEPTH, D_FF, D_MODEL), D_FF ** -0.5),
    }


def reference(x, meta_tokens, norm1_g, w_in, sb_q_g, sb_k_g, ret_q_g, ret_k_g, ret_out_g,
              s5_lam_re, s5_lam_im, s5_log_dt, s5_b_re, s5_b_im, s5_c_re, s5_c_im, s5_d, s5_w_glu,
              w_out, norm2_g, w_gate, w_up, w_down):
    bsz = x.shape[0]
    L = PAD_FRONT + N_META + x.shape[1]
    pad = jnp.zeros((bsz, PAD_FRONT, D_MODEL), x.dtype)
    meta = jnp.broadcast_to(meta_tokens[None].astype(x.dtype), (bsz, N_META, D_MODEL))
    h = jnp.concatenate([pad, meta, x], axis=1)
    idx = jnp.arange(L)
    valid = (idx >= PAD_FRONT).astype(x.dtype)[None, :, None]
    pos = (idx - PAD_FRONT).astype(jnp.float32)
    log_gamma = jnp.log1p(-jnp.exp2(-5.0 - jnp.arange(RET_HEADS, dtype=jnp.float32)))
    splits = [SB_WIDTH, 2 * SB_WIDTH, 3 * SB_WIDTH,
              3 * SB_WIDTH + RET_WIDTH, 3 * SB_WIDTH + 2 * RET_WIDTH,
              3 * SB_WIDTH + 3 * RET_WIDTH, 3 * SB_WIDTH + 4 * RET_WIDTH]
    for l in range(DEPTH):
        hn = rmsnorm(h, norm1_g[l])
        proj = hn @ w_in[l]
        sq, sk, sv, rq, rk, rv, rg, u = jnp.split(proj, splits, axis=-1)
        q = rmsnorm(to_heads(sq, SB_HEADS), sb_q_g[l])
        k = rmsnorm(to_heads(sk, SB_HEADS), sb_k_g[l])
        sb_o = from_heads(stick_breaking(q, k, to_heads(sv, SB_HEADS)))
        q = rotary(rmsnorm(to_heads(rq, RET_HEADS), ret_q_g[l]), pos)
        k = rotary(rmsnorm(to_heads(rk, RET_HEADS), ret_k_g[l]), pos)
        ro = retention(q, k, to_heads(rv, RET_HEADS), log_gamma).transpose(0, 2, 1, 3)
        ro = rmsnorm(ro, ret_out_g[l].reshape(RET_HEADS, HEAD_DIM)).reshape(bsz, L, RET_WIDTH)
        ro = ro * jax.nn.silu(rg)
        so = s5_mixer(u, s5_lam_re[l], s5_lam_im[l], s5_log_dt[l], s5_b_re[l], s5_b_im[l],
                      s5_c_re[l], s5_c_im[l], s5_d[l], s5_w_glu[l])
        mix = jnp.concatenate([sb_o.astype(h.dtype), ro.astype(h.dtype), so.astype(h.dtype)], axis=-1)
        h = h + (mix @ w_out[l]).astype(h.dtype)
        hn = rmsnorm(h, norm2_g[l])
        h = h + ((jax.nn.silu(hn @ w_gate[l]) * (hn @ w_up[l])) @ w_down[l]).astype(h.dtype)
        h = h * valid
    return h[:, PAD_FRONT + N_META:]
```

```python
import math
import os
FUSE_IN = int(os.environ.get('FUSE_IN', '1'))
FUSE_OUT = int(os.environ.get('FUSE_OUT', '1'))
INTERLEAVE = int(os.environ.get('INTERLEAVE', '1'))
GN = int(os.environ.get('GN', '3'))
SKIPDVE = int(os.environ.get('SKIPDVE', '0'))
SKIPMM = int(os.environ.get('SKIPMM', '0'))
SCB = int(os.environ.get('SCB', '3'))
RET_STOP = int(os.environ.get('RET_STOP', '99'))
from contextlib import ExitStack

import numpy as np
import ml_dtypes

import concourse.bass as bass
import concourse.mybir as mybir
from concourse.bass_utils import run_bass_kernel_spmd

F32 = mybir.dt.float32
BF16 = mybir.dt.bfloat16
AF = mybir.ActivationFunctionType
ALU = mybir.AluOpType

D = 1024
HD = 64
SBW = 384
RETW = 384
S5W = 256
D_IN = 2944
D_FF = 2816
NFF = D_FF // 128
CH = 128
N_META = 16
PAD_FRONT = 112
EPS = 1e-6
C_SQ, C_SK, C_SV, C_RQ, C_RK, C_RV, C_RG, C_U = 0, 384, 768, 1152, 1536, 1920, 2304, 2688


class T:
    def __init__(self, h):
        self.h = h
        self.w = None
        self.r = {}

    def __getitem__(self, idx):
        return self.h[idx]


class KB:
    def __init__(self, nc, es):
        self.nc = nc
        self.es = es
        self.eng = {"pe": nc.tensor, "act": nc.scalar, "dve": nc.vector, "pool": nc.gpsimd, "sp": nc.sync}
        self.semh = {}
        self.cnt = {}
        for e in self.eng:
            self.semh[e] = es.enter_context(nc.semaphore("s_" + e))
            self.cnt[e] = 0
        self.waited = {}
        self.ndma = 24
        self.dma_tot = [0] * self.ndma
        for i in range(self.ndma):
            self.semh["d%d" % i] = es.enter_context(nc.semaphore("s_d%d" % i))
        self.dma_next = 0
        self.n_ins = 0
        self.sw_sems = []

    def _nm(self, name):
        self.n_names = getattr(self, "n_names", 0) + 1
        return "t%d_%s" % (self.n_names, name)

    def sb(self, es, name, shape, dt):
        return T(es.enter_context(self.nc.sbuf_tensor(self._nm(name), list(shape), dt)))

    def ps(self, es, name, shape, dt=F32):
        return T(es.enter_context(self.nc.psum_tensor(self._nm(name), list(shape), dt)))

    def _wait(self, e, k, v):
        if self.waited.get((e, k), 0) >= v:
            return
        self.eng[e].wait_ge(self.semh[k], v)
        self.waited[(e, k)] = v

    def _deps(self, e, r, w):
        deps = {}

        def add(kv):
            k, v = kv
            if deps.get(k, 0) < v:
                deps[k] = v

        for t in r:
            if t.w:
                add(t.w)
        for t in w:
            if t.w:
                add(t.w)
            for kv in t.r.items():
                add(kv)
        for k, v in deps.items():
            if k == e and e == "pe":
                continue
            self._wait(e, k, v)

    def op(self, e, fn, r=(), w=()):
        self._deps(e, r, w)
        ins = fn()
        self.cnt[e] += 1
        ins.then_inc(self.semh[e], 1)
        seq = self.cnt[e]
        for t in r:
            t.r[e] = max(t.r.get(e, 0), seq)
        for t in w:
            t.w = (e, seq)
            t.r = {}
        self.n_ins += 1
        return ins

    def dma(self, q, out, in_, r=(), w=(), **kw):
        if q == "pool":
            k = "sw%d" % len(self.sw_sems)
            self.semh[k] = self.es.enter_context(self.nc.semaphore("s_" + k))
            self.sw_sems.append(k)
            self._deps(q, r, w)
            ins = self.eng[q].dma_start(out=out, in_=in_, **kw)
            ins.then_inc(self.semh[k], 16)
            for t in r:
                t.r[k] = 16
            for t in w:
                t.w = (k, 16)
                t.r = {}
            self.n_ins += 1
            return ins
        i = self.dma_next
        self.dma_next = (i + 1) % self.ndma
        k = "d%d" % i
        self._wait(q, k, self.dma_tot[i])
        self._deps(q, r, w)
        ins = self.eng[q].dma_start(out=out, in_=in_, **kw)
        self.dma_tot[i] += 16
        ins.then_inc(self.semh[k], 16)
        v = self.dma_tot[i]
        for t in r:
            t.r[k] = max(t.r.get(k, 0), v)
        for t in w:
            t.w = (k, v)
            t.r = {}
        self.n_ins += 1
        return ins

    def barrier(self):
        for e in self.eng:
            for k in list(self.eng.keys()):
                if self.cnt[k] > 0:
                    self._wait(e, k, self.cnt[k])
            for i in range(self.ndma):
                if self.dma_tot[i] > 0:
                    self._wait(e, "d%d" % i, self.dma_tot[i])
            for k in self.sw_sems:
                self._wait(e, k, 16)


def _consts(L):
    c = {}
    idx = np.arange(128)
    c["ident"] = np.eye(128, dtype=np.float32)
    c["ones"] = np.ones((128, 128), np.float32)
    bo = np.zeros((128, 128), np.float32)
    bo[:64, :64] = 1.0
    bo[64:, 64:] = 1.0
    c["bones"] = bo
    c["U"] = (idx[:, None] >= idx[None, :]).astype(np.float32)
    c["Ls"] = (idx[:, None] < idx[None, :]).astype(np.float32)
    c["msb"] = (idx[:, None] < idx[None, :]).astype(np.float32)
    c["mret"] = (idx[:, None] <= idx[None, :]).astype(np.float32)
    pb = np.zeros((128, 1), np.float32)
    pb[:PAD_FRONT] = -100.0
    c["padb"] = pb
    PT = np.zeros((128, 128), np.float32)
    for fp in range(128):
        if (fp % 64) < 32:
            PT[fp + 32, fp] = -1.0
        else:
            PT[fp - 32, fp] = 1.0
    c["PT"] = PT
    half = 32
    inv = (10000.0 ** (-np.arange(half, dtype=np.float32) / half)).astype(np.float32)
    pos = (np.arange(L) - PAD_FRONT).astype(np.float32)
    ang = (pos[None, :] * inv[:, None]).astype(np.float32)
    cosv = np.cos(ang).astype(np.float32)
    sinv = np.sin(ang).astype(np.float32)
    c["cosT"] = np.tile(cosv, (4, 1))
    c["sinT"] = np.tile(sinv, (4, 1))
    lg = np.log1p(-np.exp2(-5.0 - np.arange(6, dtype=np.float32))).astype(np.float32)
    i = np.arange(128, dtype=np.float32)
    qd = np.zeros((128, 3, 128), np.float32)
    kd = np.zeros((128, 3, 128), np.float32)
    cd = np.zeros((128, 3), np.float32)
    for h in range(6):
        p, hh = h // 2, h % 2
        qd[64 * hh:64 * hh + 64, p, :] = np.exp(lg[h] * (i + 1.0))[None, :]
        kd[64 * hh:64 * hh + 64, p, :] = (np.exp(-lg[h] * (i + 1.0)) * (HD ** -0.5))[None, :]
        cd[64 * hh:64 * hh + 64, p] = np.exp(lg[h] * 128.0)
    c["iota"] = np.tile(np.arange(L // 8, dtype=np.float32)[None, :], (128, 1))
    c["qdec"] = qd
    c["kdec"] = kd
    c["cdec"] = cd
    return c


class Cfg:
    def __init__(self, NS=2, NCH=33, DEPTH=2, do_ret=True, do_s5=True, do_m2=True, dbg=False):
        self.NS, self.NCH, self.DEPTH = NS, NCH, DEPTH
        self.L = NCH * CH
        self.NT = NS * self.L
        self.do_ret, self.do_s5, self.do_m2, self.dbg = do_ret, do_s5, do_m2, dbg

    def blocks(self, bs=512):
        out = []
        t = 0
        rem = self.L % bs
        if rem:
            out.append((0, rem))
            t = rem
        while t < self.L:
            out.append((t, bs))
            t += bs
        return out


class Prog:
    def __init__(self, cfg):
        self.cfg = cfg
        self.nc = bass.Bass("TRN2", target_bir_lowering=False)
        self.cn = _consts(cfg.L)

    def dram_in(self, name, shape, dt=F32):
        return self.nc.dram_tensor(name, list(shape), dt, kind="ExternalInput").ap()

    def build(self):
        cfg, nc = self.cfg, self.nc
        NS, NCH, L, NT, DEPTH = cfg.NS, cfg.NCH, cfg.L, cfg.NT, cfg.DEPTH
        d = {}
        d["x"] = self.dram_in("x", [NS, (NCH - 1) * CH, D])
        d["meta"] = self.dram_in("meta", [N_META, D])
        d["n1g"] = self.dram_in("n1g", [DEPTH, 128, 8])
        d["n2g"] = self.dram_in("n2g", [DEPTH, 128, 8])
        d["hg"] = self.dram_in("hg", [DEPTH, 128, 4])
        d["rog"] = self.dram_in("rog", [DEPTH, 128, 3])
        d["w_in"] = self.dram_in("w_in", [DEPTH, D, D_IN])
        d["w_out"] = self.dram_in("w_out", [DEPTH, D, D])
        d["w_gate"] = self.dram_in("w_gate", [DEPTH, D, D_FF])
        d["w_up"] = self.dram_in("w_up", [DEPTH, D, D_FF])
        d["w_down"] = self.dram_in("w_down", [DEPTH, D_FF, D])
        for nm in ["lam_r", "lam_i", "ldt"]:
            d[nm] = self.dram_in(nm, [DEPTH, 128, 8])
        for nm in ["Bcr", "Bci", "Ccr", "Cci"]:
            d[nm] = self.dram_in(nm, [DEPTH, 128, 8, 128])
        d["s5d"] = self.dram_in("s5d", [DEPTH, 128, 2])
        d["w_glu"] = self.dram_in("w_glu", [DEPTH, S5W, S5W])
        for k, v in self.cn.items():
            d["c_" + k] = self.dram_in("c_" + k, v.shape)
        self.d = d
        self.out = nc.dram_tensor("out", [NS, (NCH - 1) * CH, D], F32, kind="ExternalOutput").ap()
        kind_dbg = "ExternalOutput" if cfg.dbg else "Internal"
        self.hT = nc.dram_tensor("hT", [8, 128, NT], F32, kind=kind_dbg).ap()
        self.mixT = nc.dram_tensor("mixT", [8, 128, NT], BF16, kind=kind_dbg).ap()
        self.uT = nc.dram_tensor("uT", [2, 128, NT], BF16, kind=kind_dbg).ap()
        self.hT_t = [[T(None) for _ in cfg.blocks()] for _ in range(NS)]
        self.mix_t = [[T(None) for _ in cfg.blocks()] for _ in range(NS)]
        self.u_t = [T(None) for _ in range(NS)]
        self.out_t = T(None)

        with ExitStack() as es:
            self.kb = kb = KB(nc, es)
            self.load_consts(es)
            if not FUSE_IN:
                self.phase0()
            for l in range(DEPTH):
                self.phase_m1(l)
                if cfg.do_s5:
                    self.phase_s5(l)
                if cfg.do_m2:
                    self.phase_m2(l)
            if not (FUSE_OUT and cfg.do_m2):
                self.phase_f()
            kb.barrier()
        return nc

    def load_consts(self, es):
        kb, nc, d = self.kb, self.nc, self.d
        c = {}
        names = ["ident", "ones", "bones", "U", "Ls", "msb", "mret", "PT"]
        c["ident_f"] = kb.sb(es, "k_ident_f", [128, 128], F32)
        for name in names:
            c[name] = kb.sb(es, "k_" + name, [128, 128], BF16)
        c["padb"] = kb.sb(es, "k_padb", [128, 1], F32)
        c["eps"] = kb.sb(es, "k_eps", [128, 1], F32)
        c["qdec"] = kb.sb(es, "k_qdec", [128, 3, 128], F32)
        c["kdec"] = kb.sb(es, "k_kdec", [128, 3, 128], F32)
        c["cdec"] = kb.sb(es, "k_cdec", [128, 3], F32)
        with ExitStack() as tmp:
            for name in names:
                st = kb.sb(tmp, "st_" + name, [128, 128], F32)
                kb.dma("sp", st[:], d["c_" + name], w=[st])
                if name == "ident":
                    kb.op("dve", lambda: nc.vector.tensor_copy(out=c["ident_f"][:], in_=st[:]), r=[st], w=[c["ident_f"]])
                kb.op("dve", lambda: nc.vector.tensor_copy(out=c[name][:], in_=st[:]), r=[st], w=[c[name]])
            kb.barrier()
        kb.dma("sp", c["padb"][:], d["c_padb"], w=[c["padb"]])
        kb.op("dve", lambda: nc.vector.memset(c["eps"][:], EPS), w=[c["eps"]])
        kb.dma("sp", c["qdec"][:], d["c_qdec"], w=[c["qdec"]])
        kb.dma("sp", c["kdec"][:], d["c_kdec"], w=[c["kdec"]])
        kb.dma("sp", c["cdec"][:], d["c_cdec"], w=[c["cdec"]])
        self.c = c

    def tok0(self, s, t):
        return s * self.cfg.L + t

    def phase0(self):
        cfg, nc, kb, d, c = self.cfg, self.nc, self.kb, self.d, self.c
        blocks = cfg.blocks()
        with ExitStack() as es:
            xt = [kb.sb(es, "p0_xt%d" % i, [128, D], F32) for i in range(2)]
            xT = [kb.sb(es, "p0_xT%d" % i, [128, 8, 128], F32) for i in range(2)]
            ps = [kb.ps(es, "p0_ps%d" % i, [128, 1024], F32) for i in range(2)]
            it = 0
            for s in range(cfg.NS):
                for ch in range(cfg.NCH):
                    a, b, p = xt[it % 2], xT[it % 2], ps[it % 2]
                    if ch == 0:
                        kb.op("dve", lambda: nc.vector.memset(a[:], 0.0), w=[a])
                        kb.dma("sp", a[PAD_FRONT:128, :], d["meta"], w=[a])
                    else:
                        kb.dma("sp", a[:], d["x"][s, (ch - 1) * CH:ch * CH, :], w=[a])
                    for k in range(8):
                        kb.op("pe", lambda: nc.tensor.transpose(out=p[:, k * 128:(k + 1) * 128], in_=a[:, k * 128:(k + 1) * 128],
                                                               identity=c["ident_f"][:]), r=[a, c["ident_f"]], w=[p])
                    kb.op("act", lambda: nc.scalar.copy(out=b[:].rearrange("p k t -> p (k t)"), in_=p[:]), r=[p], w=[b])
                    t = ch * CH
                    bi = [i for i, (t0, n) in enumerate(blocks) if t0 <= t < t0 + n][0]
                    g0 = self.tok0(s, t)
                    kb.dma("sp", self.hT[:, :, g0:g0 + CH].rearrange("k p t -> p k t"), b[:], r=[b], w=[self.hT_t[s][bi]])
                    it += 1
            kb.barrier()

    def load_w_bf16(self, dst, src2d, nk):
        kb = self.kb
        ncols = src2d.shape[1]
        src = src2d.rearrange("(k p) c -> p k c", p=128)
        step = 1024
        for c0 in range(0, ncols, step):
            c1 = min(ncols, c0 + step)
            kb.dma("pool", dst[:, :, c0:c1], src[:, :, c0:c1], w=[dst])

    def rmsnorm_fm(self, n, src, src_t, gcol, hsq, ps_ss, lnv, rstd, hn):
        nc, kb, c = self.nc, self.kb, self.c
        kb.op("act", lambda: nc.scalar.activation(out=hsq[:, :, :n], in_=src[:, :, :n], func=AF.Square), r=[src_t], w=[hsq])
        for k in range(8):
            kb.op("pe", lambda: nc.tensor.matmul(ps_ss[:, :n], lhsT=c["ones"][:], rhs=hsq[:, k, :n], start=(k == 0), stop=(k == 7)),
                  r=[hsq, c["ones"]], w=[ps_ss])
        kb.op("act", lambda: nc.scalar.activation(out=lnv[:, :n], in_=ps_ss[:, :n], func=AF.Ln, scale=1.0 / D, bias=c["eps"][:, 0:1]),
              r=[ps_ss, c["eps"]], w=[lnv])
        kb.op("act", lambda: nc.scalar.activation(out=rstd[:, :n], in_=lnv[:, :n], func=AF.Exp, scale=-0.5), r=[lnv], w=[rstd])
        for k in range(8):
            kb.op("dve", lambda: nc.vector.scalar_tensor_tensor(out=hn[:, k, :n], in0=src[:, k, :n], scalar=gcol[:, k:k + 1], in1=rstd[:, :n],
                                                                op0=ALU.mult, op1=ALU.mult), r=[src_t, gcol, rstd], w=[hn])

    def headnorm(self, P, n, gcol_ap, gt, dst_ap, dst_t, W):
        nc, kb, c = self.nc, self.kb, self.c
        sq, ps_ss, lnv, rs = W["sq"], W["ps_ss"], W["lnv"], W["rs"]
        kb.op("act", lambda: nc.scalar.activation(out=sq[:, :n], in_=P[:, :n], func=AF.Square), r=[P], w=[sq])
        kb.op("pe", lambda: nc.tensor.matmul(ps_ss[:, :n], lhsT=c["bones"][:], rhs=sq[:, :n], start=True, stop=True), r=[sq, c["bones"]], w=[ps_ss])
        kb.op("act", lambda: nc.scalar.activation(out=lnv[:, :n], in_=ps_ss[:, :n], func=AF.Ln, scale=1.0 / HD, bias=c["eps"][:, 0:1]),
              r=[ps_ss, c["eps"]], w=[lnv])
        kb.op("act", lambda: nc.scalar.activation(out=rs[:, :n], in_=lnv[:, :n], func=AF.Exp, scale=-0.5), r=[lnv], w=[rs])
        kb.op("dve", lambda: nc.vector.scalar_tensor_tensor(out=dst_ap, in0=P[:, :n], scalar=gcol_ap, in1=rs[:, :n], op0=ALU.mult, op1=ALU.mult),
              r=[P, gt, rs], w=[dst_t])

    def headnorm_g(self, P, n, gcol_ap, gt, dst_ap, dst_t, W):
        nc, kb, c = self.nc, self.kb, self.c
        sq, ps_ss, lnv, rs = W["sq"], W["ps_ss"], W["lnv"], W["rs"]
        yield
        kb.op("act", lambda: nc.scalar.activation(out=sq[:, :n], in_=P[:, :n], func=AF.Square), r=[P], w=[sq])
        yield
        kb.op("pe", lambda: nc.tensor.matmul(ps_ss[:, :n], lhsT=c["bones"][:], rhs=sq[:, :n], start=True, stop=True), r=[sq, c["bones"]], w=[ps_ss])
        yield
        kb.op("act", lambda: nc.scalar.activation(out=lnv[:, :n], in_=ps_ss[:, :n], func=AF.Ln, scale=1.0 / HD, bias=c["eps"][:, 0:1]),
              r=[ps_ss, c["eps"]], w=[lnv])
        kb.op("act", lambda: nc.scalar.activation(out=rs[:, :n], in_=lnv[:, :n], func=AF.Exp, scale=-0.5), r=[lnv], w=[rs])
        yield
        kb.op("dve", lambda: nc.vector.scalar_tensor_tensor(out=dst_ap, in0=P[:, :n], scalar=gcol_ap, in1=rs[:, :n], op0=ALU.mult, op1=ALU.mult),
              r=[P, gt, rs], w=[dst_t])
        yield

    def phase_m1(self, l):
        cfg, nc, kb, d, c = self.cfg, self.nc, self.kb, self.d, self.c
        L, NCH = cfg.L, cfg.NCH
        blocks = cfg.blocks()
        with ExitStack() as es:
            w_in = kb.sb(es, "w_in", [128, 8, D_IN], BF16)
            self.load_w_bf16(w_in, d["w_in"][l], 8)
            n1g = kb.sb(es, "n1g", [128, 8], F32)
            kb.dma("sp", n1g[:], d["n1g"][l], w=[n1g])
            hg = kb.sb(es, "hg", [128, 4], F32)
            kb.dma("sp", hg[:], d["hg"][l], w=[hg])
            hg8 = kb.sb(es, "hg8", [128, 1], F32)
            kb.op("dve", lambda: nc.vector.tensor_scalar(out=hg8[:], in0=hg[:, 0:1], scalar1=HD ** -0.5, scalar2=None, op0=ALU.mult), r=[hg], w=[hg8])
            KT = [kb.sb(es, "KT%d" % p, [128, L], BF16) for p in range(3)]
            KT_t = [[T(None) for _ in blocks] for p in range(3)]
            Vt = kb.sb(es, "Vtok", [128, NCH, SBW], BF16)
            Vt_t = [T(None) for _ in blocks]
            hT_sb = kb.sb(es, "hT_sb", [128, 8, 512], F32)
            hsq = kb.sb(es, "hsq", [128, 8, 512], BF16)
            hn = kb.sb(es, "hn", [128, 8, 512], BF16)
            lnv = kb.sb(es, "lnv", [128, 512], F32)
            rstd = kb.sb(es, "rstd", [128, 512], F32)
            W = {"sq": kb.sb(es, "hn_sq", [128, 512], BF16), "lnv": kb.sb(es, "hn_lnv", [128, 512], F32),
                 "rs": kb.sb(es, "hn_rs", [128, 512], F32)}
            QN = [kb.sb(es, "QN%d" % p, [128, 512], BF16) for p in range(3)]
            NQ = [kb.sb(es, "NQ%d" % p, [128, 512], BF16) for p in range(3)]
            ebf = [[kb.sb(es, "ebf%d_%d" % (hh, i), [128, 512], BF16) for i in range(2)] for hh in range(2)]
            xc = [[kb.sb(es, "xc%d_%d" % (hh, i), [128, 512], BF16) for i in range(2)] for hh in range(2)]
            sp = [[kb.sb(es, "sp%d_%d" % (hh, i), [128, 512], BF16) for i in range(3)] for hh in range(2)]
            wt = [[kb.sb(es, "wt%d_%d" % (hh, i), [128, 512], BF16) for i in range(2)] for hh in range(2)]
            mixblk = [kb.sb(es, "mixblk%d" % i, [128, 512], BF16) for i in range(2)]
            ublk = kb.sb(es, "ublk", [128, 2, 512], BF16)
            rog = kb.sb(es, "rog", [128, 3], F32)
            kb.dma("sp", rog[:], d["rog"][l], w=[rog])
            cosb = kb.sb(es, "cosb", [128, 512], F32)
            sinb = kb.sb(es, "sinb", [128, 512], F32)
            rn = kb.sb(es, "rn", [128, 512], BF16)
            tmp1 = kb.sb(es, "tmp1", [128, 512], F32)
            tmp2 = kb.sb(es, "tmp2", [128, 512], F32)
            kp = [kb.sb(es, "kp%d" % p, [128, 512], BF16) for p in range(3)]
            qp = [kb.sb(es, "qp%d" % p, [128, 2, 512], BF16) for p in range(3)]
            for p in range(3):
                kb.op("dve", lambda: nc.vector.memset(qp[p][:], 0.0), w=[qp[p]])
            kptok = [kb.sb(es, "kptok%d" % p, [128, 4, 128], BF16) for p in range(3)]
            rvt = kb.sb(es, "rvt", [128, 4, RETW], BF16)
            gate = [kb.sb(es, "gate%d" % p, [128, 512], F32) for p in range(3)]
            scm = kb.sb(es, "scm", [128, 2, 128], BF16)
            st32 = [kb.sb(es, "st32_%d" % p, [128, HD], F32) for p in range(3)]
            stbf = [kb.sb(es, "stbf_%d" % p, [128, HD], BF16) for p in range(3)]
            ps = [kb.ps(es, "m1ps%d" % i, [128, 512], F32) for i in range(8)]
            X0, X1, X2 = ps[5], ps[6], ps[7]
            W["ps_ss"] = X1
            QN2 = [QN, [kb.sb(es, "QNb%d" % p, [128, 512], BF16) for p in range(3)]]
            xh = kb.sb(es, "xh", [128, 512], F32) if (l == 0 and FUSE_IN) else None
            mixc = [0]

            def next_mb():
                mb = mixblk[mixc[0] % 2]
                mixc[0] += 1
                return mb

            def stage_p(s, bi, part=0):
                t0, n = blocks[bi]
                g0 = self.tok0(s, t0)
                nq = n // CH
                qc0 = t0 // CH
                QNc = QN2[bi % 2]
                def proj_fm(P, col0):
                    for k in range(8):
                        kb.op("pe", lambda: nc.tensor.matmul(P[:, :n], lhsT=w_in[:, k, col0:col0 + 128], rhs=hn[:, k, :n], start=(k == 0), stop=(k == 7)),
                              r=[w_in, hn], w=[P])

                if part in (0, 1):
                    if bi == 0:
                        for p in range(3):
                            kb.op("dve", lambda: nc.vector.memset(st32[p][:], 0.0), w=[st32[p]])
                            kb.op("dve", lambda: nc.vector.memset(stbf[p][:], 0.0), w=[stbf[p]])
                    if l == 0 and FUSE_IN:
                        for j in range(nq):
                            ch = qc0 + j
                            for half in range(2):
                                fs_ = slice(half * 512, (half + 1) * 512)
                                if ch == 0:
                                    kb.op("dve", lambda: nc.vector.memset(xh[:], 0.0), w=[xh])
                                    kb.dma("sp", xh[PAD_FRONT:128, :], d["meta"][:, fs_], w=[xh])
                                else:
                                    kb.dma("sp", xh[:], d["x"][s, (ch - 1) * CH:ch * CH, fs_], w=[xh])
                                for q in range(4):
                                    kb.op("pe", lambda: nc.tensor.transpose(out=X2[:, q * 128:(q + 1) * 128], in_=xh[:, q * 128:(q + 1) * 128], identity=c["ident_f"][:]),
                                          r=[xh, c["ident_f"]], w=[X2])
                                kb.op("act", lambda: nc.scalar.copy(out=hT_sb[:, 4 * half:4 * half + 4, j * CH:(j + 1) * CH],
                                                                    in_=X2[:, :].rearrange("p (q t) -> p q t", t=CH)), r=[X2], w=[hT_sb])
                                yield
                        kb.dma("sp", self.hT[:, :, g0:g0 + n].rearrange("k p t -> p k t"), hT_sb[:, :, :n], r=[hT_sb], w=[self.hT_t[s][bi]])
                    else:
                        kb.dma("sp", hT_sb[:, :, :n], self.hT[:, :, g0:g0 + n].rearrange("k p t -> p k t"), r=[self.hT_t[s][bi]], w=[hT_sb])
                    self.rmsnorm_fm(n, hT_sb, hT_sb, n1g, hsq, X1, lnv, rstd, hn)
                    yield
                for p in (range(3) if part in (0, 1) else []):
                    proj_fm(X0, C_SK + p * 128)
                    yield
                    proj_fm(X2, C_SQ + p * 128)
                    yield from self.headnorm_g(X0, n, hg[:, 1:2], hg, KT[p][:, t0:t0 + n], KT_t[p][bi], W)
                    yield from self.headnorm_g(X2, n, hg8[:, 0:1], hg8, QNc[p][:, :n], QNc[p], W)
                for j in (range(nq) if part in (0, 1) else []):
                    for k in range(8):
                        kb.op("pe", lambda: nc.tensor.matmul(X2[:, :SBW], lhsT=hn[:, k, j * CH:(j + 1) * CH], rhs=w_in[:, k, C_SV:C_SV + SBW],
                                                             start=(k == 0), stop=(k == 7)), r=[w_in, hn], w=[X2])
                    kb.op("act", lambda: nc.scalar.copy(out=Vt[:, qc0 + j, :], in_=X2[:, :SBW]), r=[X2], w=[Vt_t[bi]])
                    yield
                if part == 1:
                    return
                for uh in range(2):
                    proj_fm(X0, C_U + uh * 128)
                    kb.op("act", lambda: nc.scalar.copy(out=ublk[:, uh, :n], in_=X0[:, :n]), r=[X0], w=[ublk])
                    yield
                kb.dma("sp", self.uT[:, :, g0:g0 + n].rearrange("k p t -> p k t"), ublk[:, :, :n], r=[ublk], w=[self.u_t[s]])
                if not cfg.do_ret:
                    return
                kb.dma("sp", cosb[:, :n], d["c_cosT"][:, t0:t0 + n], w=[cosb])
                kb.dma("sp", sinb[:, :n], d["c_sinT"][:, t0:t0 + n], w=[sinb])

                def rot(dst, gi, dec, p, padded=False):
                    yield from self.headnorm_g(X0, n, hg[:, gi:gi + 1], hg, rn[:, :n], rn, W)
                    kb.op("pe", lambda: nc.tensor.matmul(X2[:, :n], lhsT=c["PT"][:], rhs=rn[:, :n], start=True, stop=True), r=[c["PT"], rn], w=[X2])
                    kb.op("dve", lambda: nc.vector.tensor_tensor(out=tmp1[:, :n], in0=rn[:, :n], in1=cosb[:, :n], op=ALU.mult), r=[rn, cosb], w=[tmp1])
                    yield
                    kb.op("dve", lambda: nc.vector.tensor_tensor(out=tmp2[:, :n], in0=X2[:, :n], in1=sinb[:, :n], op=ALU.mult), r=[X2, sinb], w=[tmp2])
                    kb.op("dve", lambda: nc.vector.tensor_tensor(out=tmp1[:, :n], in0=tmp1[:, :n], in1=tmp2[:, :n], op=ALU.add), r=[tmp1, tmp2], w=[tmp1])
                    if padded:
                        for hh in range(2):
                            rs_ = slice(64 * hh, 64 * hh + 64)
                            kb.op("dve", lambda: nc.vector.tensor_tensor(out=dst[rs_, hh, :n].rearrange("p (j i) -> p j i", i=CH),
                                                                          in0=tmp1[rs_, :n].rearrange("p (j i) -> p j i", i=CH),
                                                                          in1=dec[rs_, p, :].unsqueeze(1).broadcast_to([64, nq, CH]), op=ALU.mult),
                                  r=[tmp1, dec], w=[dst])
                    else:
                        kb.op("dve", lambda: nc.vector.tensor_tensor(out=dst[:, :n].rearrange("p (j i) -> p j i", i=CH),
                                                                      in0=tmp1[:, :n].rearrange("p (j i) -> p j i", i=CH),
                                                                      in1=dec[:, p, :].unsqueeze(1).broadcast_to([128, nq, CH]), op=ALU.mult),
                              r=[tmp1, dec], w=[dst])

                for p in range(3):
                    proj_fm(X0, C_RK + p * 128)
                    yield from rot(kp[p], 3, c["kdec"], p)
                    yield
                    proj_fm(X0, C_RQ + p * 128)
                    yield from rot(qp[p], 2, c["qdec"], p, padded=True)
                    yield
                    pst = X2[:].bitcast(BF16)
                    for j in range(nq):
                        kb.op("pe", lambda: nc.tensor.transpose(out=pst[:, j * CH:(j + 1) * CH], in_=kp[p][:, j * CH:(j + 1) * CH], identity=c["ident"][:]),
                              r=[kp[p], c["ident"]], w=[X2])
                    kb.op("act", lambda: nc.scalar.copy(out=kptok[p][:, :nq, :].rearrange("p j f -> p (j f)"), in_=pst[:, :n]), r=[X2], w=[kptok[p]])
                    yield
                    proj_fm(X0, C_RG + p * 128)
                    yield
                    kb.op("act", lambda: nc.scalar.activation(out=tmp1[:, :n], in_=X0[:, :n], func=AF.Exp, scale=-1.0), r=[X0], w=[tmp1])
                    yield
                    kb.op("dve", lambda: nc.vector.tensor_scalar(out=tmp1[:, :n], in0=tmp1[:, :n], scalar1=1.0, scalar2=None, op0=ALU.add), r=[tmp1], w=[tmp1])
                    kb.op("dve", lambda: nc.vector.reciprocal(out=tmp1[:, :n], in_=tmp1[:, :n]), r=[tmp1], w=[tmp1])
                    kb.op("dve", lambda: nc.vector.tensor_tensor(out=gate[p][:, :n], in0=X0[:, :n], in1=tmp1[:, :n], op=ALU.mult), r=[X0, tmp1], w=[gate[p]])
                    yield
                for j in range(nq):
                    for k in range(8):
                        kb.op("pe", lambda: nc.tensor.matmul(X2[:, :RETW], lhsT=hn[:, k, j * CH:(j + 1) * CH], rhs=w_in[:, k, C_RV:C_RV + RETW],
                                                             start=(k == 0), stop=(k == 7)), r=[w_in, hn], w=[X2])
                    kb.op("act", lambda: nc.scalar.copy(out=rvt[:, j, :], in_=X2[:, :RETW]), r=[X2], w=[rvt])
                    yield
                for p in range(3):
                    po = X2
                    for j in range(nq):
                        jc = slice(j * CH, (j + 1) * CH)
                        kb.op("pe", lambda: nc.tensor.matmul(X0[:, 0:2 * CH].rearrange("p (a i) -> p a i", i=CH), lhsT=kp[p][:, jc], rhs=qp[p][:, :, jc], start=True, stop=True),
                              r=[kp[p], qp[p]], w=[X0])
                        for hh in range(2):
                            kb.op("dve", lambda: nc.vector.tensor_tensor(out=scm[:, hh, :], in0=X0[:, hh * CH:(hh + 1) * CH], in1=c["mret"][:], op=ALU.mult),
                                  r=[X0, c["mret"]], w=[scm])
                        for hh in range(2):
                            h = 2 * p + hh
                            rs_ = slice(64 * hh, 64 * hh + 64)
                            kb.op("pe", lambda: nc.tensor.matmul(po[rs_, jc], lhsT=rvt[:, j, h * HD:(h + 1) * HD], rhs=scm[:, hh, :], start=True, stop=False),
                                  r=[rvt, scm], w=[po])
                            kb.op("pe", lambda: nc.tensor.matmul(po[rs_, jc], lhsT=stbf[p][:, :], rhs=qp[p][:, hh, jc], start=False, stop=True),
                                  r=[stbf[p], qp[p]], w=[po])
                        kb.op("pe", lambda: nc.tensor.matmul(X1[:, 0:CH], lhsT=kptok[p][:, j, :], rhs=rvt[:, j, p * CH:(p + 1) * CH], start=True, stop=True),
                              r=[kptok[p], rvt], w=[X1])
                        for hh in range(2):
                            rs_ = slice(64 * hh, 64 * hh + 64)
                            kb.op("dve", lambda: nc.vector.tensor_tensor(out=st32[p][rs_, :], in0=st32[p][rs_, :], in1=X1[rs_, hh * HD:(hh + 1) * HD], op=ALU.add),
                                  r=[st32[p], X1], w=[st32[p]])
                        kb.op("dve", lambda: nc.vector.tensor_scalar(out=st32[p][:], in0=st32[p][:], scalar1=c["cdec"][:, p:p + 1], scalar2=None, op0=ALU.mult),
                              r=[st32[p], c["cdec"]], w=[st32[p]])
                        kb.op("dve", lambda: nc.vector.tensor_copy(out=stbf[p][:], in_=st32[p][:]), r=[st32[p]], w=[stbf[p]])
                        yield
                    yield from self.headnorm_g(po, n, rog[:, p:p + 1], rog, tmp2[:, :n], tmp2, W)
                    mb = next_mb()
                    kb.op("dve", lambda: nc.vector.tensor_tensor(out=mb[:, :n], in0=tmp2[:, :n], in1=gate[p][:, :n], op=ALU.mult), r=[tmp2, gate[p]], w=[mb])
                    kb.dma("sp", self.mixT[3 + p, :, g0:g0 + n], mb[:, :n], r=[mb], w=[self.mix_t[s][bi]])
                    yield

            def stage_sb(s, bi):
                t0, n = blocks[bi]
                g0 = self.tok0(s, t0)
                nq = n // CH
                qc0 = t0 // CH
                QNc = QN2[bi % 2]
                kt_all = lambda p: [KT_t[p][i] for i in range(bi + 1)]
                vt_all = [Vt_t[i] for i in range(bi + 1)]
                kcs = list(range(qc0 + nq - 1, -1, -1))
                NST = len(kcs)
                ob = ps[4]

                def geo(kc):
                    col0 = max(0, kc - qc0) * CH
                    return col0, slice(col0, n), slice(kc * CH, (kc + 1) * CH), kc >= qc0

                for p in range(3):
                    def st_z(i):
                        col0, cols, kcols, diag = geo(kcs[i])
                        for hh in range(2):
                            rs_ = slice(64 * hh, 64 * hh + 64)
                            z = ps[hh]
                            kb.op("pe", lambda: nc.tensor.matmul(z[:, cols], lhsT=KT[p][rs_, kcols], rhs=QNc[p][rs_, cols], start=True, stop=True),
                                  r=kt_all(p) + [QNc[p]], w=[z])

                    def st_e(i):
                        kc = kcs[i]
                        col0, cols, kcols, diag = geo(kc)
                        dc = slice(col0, col0 + CH)
                        bias = c["padb"][:, 0:1] if kc == 0 else 0.0
                        for hh in range(2):
                            z = ps[hh]
                            eb = ebf[hh][i % 2]
                            kb.op("act", lambda: nc.scalar.activation(out=eb[:, cols], in_=z[:, cols], func=AF.Exp, bias=bias), r=[z, c["padb"]], w=[eb])
                        if diag:
                            for hh in range(2):
                                eb = ebf[hh][i % 2]
                                kb.op("dve", lambda: nc.vector.tensor_tensor(out=eb[:, dc], in0=eb[:, dc], in1=c["msb"][:], op=ALU.mult),
                                      r=[eb, c["msb"]], w=[eb])
                        for hh in range(2):
                            eb = ebf[hh][i % 2]
                            spc = sp[hh][i % 3]
                            kb.op("act", lambda: nc.scalar.activation(out=spc[:, cols], in_=eb[:, cols], func=AF.Ln, bias=1.0), r=[eb], w=[spc])

                    def st_acc(i):
                        kc = kcs[i]
                        col0, cols, kcols, diag = geo(kc)
                        for hh in range(2):
                            Bk = ps[2 + hh]
                            spc = sp[hh][i % 3]
                            if i > 0:
                                pcol0, pcols, pkcols, _ = geo(kcs[i - 1])
                                psp = sp[hh][(i - 1) % 3]
                                kb.op("pe", lambda: nc.tensor.matmul(Bk[:, pcols], lhsT=c["Ls"][:], rhs=psp[:, pcols], start=False, stop=False, skip_group_check=True),
                                      r=[c["Ls"], psp], w=[Bk])
                            kb.op("pe", lambda: nc.tensor.matmul(Bk[:, cols], lhsT=c["U"][:], rhs=spc[:, cols], start=(i == 0), stop=(kc == 0), skip_group_check=True),
                                  r=[c["U"], spc], w=[Bk])

                    def st_x(i):
                        col0, cols, kcols, diag = geo(kcs[i])
                        for hh in range(2):
                            Bk = ps[2 + hh]
                            x_ = xc[hh][i % 2]
                            kb.op("act", lambda: nc.scalar.activation(out=x_[:, cols], in_=Bk[:, cols], func=AF.Exp, scale=-1.0), r=[Bk], w=[x_])

                    def st_w(i):
                        col0, cols, kcols, diag = geo(kcs[i])
                        for hh in range(2):
                            wc, eb, x_ = wt[hh][i % 2], ebf[hh][i % 2], xc[hh][i % 2]
                            kb.op("dve", lambda: nc.vector.tensor_tensor(out=wc[:, cols], in0=eb[:, cols], in1=x_[:, cols], op=ALU.mult),
                                  r=[eb, x_], w=[wc])

                    def st_pv(i):
                        kc = kcs[i]
                        col0, cols, kcols, diag = geo(kc)
                        for hh in range(2):
                            h = 2 * p + hh
                            rs_ = slice(64 * hh, 64 * hh + 64)
                            wc = wt[hh][i % 2]
                            kb.op("pe", lambda: nc.tensor.matmul(ob[rs_, cols], lhsT=Vt[:, kc, h * HD:(h + 1) * HD], rhs=wc[:, cols],
                                                                 start=(i == 0), stop=(kc == 0), skip_group_check=True), r=vt_all + [wc], w=[ob])

                    st_z(0)
                    st_e(0)
                    if NST > 1:
                        st_z(1)
                    for tau in range(NST):
                        st_acc(tau)
                        if tau + 1 < NST:
                            st_e(tau + 1)
                        if tau + 2 < NST:
                            st_z(tau + 2)
                        st_x(tau)
                        st_w(tau)
                        if tau >= 1:
                            st_pv(tau - 1)
                        yield
                    st_pv(NST - 1)
                    mb = next_mb()
                    kb.op("act", lambda: nc.scalar.copy(out=mb[:, :n], in_=ob[:, :n]), r=[ob], w=[mb])
                    kb.dma("sp", self.mixT[p, :, g0:g0 + n], mb[:, :n], r=[mb], w=[self.mix_t[s][bi]])
                    yield

            def drain(g):
                for _ in g:
                    pass

            flat = [(s, bi) for s in range(cfg.NS) for bi in range(len(blocks))]

            def chain(*gens):
                for g in gens:
                    for _ in g:
                        yield

            NP_EST = 170 if cfg.do_ret else 50
            for idx, (s, bi) in enumerate(flat):
                if bi == 0:
                    drain(stage_p(s, bi, part=1))
                sbg = stage_sb(s, bi)
                gens = [stage_p(s, bi, part=2)]
                if bi + 1 < len(blocks):
                    gens.append(stage_p(s, bi + 1, part=1))
                pg = chain(*gens)
                if INTERLEAVE:
                    t0, n = blocks[bi]
                    n_sb = 3 * (t0 // CH + n // CH + 1)
                    per = max(1, -(-NP_EST // n_sb))
                    for _ in sbg:
                        for _k in range(per):
                            next(pg, None)
                    drain(pg)
                else:
                    drain(pg)
                    drain(sbg)
            kb.barrier()


def _col(v, nk):
    return np.ascontiguousarray(np.asarray(v, np.float32).reshape(nk, 128).T)


def make_in_map(cfg, cn, inp, x_shard):
    DEPTH = cfg.DEPTH
    m = {"x": np.ascontiguousarray(x_shard, dtype=np.float32), "meta": np.asarray(inp["meta_tokens"], np.float32)}
    m["n1g"] = np.stack([_col(inp["norm1_g"][l], 8) for l in range(DEPTH)])
    m["n2g"] = np.stack([_col(inp["norm2_g"][l], 8) for l in range(DEPTH)])
    hg = np.zeros((DEPTH, 128, 4), np.float32)
    for l in range(DEPTH):
        for j, nm in enumerate(["sb_q_g", "sb_k_g", "ret_q_g", "ret_k_g"]):
            hg[l, :, j] = np.tile(np.asarray(inp[nm][l], np.float32), 2)
    m["hg"] = hg
    m["rog"] = np.stack([_col(inp["ret_out_g"][l], 3) for l in range(DEPTH)])
    for nm in ["w_in", "w_out", "w_gate", "w_up", "w_down"]:
        m[nm] = np.ascontiguousarray(np.asarray(inp[nm], np.float32)[:DEPTH])
    G = 16
    def colq(a):
        return np.ascontiguousarray(np.asarray(a, np.float32).reshape(8, 2, 64).transpose(1, 2, 0).reshape(128, 8))
    lam_r, lam_i, ldt = [], [], []
    Bcr, Bci, Ccr, Cci, dcol = [], [], [], [], []
    for l in range(DEPTH):
        lam_r.append(colq(inp["s5_lam_re"][l]))
        lam_i.append(colq(inp["s5_lam_im"][l]))
        ldt.append(colq(np.repeat(np.asarray(inp["s5_log_dt"][l], np.float32)[:, None], 64, axis=1)))
        def padB(b):
            out = np.zeros((128, 8, 128), np.float32)
            for g in range(G):
                j, gl, gi = g // 2, g % 2, g % 8
                out[gl * 64:(gl + 1) * 64, j, gi * 16:(gi + 1) * 16] = b[g]
            return out
        def padC(cm):
            out = np.zeros((128, 8, 128), np.float32)
            for g in range(G):
                j, gl, gi = g // 2, g % 2, g % 8
                out[gl * 64:(gl + 1) * 64, j, gi * 16:(gi + 1) * 16] = cm[g].T
            return out
        Bcr.append(padB(np.asarray(inp["s5_b_re"][l], np.float32)))
        Bci.append(padB(np.asarray(inp["s5_b_im"][l], np.float32)))
        Ccr.append(padC(np.asarray(inp["s5_c_re"][l], np.float32)))
        Cci.append(padC(np.asarray(inp["s5_c_im"][l], np.float32)))
        dcol.append(_col(inp["s5_d"][l], 2))
    m["lam_r"], m["lam_i"], m["ldt"] = np.stack(lam_r), np.stack(lam_i), np.stack(ldt)
    m["Bcr"], m["Bci"], m["Ccr"], m["Cci"] = np.stack(Bcr), np.stack(Bci), np.stack(Ccr), np.stack(Cci)
    m["s5d"] = np.stack(dcol)
    m["w_glu"] = np.ascontiguousarray(np.asarray(inp["s5_w_glu"], np.float32)[:DEPTH])
    for k, v in cn.items():
        m["c_" + k] = v
    return m


TWO_PI_LO = 6.2831845
MAGIC = 12582912.0
GELU_C = math.sqrt(2.0 / math.pi)


def _phase_s5(self, l):
    cfg, nc, kb, d, c = self.cfg, self.nc, self.kb, self.d, self.c
    L = cfg.L
    NB = L // 8
    blocks = cfg.blocks()

    def tt(out, a, b, op, r, w, e="dve"):
        kb.op(e, lambda: self.eng_of(e).tensor_tensor(out=out, in0=a, in1=b, op=op), r=r, w=w)

    def ts(out, a, s1, op0, r, w, s2=None, op1=None):
        if op1 is None:
            kb.op("dve", lambda: nc.vector.tensor_scalar(out=out, in0=a, scalar1=s1, scalar2=None, op0=op0), r=r, w=w)
        else:
            kb.op("dve", lambda: nc.vector.tensor_scalar(out=out, in0=a, scalar1=s1, scalar2=s2, op0=op0, op1=op1), r=r, w=w)

    def act(out, a, func, r, w, **kw):
        kb.op("act", lambda: nc.scalar.activation(out=out, in_=a, func=func, **kw), r=r, w=w)

    with ExitStack() as es:
        BP = kb.sb(es, "BP", [128, 8, 2, 8, 128], BF16)
        CP = kb.sb(es, "CP", [128, 8, 2, 8, 128], BF16)
        Kt = kb.sb(es, "Ktap", [128, 8, 2, 128], BF16)
        cosn = kb.sb(es, "cosn", [128, 8, NB], F32)
        sinn = kb.sb(es, "sinn", [128, 8, NB], F32)
        m8 = kb.sb(es, "m8", [128, 8], F32)
        dcol = kb.sb(es, "s5dcol", [128, 2], F32)
        kb.dma("sp", dcol[:], d["s5d"][l], w=[dcol])
        wglu = kb.sb(es, "wglu", [128, 2, S5W], BF16)
        self.load_w_bf16(wglu, d["w_glu"][l], 2)
        ps = [kb.ps(es, "s5ps%d" % i, [128, 512], F32) for i in range(8)]
        with ExitStack() as pes:
            P8 = kb.sb(pes, "P8", [128, 24, 8], F32)
            pwr = kb.sb(pes, "pwr", [128, 9, 8], F32)
            pwi = kb.sb(pes, "pwi", [128, 9, 8], F32)
            iota = kb.sb(pes, "iota", [128, NB], F32)
            kb.dma("sp", iota[:], d["c_iota"], w=[iota])
            big = [kb.sb(pes, "s5big%d" % i, [128, 8, 128], F32) for i in range(11)]
            Bcr, Bci, Ccr, Cci, nCi, Bbr, Bbi, Er, Ei, T1, T2 = big
            for tl, nm in [(Bcr, "Bcr"), (Bci, "Bci"), (Ccr, "Ccr"), (Cci, "Cci")]:
                kb.dma("sp", tl[:], d[nm][l], w=[tl])
            nbt = [kb.sb(pes, "s5nb%d" % i, [128, NB], F32) for i in range(4)]
            V = lambda i: P8[:, i, :]
            LR, LI, LDT, DT, LRDT, MAG, R, SIN, COS, AR, AI, DEN, AM1, FR, FI, X1, X2, X3, F8 = range(19)
            kb.dma("sp", V(LR), d["lam_r"][l], w=[P8])
            kb.dma("sp", V(LI), d["lam_i"][l], w=[P8])
            kb.dma("sp", V(LDT), d["ldt"][l], w=[P8])
            p8 = [P8]
            act(V(DT), V(LDT), AF.Exp, p8, p8)
            tt(V(LRDT), V(LR), V(DT), ALU.mult, p8, p8)
            act(V(MAG), V(LRDT), AF.Exp, p8, p8)
            act(m8[:], V(LRDT), AF.Exp, p8, [m8], scale=8.0)
            tt(V(R), V(LI), V(DT), ALU.mult, p8, p8)
            ts(V(R), V(R), 1.0 / (2.0 * math.pi), ALU.mult, p8, p8)

            def red_sin(dst, dst_t, src, src_t, tmpa, tmpb, tmp_t, shift=0.0):
                if shift != 0.0:
                    ts(tmpb, src, shift, ALU.add, src_t + tmp_t, tmp_t)
                    src = tmpb
                ts(tmpa, src, MAGIC, ALU.add, src_t + tmp_t, tmp_t)
                ts(tmpa, tmpa, MAGIC, ALU.subtract, tmp_t, tmp_t)
                tt(tmpa, src, tmpa, ALU.subtract, src_t + tmp_t, tmp_t)
                act(dst, tmpa, AF.Sin, tmp_t, dst_t, scale=TWO_PI_LO)

            red_sin(V(SIN), p8, V(R), p8, V(X1), V(X2), p8)
            red_sin(V(COS), p8, V(R), p8, V(X1), V(X2), p8, shift=0.25)
            tt(V(AR), V(MAG), V(COS), ALU.mult, p8, p8)
            tt(V(AI), V(MAG), V(SIN), ALU.mult, p8, p8)
            tt(V(DEN), V(LR), V(LR), ALU.mult, p8, p8)
            tt(V(X1), V(LI), V(LI), ALU.mult, p8, p8)
            tt(V(DEN), V(DEN), V(X1), ALU.add, p8, p8)
            kb.op("dve", lambda: nc.vector.reciprocal(out=V(DEN), in_=V(DEN)), r=p8, w=p8)
            ts(V(AM1), V(AR), -1.0, ALU.add, p8, p8)
            tt(V(X1), V(AM1), V(LR), ALU.mult, p8, p8)
            tt(V(X2), V(AI), V(LI), ALU.mult, p8, p8)
            tt(V(X1), V(X1), V(X2), ALU.add, p8, p8)
            tt(V(FR), V(X1), V(DEN), ALU.mult, p8, p8)
            tt(V(X1), V(AI), V(LR), ALU.mult, p8, p8)
            tt(V(X2), V(AM1), V(LI), ALU.mult, p8, p8)
            tt(V(X1), V(X1), V(X2), ALU.subtract, p8, p8)
            tt(V(FI), V(X1), V(DEN), ALU.mult, p8, p8)
            pw = [pwr, pwi]
            kb.op("dve", lambda: nc.vector.memset(pwr[:, 0, :], 1.0), w=[pwr])
            kb.op("dve", lambda: nc.vector.memset(pwi[:, 0, :], 0.0), w=[pwi])
            kb.op("dve", lambda: nc.vector.tensor_copy(out=pwr[:, 1, :], in_=V(AR)), r=p8, w=[pwr])
            kb.op("dve", lambda: nc.vector.tensor_copy(out=pwi[:, 1, :], in_=V(AI)), r=p8, w=[pwi])
            for k in range(1, 8):
                tt(V(X1), pwr[:, k, :], V(AR), ALU.mult, p8 + pw, p8)
                tt(V(X2), pwi[:, k, :], V(AI), ALU.mult, p8 + pw, p8)
                tt(pwr[:, k + 1, :], V(X1), V(X2), ALU.subtract, p8, [pwr])
                tt(V(X1), pwr[:, k, :], V(AI), ALU.mult, p8 + pw, p8)
                tt(V(X2), pwi[:, k, :], V(AR), ALU.mult, p8 + pw, p8)
                tt(pwi[:, k + 1, :], V(X1), V(X2), ALU.add, p8, [pwi])
            ts(V(X3), V(R), 8.0, ALU.mult, p8, p8)
            ts(V(X1), V(X3), MAGIC, ALU.add, p8, p8)
            ts(V(X1), V(X1), MAGIC, ALU.subtract, p8, p8)
            tt(V(F8), V(X3), V(X1), ALU.subtract, p8, p8)
            for j in range(8):
                ts(nbt[0][:], iota[:], P8[:, F8, j:j + 1], ALU.mult, [iota, P8], [nbt[0]])
                red_sin(sinn[:, j, :], [sinn], nbt[0][:], [nbt[0]], nbt[1][:], nbt[2][:], [nbt[1], nbt[2]])
                red_sin(cosn[:, j, :], [cosn], nbt[0][:], [nbt[0]], nbt[1][:], nbt[2][:], [nbt[1], nbt[2]], shift=0.25)
            bc = lambda i: P8[:, i, :].unsqueeze(2).broadcast_to([128, 8, 128])
            pb = lambda t_, k: t_[:, k, :].unsqueeze(2).broadcast_to([128, 8, 128])
            ts(nCi[:], Cci[:], -1.0, ALU.mult, [Cci], [nCi])
            tt(T1[:], Bcr[:], bc(FR), ALU.mult, [Bcr, P8], [T1])
            tt(T2[:], Bci[:], bc(FI), ALU.mult, [Bci, P8], [T2])
            tt(Bbr[:], T1[:], T2[:], ALU.subtract, [T1, T2], [Bbr])
            tt(T1[:], Bci[:], bc(FR), ALU.mult, [Bci, P8], [T1])
            tt(T2[:], Bcr[:], bc(FI), ALU.mult, [Bcr, P8], [T2])
            tt(Bbi[:], T1[:], T2[:], ALU.add, [T1, T2], [Bbi])
            for tau in range(8):
                tt(T1[:], Bbr[:], pb(pwr, tau), ALU.mult, [Bbr, pwr], [T1])
                tt(T2[:], Bbi[:], pb(pwi, tau), ALU.mult, [Bbi, pwi], [T2])
                tt(Er[:], T1[:], T2[:], ALU.subtract, [T1, T2], [Er])
                tt(T1[:], Bbi[:], pb(pwr, tau), ALU.mult, [Bbi, pwr], [T1])
                tt(T2[:], Bbr[:], pb(pwi, tau), ALU.mult, [Bbr, pwi], [T2])
                tt(Ei[:], T1[:], T2[:], ALU.add, [T1, T2], [Ei])
                for h in range(2):
                    pk = ps[h]
                    for jj in range(4):
                        j = 4 * h + jj
                        kb.op("pe", lambda: nc.tensor.matmul(pk[:, 0:128], lhsT=Er[:, j, :], rhs=Ccr[:, j, :], start=(jj == 0), stop=False), r=[Er, Ccr], w=[pk])
                        kb.op("pe", lambda: nc.tensor.matmul(pk[:, 0:128], lhsT=Ei[:, j, :], rhs=nCi[:, j, :], start=False, stop=(jj == 3)), r=[Ei, nCi], w=[pk])
                    kb.op("act", lambda: nc.scalar.copy(out=Kt[:, tau, h, :], in_=pk[:, 0:128]), r=[pk], w=[Kt])
                sidx = 7 - tau
                for ri, E in enumerate([Er, Ei]):
                    for h in range(2):
                        pt = ps[2 + 2 * ri + h]
                        for jj in range(4):
                            j = 4 * h + jj
                            kb.op("pe", lambda: nc.tensor.transpose(out=pt[:, jj * 128:(jj + 1) * 128], in_=E[:, j, :], identity=c["ident_f"][:]),
                                  r=[E, c["ident_f"]], w=[pt])
                        kb.op("act", lambda: nc.scalar.copy(out=BP[:, sidx, ri, 4 * h:4 * h + 4, :].rearrange("p j f -> p (j f)"), in_=pt[:]), r=[pt], w=[BP])
            for t in range(8):
                tt(T1[:], Ccr[:], pb(pwr, t + 1), ALU.mult, [Ccr, pwr], [T1])
                tt(T2[:], Cci[:], pb(pwi, t + 1), ALU.mult, [Cci, pwi], [T2])
                tt(CP[:, t, 0, :, :], T1[:], T2[:], ALU.subtract, [T1, T2], [CP])
                tt(T1[:], nCi[:], pb(pwr, t + 1), ALU.mult, [nCi, pwr], [T1])
                tt(T2[:], Ccr[:], pb(pwi, t + 1), ALU.mult, [Ccr, pwi], [T2])
                tt(CP[:, t, 1, :, :], T1[:], T2[:], ALU.subtract, [T1, T2], [CP])
            kb.barrier()
        with ExitStack() as ses:
            u = kb.sb(ses, "s5u", [128, 2, L], BF16)
            W = kb.sb(ses, "s5W", [128, 2, 8, NB], F32)
            X0 = kb.sb(ses, "s5X0", [128, 2, 8, NB], BF16)
            tb = [kb.sb(ses, "s5t%d" % i, [128, max(NB, 512)], F32) for i in range(4)]
            yv = kb.sb(ses, "s5yv", [128, 2, 512], F32)
            gf = kb.sb(ses, "s5gf", [128, 2, 512], F32)
            gb = kb.sb(ses, "s5gb", [128, 2, 512], BF16)
            sob = [kb.sb(ses, "s5so%d" % i, [128, 512], BF16) for i in range(2)]
            nchk = (NB + 511) // 512
            csz = NB // nchk
            assert csz * nchk == NB
            for s in range(cfg.NS):
                kb.dma("sp", u[:], self.uT[:, :, s * L:(s + 1) * L].rearrange("k p t -> p k t"), r=[self.u_t[s]], w=[u])
                kb.op("dve", lambda: nc.vector.memset(X0[:], 0.0), w=[X0])
                for j in range(8):
                    h = j // 4
                    uv = u[:, h, :].rearrange("p (m s) -> p s m", s=8)
                    for ck in range(nchk):
                        cs = slice(ck * csz, (ck + 1) * csz)
                        for ri in range(2):
                            pS = ps[2 * ck + ri]
                            for sft in range(8):
                                kb.op("pe", lambda: nc.tensor.matmul(pS[:, :csz], lhsT=BP[:, sft, ri, j, :], rhs=uv[:, sft, cs], start=(sft == 0), stop=(sft == 7)),
                                      r=[BP, u], w=[pS])
                        pSr, pSi = ps[2 * ck], ps[2 * ck + 1]
                        tt(tb[0][:, :csz], pSr[:, :csz], cosn[:, j, cs], ALU.mult, [pSr, cosn], [tb[0]])
                        tt(tb[1][:, :csz], pSi[:, :csz], sinn[:, j, cs], ALU.mult, [pSi, sinn], [tb[1]])
                        tt(W[:, 0, j, cs], tb[0][:, :csz], tb[1][:, :csz], ALU.add, [tb[0], tb[1]], [W])
                        tt(tb[0][:, :csz], pSi[:, :csz], cosn[:, j, cs], ALU.mult, [pSi, cosn], [tb[0]])
                        tt(tb[1][:, :csz], pSr[:, :csz], sinn[:, j, cs], ALU.mult, [pSr, sinn], [tb[1]])
                        tt(W[:, 1, j, cs], tb[0][:, :csz], tb[1][:, :csz], ALU.subtract, [tb[0], tb[1]], [W])
                    for ri in range(2):
                        kb.op("dve", lambda: nc.vector.tensor_tensor_scan(out=tb[2 + ri][:, :NB], data0=m8[:, j:j + 1].broadcast_to([128, NB]), data1=W[:, ri, j, :],
                                                                          initial=0.0, op0=ALU.mult, op1=ALU.add), r=[m8, W], w=[tb[2 + ri]])
                    if NB > 1:
                        a_, b_ = slice(0, NB - 1), slice(1, NB)
                        tt(tb[0][:, a_], tb[2][:, a_], cosn[:, j, a_], ALU.mult, [tb[2], cosn], [tb[0]])
                        tt(tb[1][:, a_], tb[3][:, a_], sinn[:, j, a_], ALU.mult, [tb[3], sinn], [tb[1]])
                        tt(X0[:, 0, j, b_], tb[0][:, a_], tb[1][:, a_], ALU.subtract, [tb[0], tb[1]], [X0])
                        tt(tb[0][:, a_], tb[2][:, a_], sinn[:, j, a_], ALU.mult, [tb[2], sinn], [tb[0]])
                        tt(tb[1][:, a_], tb[3][:, a_], cosn[:, j, a_], ALU.mult, [tb[3], cosn], [tb[1]])
                        tt(X0[:, 1, j, b_], tb[0][:, a_], tb[1][:, a_], ALU.add, [tb[0], tb[1]], [X0])
                for bi, (t0, n) in enumerate(blocks):
                    g0 = self.tok0(s, t0)
                    nb, n0 = n // 8, t0 // 8
                    for h in range(2):
                        pY = ps[4 + h]
                        uv = u[:, h, t0:t0 + n].rearrange("p (m s) -> p s m", s=8)
                        for t in range(8):
                            oap = pY[:, :n].rearrange("p (m s) -> p s m", s=8)[:, t, :]
                            mm = []
                            for sft in range(t + 1):
                                mm.append((Kt[:, t - sft, h, :], uv[:, sft, :], [Kt, u]))
                            for jj in range(4):
                                j = 4 * h + jj
                                mm.append((CP[:, t, 0, j, :], X0[:, 0, j, n0:n0 + nb], [CP, X0]))
                                mm.append((CP[:, t, 1, j, :], X0[:, 1, j, n0:n0 + nb], [CP, X0]))
                            for i, (lh, rh, rr) in enumerate(mm):
                                kb.op("pe", lambda: nc.tensor.matmul(oap, lhsT=lh, rhs=rh, start=(i == 0), stop=(i == len(mm) - 1), skip_group_check=True), r=rr, w=[pY])
                        kb.op("dve", lambda: nc.vector.scalar_tensor_tensor(out=yv[:, h, :n], in0=u[:, h, t0:t0 + n], scalar=dcol[:, h:h + 1], in1=pY[:, :n],
                                                                            op0=ALU.mult, op1=ALU.add), r=[u, dcol, pY], w=[yv])
                        tt(tb[0][:, :n], yv[:, h, :n], yv[:, h, :n], ALU.mult, [yv], [tb[0]])
                        ts(tb[0][:, :n], tb[0][:, :n], 2.0 * GELU_C * 0.044715, ALU.mult, [tb[0]], [tb[0]], s2=2.0 * GELU_C, op1=ALU.add)
                        tt(tb[0][:, :n], tb[0][:, :n], yv[:, h, :n], ALU.mult, [tb[0], yv], [tb[0]])
                        act(tb[1][:, :n], tb[0][:, :n], AF.Exp, [tb[0]], [tb[1]], scale=-1.0)
                        ts(tb[1][:, :n], tb[1][:, :n], 1.0, ALU.add, [tb[1]], [tb[1]])
                        kb.op("dve", lambda: nc.vector.reciprocal(out=tb[1][:, :n], in_=tb[1][:, :n]), r=[tb[1]], w=[tb[1]])
                        tt(gf[:, h, :n], yv[:, h, :n], tb[1][:, :n], ALU.mult, [yv, tb[1]], [gf])
                        kb.op("dve", lambda: nc.vector.tensor_copy(out=gb[:, h, :n], in_=gf[:, h, :n]), r=[gf], w=[gb])
                    for ho in range(2):
                        pV = ps[6 + ho]
                        for hi in range(2):
                            kb.op("pe", lambda: nc.tensor.matmul(pV[:, :n], lhsT=wglu[:, hi, ho * 128:(ho + 1) * 128], rhs=gb[:, hi, :n], start=(hi == 0), stop=(hi == 1)),
                                  r=[wglu, gb], w=[pV])
                        act(tb[2][:, :n], pV[:, :n], AF.Exp, [pV], [tb[2]], scale=-1.0)
                        ts(tb[2][:, :n], tb[2][:, :n], 1.0, ALU.add, [tb[2]], [tb[2]])
                        kb.op("dve", lambda: nc.vector.reciprocal(out=tb[2][:, :n], in_=tb[2][:, :n]), r=[tb[2]], w=[tb[2]])
                        tt(sob[ho][:, :n], gf[:, ho, :n], tb[2][:, :n], ALU.mult, [gf, tb[2]], [sob[ho]])
                        kb.dma("sp", self.mixT[6 + ho, :, g0:g0 + n], sob[ho][:, :n], r=[sob[ho]], w=[self.mix_t[s][bi]])
            kb.barrier()


def _eng_of(self, e):
    return self.kb.eng[e]


Prog.phase_s5 = _phase_s5
Prog.eng_of = _eng_of


def _phase_m2(self, l):
    cfg, nc, kb, d, c = self.cfg, self.nc, self.kb, self.d, self.c
    blocks1 = cfg.blocks()
    TB = 256
    with ExitStack() as es:
        w_out = kb.sb(es, "w_out", [128, 8, D], BF16)
        w_gate = kb.sb(es, "w_gate", [128, 8, D_FF], BF16)
        w_up = kb.sb(es, "w_up", [128, 8, D_FF], BF16)
        w_down = kb.sb(es, "w_down", [128, NFF, D], BF16)
        self.load_w_bf16(w_out, d["w_out"][l], 8)
        self.load_w_bf16(w_gate, d["w_gate"][l], 8)
        self.load_w_bf16(w_up, d["w_up"][l], 8)
        self.load_w_bf16(w_down, d["w_down"][l], NFF)
        n2g = kb.sb(es, "n2g", [128, 8], F32)
        kb.dma("sp", n2g[:], d["n2g"][l], w=[n2g])
        hbs = [kb.sb(es, "m2_h%d" % i, [128, 8, TB], F32) for i in range(2)]
        mixbs = [kb.sb(es, "m2_mix%d" % i, [128, 8, TB], BF16) for i in range(2)]
        hns = [kb.sb(es, "m2_hn%d" % i, [128, 8, TB], BF16) for i in range(2)]
        hsq = kb.sb(es, "m2_hsq", [128, 8, TB], BF16)
        actT = kb.sb(es, "m2_act", [128, NFF, TB], BF16)
        lnv = kb.sb(es, "m2_lnv", [128, TB], F32)
        rstd = kb.sb(es, "m2_rstd", [128, TB], F32)
        ev = [kb.sb(es, "m2_e%d" % i, [128, TB], F32) for i in range(2)]
        ps = [kb.ps(es, "m2ps%d" % i, [128, 512], F32) for i in range(8)]
        last_layer = (l == cfg.DEPTH - 1) and FUSE_OUT
        otile = kb.sb(es, "m2_ot", [128, 512], F32) if last_layer else None
        work = []
        for s in range(cfg.NS):
            for bi, (tb0, nb_) in enumerate(blocks1):
                for t0 in range(tb0, tb0 + nb_, TB):
                    work.append((s, bi, t0, min(TB, tb0 + nb_ - t0)))

        def stage_a(i):
            s, bi, t0, n = work[i]
            hb, mixb, hn = hbs[i % 2], mixbs[i % 2], hns[i % 2]
            g0 = self.tok0(s, t0)
            hview = self.hT[:, :, g0:g0 + n].rearrange("k p t -> p k t")
            kb.dma("sp", mixb[:, :, :n], self.mixT[:, :, g0:g0 + n].rearrange("k p t -> p k t"), r=[self.mix_t[s][bi]], w=[mixb])
            kb.dma("sp", hb[:, :, :n], hview, r=[self.hT_t[s][bi]], w=[hb])
            yield
            for m in range(8):
                p = ps[m % 2]
                for k in range(8):
                    kb.op("pe", lambda: nc.tensor.matmul(p[:, :n], lhsT=w_out[:, k, m * 128:(m + 1) * 128], rhs=mixb[:, k, :n], start=(k == 0), stop=(k == 7)),
                          r=[w_out, mixb], w=[p])
                kb.op("dve", lambda: nc.vector.tensor_tensor(out=hb[:, m, :n], in0=p[:, :n], in1=hb[:, m, :n], op=ALU.add), r=[p, hb], w=[hb])
                yield
            kb.op("act", lambda: nc.scalar.activation(out=hsq[:, :, :n], in_=hb[:, :, :n], func=AF.Square), r=[hb], w=[hsq])
            for k in range(8):
                kb.op("pe", lambda: nc.tensor.matmul(ps[0][:, :n], lhsT=c["ones"][:], rhs=hsq[:, k, :n], start=(k == 0), stop=(k == 7)),
                      r=[hsq, c["ones"]], w=[ps[0]])
            yield
            kb.op("act", lambda: nc.scalar.activation(out=lnv[:, :n], in_=ps[0][:, :n], func=AF.Ln, scale=1.0 / D, bias=c["eps"][:, 0:1]),
                  r=[ps[0], c["eps"]], w=[lnv])
            kb.op("act", lambda: nc.scalar.activation(out=rstd[:, :n], in_=lnv[:, :n], func=AF.Exp, scale=-0.5), r=[lnv], w=[rstd])
            yield
            for k in range(8):
                kb.op("dve", lambda: nc.vector.scalar_tensor_tensor(out=hn[:, k, :n], in0=hb[:, k, :n], scalar=n2g[:, k:k + 1], in1=rstd[:, :n],
                                                                    op0=ALU.mult, op1=ALU.mult), r=[hb, n2g, rstd], w=[hn])
                if k % 2 == 1:
                    yield

        def stage_b(i):
            s, bi, t0, n = work[i]
            hb, hn = hbs[i % 2], hns[i % 2]
            g0 = self.tok0(s, t0)
            hview = self.hT[:, :, g0:g0 + n].rearrange("k p t -> p k t")
            for f in range(NFF):
                pg, pu, e = ps[2 + f % 2], ps[4 + f % 2], ev[f % 2]
                fs = slice(f * 128, (f + 1) * 128)
                for k in range(8):
                    kb.op("pe", lambda: nc.tensor.matmul(pg[:, :n], lhsT=w_gate[:, k, fs], rhs=hn[:, k, :n], start=(k == 0), stop=(k == 7)), r=[w_gate, hn], w=[pg])
                for k in range(8):
                    kb.op("pe", lambda: nc.tensor.matmul(pu[:, :n], lhsT=w_up[:, k, fs], rhs=hn[:, k, :n], start=(k == 0), stop=(k == 7)), r=[w_up, hn], w=[pu])
                kb.op("act", lambda: nc.scalar.activation(out=e[:, :n], in_=pg[:, :n], func=AF.Silu), r=[pg], w=[e])
                kb.op("dve", lambda: nc.vector.tensor_tensor(out=actT[:, f, :n], in0=pu[:, :n], in1=e[:, :n], op=ALU.mult), r=[pu, e], w=[actT])
                yield
            for m in range(8):
                p = ps[6 + m % 2]
                for f in range(NFF):
                    kb.op("pe", lambda: nc.tensor.matmul(p[:, :n], lhsT=w_down[:, f, m * 128:(m + 1) * 128], rhs=actT[:, f, :n], start=(f == 0), stop=(f == NFF - 1)),
                          r=[w_down, actT], w=[p])
                kb.op("dve", lambda: nc.vector.tensor_tensor(out=hb[:, m, :n], in0=p[:, :n], in1=hb[:, m, :n], op=ALU.add), r=[p, hb], w=[hb])
                yield
            if not last_layer:
                kb.dma("sp", hview, hb[:, :, :n], r=[hb], w=[self.hT_t[s][bi]])
                return
            for cc in range(n // CH):
                t = t0 + cc * CH
                if t < CH:
                    continue
                for half in range(2):
                    pT = ps[6 + half]
                    for q in range(4):
                        kb.op("pe", lambda: nc.tensor.transpose(out=pT[:, q * 128:(q + 1) * 128], in_=hb[:, 4 * half + q, cc * CH:(cc + 1) * CH], identity=c["ident_f"][:]),
                              r=[hb, c["ident_f"]], w=[pT])
                    kb.op("act", lambda: nc.scalar.copy(out=otile[:], in_=pT[:]), r=[pT], w=[otile])
                    kb.dma("sp", self.out[s, t - CH:t, half * 512:(half + 1) * 512], otile[:], r=[otile], w=[self.out_t])
                yield

        def drain(g):
            for _ in g:
                pass

        drain(stage_a(0))
        for i in range(len(work)):
            bg = stage_b(i)
            if i + 1 < len(work):
                ag = stage_a(i + 1)
                cnt = 0
                for _ in bg:
                    cnt += 1
                    if cnt % 2 == 0:
                        next(ag, None)
                drain(ag)
            else:
                drain(bg)
        kb.barrier()


def _phase_f(self):
    cfg, nc, kb, d, c = self.cfg, self.nc, self.kb, self.d, self.c
    blocks = cfg.blocks()
    with ExitStack() as es:
        hb = [kb.sb(es, "f_h%d" % i, [128, 8, 128], F32) for i in range(2)]
        ot = [kb.sb(es, "f_o%d" % i, [128, D], F32) for i in range(2)]
        ps = [kb.ps(es, "f_ps%d" % i, [128, 1024], F32) for i in range(2)]
        it = 0
        for s in range(cfg.NS):
            for ch in range(1, cfg.NCH):
                a, b, p = hb[it % 2], ot[it % 2], ps[it % 2]
                t = ch * CH
                bi = [i for i, (t0, n) in enumerate(blocks) if t0 <= t < t0 + n][0]
                g0 = self.tok0(s, t)
                kb.dma("sp", a[:], self.hT[:, :, g0:g0 + CH].rearrange("k p t -> p k t"), r=[self.hT_t[s][bi]], w=[a])
                for k in range(8):
                    kb.op("pe", lambda: nc.tensor.transpose(out=p[:, k * 128:(k + 1) * 128], in_=a[:, k, :], identity=c["ident_f"][:]), r=[a, c["ident_f"]], w=[p])
                kb.op("act", lambda: nc.scalar.copy(out=b[:], in_=p[:]), r=[p], w=[b])
                kb.dma("sp", self.out[s, (ch - 1) * CH:ch * CH, :], b[:], r=[b], w=[self.out_t])
                it += 1
        kb.barrier()


Prog.phase_m2 = _phase_m2
Prog.phase_f = _phase_f


_CACHE = {}


def kernel(**inputs):
    n_cores = 8
    x = np.asarray(inputs["x"], np.float32)
    B, S, _ = x.shape
    NS = B // n_cores
    NCH = S // CH + 1
    DEPTH = int(np.asarray(inputs["w_in"]).shape[0])
    key = (NS, NCH, DEPTH)
    if key not in _CACHE:
        cfg = Cfg(NS=NS, NCH=NCH, DEPTH=DEPTH)
        P = Prog(cfg)
        P.build()
        _CACHE[key] = (cfg, P)
    cfg, P = _CACHE[key]
    in_maps = [make_in_map(cfg, P.cn, inputs, x[NS * i:NS * (i + 1)]) for i in range(n_cores)]
    res = run_bass_kernel_spmd(P.nc, in_maps, core_ids=list(range(n_cores)))
    out = np.concatenate([np.asarray(res.results[i]["out"]).reshape(NS, S, D) for i in range(n_cores)], axis=0)
    return out.astype(np.float32)
```

```python
import math
import os
FUSE_IN = int(os.environ.get('FUSE_IN', '1'))
FUSE_OUT = int(os.environ.get('FUSE_OUT', '1'))
INTERLEAVE = int(os.environ.get('INTERLEAVE', '1'))
GN = int(os.environ.get('GN', '3'))
SKIPDVE = int(os.environ.get('SKIPDVE', '0'))
SKIPMM = int(os.environ.get('SKIPMM', '0'))
SCB = int(os.environ.get('SCB', '3'))
RET_STOP = int(os.environ.get('RET_STOP', '99'))
from contextlib import ExitStack

import numpy as np
import ml_dtypes

import concourse.bass as bass
import concourse.mybir as mybir
from concourse.bass_utils import run_bass_kernel_spmd

F32 = mybir.dt.float32
BF16 = mybir.dt.bfloat16
AF = mybir.ActivationFunctionType
ALU = mybir.AluOpType

D = 1024
HD = 64
SBW = 384
RETW = 384
S5W = 256
D_IN = 2944
D_FF = 2816
NFF = D_FF // 128
CH = 128
N_META = 16
PAD_FRONT = 112
EPS = 1e-6
C_SQ, C_SK, C_SV, C_RQ, C_RK, C_RV, C_RG, C_U = 0, 384, 768, 1152, 1536, 1920, 2304, 2688


class T:
    def __init__(self, h):
        self.h = h
        self.w = None
        self.r = {}

    def __getitem__(self, idx):
        return self.h[idx]


class KB:
    def __init__(self, nc, es):
        self.nc = nc
        self.es = es
        self.eng = {"pe": nc.tensor, "act": nc.scalar, "dve": nc.vector, "pool": nc.gpsimd, "sp": nc.sync}
        self.semh = {}
        self.cnt = {}
        for e in self.eng:
            self.semh[e] = es.enter_context(nc.semaphore("s_" + e))
            self.cnt[e] = 0
        self.waited = {}
        self.ndma = 24
        self.dma_tot = [0] * self.ndma
        for i in range(self.ndma):
            self.semh["d%d" % i] = es.enter_context(nc.semaphore("s_d%d" % i))
        self.dma_next = 0
        self.n_ins = 0
        self.sw_sems = []

    def _nm(self, name):
        self.n_names = getattr(self, "n_names", 0) + 1
        return "t%d_%s" % (self.n_names, name)

    def sb(self, es, name, shape, dt):
        return T(es.enter_context(self.nc.sbuf_tensor(self._nm(name), list(shape), dt)))

    def ps(self, es, name, shape, dt=F32):
        return T(es.enter_context(self.nc.psum_tensor(self._nm(name), list(shape), dt)))

    def _wait(self, e, k, v):
        if self.waited.get((e, k), 0) >= v:
            return
        self.eng[e].wait_ge(self.semh[k], v)
        self.waited[(e, k)] = v

    def _deps(self, e, r, w):
        deps = {}

        def add(kv):
            k, v = kv
            if deps.get(k, 0) < v:
                deps[k] = v

        for t in r:
            if t.w:
                add(t.w)
        for t in w:
            if t.w:
                add(t.w)
            for kv in t.r.items():
                add(kv)
        for k, v in deps.items():
            if k == e and e == "pe":
                continue
            self._wait(e, k, v)

    def op(self, e, fn, r=(), w=()):
        self._deps(e, r, w)
        ins = fn()
        self.cnt[e] += 1
        ins.then_inc(self.semh[e], 1)
        seq = self.cnt[e]
        for t in r:
            t.r[e] = max(t.r.get(e, 0), seq)
        for t in w:
            t.w = (e, seq)
            t.r = {}
        self.n_ins += 1
        return ins

    def dma(self, q, out, in_, r=(), w=(), **kw):
        if q == "pool":
            k = "sw%d" % len(self.sw_sems)
            self.semh[k] = self.es.enter_context(self.nc.semaphore("s_" + k))
            self.sw_sems.append(k)
            self._deps(q, r, w)
            ins = self.eng[q].dma_start(out=out, in_=in_, **kw)
            ins.then_inc(self.semh[k], 16)
            for t in r:
                t.r[k] = 16
            for t in w:
                t.w = (k, 16)
                t.r = {}
            self.n_ins += 1
            return ins
        i = self.dma_next
        self.dma_next = (i + 1) % self.ndma
        k = "d%d" % i
        self._wait(q, k, self.dma_tot[i])
        self._deps(q, r, w)
        ins = self.eng[q].dma_start(out=out, in_=in_, **kw)
        self.dma_tot[i] += 16
        ins.then_inc(self.semh[k], 16)
        v = self.dma_tot[i]
        for t in r:
            t.r[k] = max(t.r.get(k, 0), v)
        for t in w:
            t.w = (k, v)
            t.r = {}
        self.n_ins += 1
        return ins

    def barrier(self):
        for e in self.eng:
            for k in list(self.eng.keys()):
                if self.cnt[k] > 0:
                    self._wait(e, k, self.cnt[k])
            for i in range(self.ndma):
                if self.dma_tot[i] > 0:
                    self._wait(e, "d%d" % i, self.dma_tot[i])
            for k in self.sw_sems:
                self._wait(e, k, 16)


def _consts(L):
    c = {}
    idx = np.arange(128)
    c["ident"] = np.eye(128, dtype=np.float32)
    c["ones"] = np.ones((128, 128), np.float32)
    bo = np.zeros((128, 128), np.float32)
    bo[:64, :64] = 1.0
    bo[64:, 64:] = 1.0
    c["bones"] = bo
    c["U"] = (idx[:, None] >= idx[None, :]).astype(np.float32)
    c["Ls"] = (idx[:, None] < idx[None, :]).astype(np.float32)
    c["msb"] = (idx[:, None] < idx[None, :]).astype(np.float32)
    c["mret"] = (idx[:, None] <= idx[None, :]).astype(np.float32)
    pb = np.zeros((128, 1), np.float32)
    pb[:PAD_FRONT] = -100.0
    c["padb"] = pb
    PT = np.zeros((128, 128), np.float32)
    for fp in range(128):
        if (fp % 64) < 32:
            PT[fp + 32, fp] = -1.0
        else:
            PT[fp - 32, fp] = 1.0
    c["PT"] = PT
    half = 32
    inv = (10000.0 ** (-np.arange(half, dtype=np.float32) / half)).astype(np.float32)
    pos = (np.arange(L) - PAD_FRONT).astype(np.float32)
    ang = (pos[None, :] * inv[:, None]).astype(np.float32)
    cosv = np.cos(ang).astype(np.float32)
    sinv = np.sin(ang).astype(np.float32)
    c["cosT"] = np.tile(cosv, (4, 1))
    c["sinT"] = np.tile(sinv, (4, 1))
    lg = np.log1p(-np.exp2(-5.0 - np.arange(6, dtype=np.float32))).astype(np.float32)
    i = np.arange(128, dtype=np.float32)
    qd = np.zeros((128, 3, 128), np.float32)
    kd = np.zeros((128, 3, 128), np.float32)
    cd = np.zeros((128, 3), np.float32)
    for h in range(6):
        p, hh = h // 2, h % 2
        qd[64 * hh:64 * hh + 64, p, :] = np.exp(lg[h] * (i + 1.0))[None, :]
        kd[64 * hh:64 * hh + 64, p, :] = (np.exp(-lg[h] * (i + 1.0)) * (HD ** -0.5))[None, :]
        cd[64 * hh:64 * hh + 64, p] = np.exp(lg[h] * 128.0)
    c["iota"] = np.tile(np.arange(L // 8, dtype=np.float32)[None, :], (128, 1))
    c["qdec"] = qd
    c["kdec"] = kd
    c["cdec"] = cd
    return c


class Cfg:
    def __init__(self, NS=2, NCH=33, DEPTH=2, do_ret=True, do_s5=True, do_m2=True, dbg=False):
        self.NS, self.NCH, self.DEPTH = NS, NCH, DEPTH
        self.L = NCH * CH
        self.NT = NS * self.L
        self.do_ret, self.do_s5, self.do_m2, self.dbg = do_ret, do_s5, do_m2, dbg

    def blocks(self, bs=512):
        out = []
        t = 0
        rem = self.L % bs
        if rem:
            out.append((0, rem))
            t = rem
        while t < self.L:
            out.append((t, bs))
            t += bs
        return out


class Prog:
    def __init__(self, cfg):
        self.cfg = cfg
        self.nc = bass.Bass("TRN2", target_bir_lowering=False)
        self.cn = _consts(cfg.L)

    def dram_in(self, name, shape, dt=F32):
        return self.nc.dram_tensor(name, list(shape), dt, kind="ExternalInput").ap()

    def build(self):
        cfg, nc = self.cfg, self.nc
        NS, NCH, L, NT, DEPTH = cfg.NS, cfg.NCH, cfg.L, cfg.NT, cfg.DEPTH
        d = {}
        d["x"] = self.dram_in("x", [NS, (NCH - 1) * CH, D])
        d["meta"] = self.dram_in("meta", [N_META, D])
        d["n1g"] = self.dram_in("n1g", [DEPTH, 128, 8])
        d["n2g"] = self.dram_in("n2g", [DEPTH, 128, 8])
        d["hg"] = self.dram_in("hg", [DEPTH, 128, 4])
        d["rog"] = self.dram_in("rog", [DEPTH, 128, 3])
        d["w_in"] = self.dram_in("w_in", [DEPTH, D, D_IN])
        d["w_out"] = self.dram_in("w_out", [DEPTH, D, D])
        d["w_gate"] = self.dram_in("w_gate", [DEPTH, D, D_FF])
        d["w_up"] = self.dram_in("w_up", [DEPTH, D, D_FF])
        d["w_down"] = self.dram_in("w_down", [DEPTH, D_FF, D])
        for nm in ["lam_r", "lam_i", "ldt"]:
            d[nm] = self.dram_in(nm, [DEPTH, 128, 8])
        for nm in ["Bcr", "Bci", "Ccr", "Cci"]:
            d[nm] = self.dram_in(nm, [DEPTH, 128, 8, 128])
        d["s5d"] = self.dram_in("s5d", [DEPTH, 128, 2])
        d["w_glu"] = self.dram_in("w_glu", [DEPTH, S5W, S5W])
        for k, v in self.cn.items():
            d["c_" + k] = self.dram_in("c_" + k, v.shape)
        self.d = d
        self.out = nc.dram_tensor("out", [NS, (NCH - 1) * CH, D], F32, kind="ExternalOutput").ap()
        kind_dbg = "ExternalOutput" if cfg.dbg else "Internal"
        self.hT = nc.dram_tensor("hT", [8, 128, NT], F32, kind=kind_dbg).ap()
        self.mixT = nc.dram_tensor("mixT", [8, 128, NT], BF16, kind=kind_dbg).ap()
        self.uT = nc.dram_tensor("uT", [2, 128, NT], BF16, kind=kind_dbg).ap()
        self.hT_t = [[T(None) for _ in cfg.blocks()] for _ in range(NS)]
        self.mix_t = [[T(None) for _ in cfg.blocks()] for _ in range(NS)]
        self.u_t = [T(None) for _ in range(NS)]
        self.out_t = T(None)

        with ExitStack() as es:
            self.kb = kb = KB(nc, es)
            self.load_consts(es)
            if not FUSE_IN:
                self.phase0()
            for l in range(DEPTH):
                self.phase_m1(l)
                if cfg.do_s5:
                    self.phase_s5(l)
                if cfg.do_m2:
                    self.phase_m2(l)
            if not (FUSE_OUT and cfg.do_m2):
                self.phase_f()
            kb.barrier()
        return nc

    def load_consts(self, es):
        kb, nc, d = self.kb, self.nc, self.d
        c = {}
        names = ["ident", "ones", "bones", "U", "Ls", "msb", "mret", "PT"]
        c["ident_f"] = kb.sb(es, "k_ident_f", [128, 128], F32)
        for name in names:
            c[name] = kb.sb(es, "k_" + name, [128, 128], BF16)
        c["padb"] = kb.sb(es, "k_padb", [128, 1], F32)
        c["eps"] = kb.sb(es, "k_eps", [128, 1], F32)
        c["qdec"] = kb.sb(es, "k_qdec", [128, 3, 128], F32)
        c["kdec"] = kb.sb(es, "k_kdec", [128, 3, 128], F32)
        c["cdec"] = kb.sb(es, "k_cdec", [128, 3], F32)
        with ExitStack() as tmp:
            for name in names:
                st = kb.sb(tmp, "st_" + name, [128, 128], F32)
                kb.dma("sp", st[:], d["c_" + name], w=[st])
                if name == "ident":
                    kb.op("dve", lambda: nc.vector.tensor_copy(out=c["ident_f"][:], in_=st[:]), r=[st], w=[c["ident_f"]])
                kb.op("dve", lambda: nc.vector.tensor_copy(out=c[name][:], in_=st[:]), r=[st], w=[c[name]])
            kb.barrier()
        kb.dma("sp", c["padb"][:], d["c_padb"], w=[c["padb"]])
        kb.op("dve", lambda: nc.vector.memset(c["eps"][:], EPS), w=[c["eps"]])
        kb.dma("sp", c["qdec"][:], d["c_qdec"], w=[c["qdec"]])
        kb.dma("sp", c["kdec"][:], d["c_kdec"], w=[c["kdec"]])
        kb.dma("sp", c["cdec"][:], d["c_cdec"], w=[c["cdec"]])
        self.c = c

    def tok0(self, s, t):
        return s * self.cfg.L + t

    def phase0(self):
        cfg, nc, kb, d, c = self.cfg, self.nc, self.kb, self.d, self.c
        blocks = cfg.blocks()
        with ExitStack() as es:
            xt = [kb.sb(es, "p0_xt%d" % i, [128, D], F32) for i in range(2)]
            xT = [kb.sb(es, "p0_xT%d" % i, [128, 8, 128], F32) for i in range(2)]
            ps = [kb.ps(es, "p0_ps%d" % i, [128, 1024], F32) for i in range(2)]
            it = 0
            for s in range(cfg.NS):
                for ch in range(cfg.NCH):
                    a, b, p = xt[it % 2], xT[it % 2], ps[it % 2]
                    if ch == 0:
                        kb.op("dve", lambda: nc.vector.memset(a[:], 0.0), w=[a])
                        kb.dma("sp", a[PAD_FRONT:128, :], d["meta"], w=[a])
                    else:
                        kb.dma("sp", a[:], d["x"][s, (ch - 1) * CH:ch * CH, :], w=[a])
                    for k in range(8):
                        kb.op("pe", lambda: nc.tensor.transpose(out=p[:, k * 128:(k + 1) * 128], in_=a[:, k * 128:(k + 1) * 128],
                                                               identity=c["ident_f"][:]), r=[a, c["ident_f"]], w=[p])
                    kb.op("act", lambda: nc.scalar.copy(out=b[:].rearrange("p k t -> p (k t)"), in_=p[:]), r=[p], w=[b])
                    t = ch * CH
                    bi = [i for i, (t0, n) in enumerate(blocks) if t0 <= t < t0 + n][0]
                    g0 = self.tok0(s, t)
                    kb.dma("sp", self.hT[:, :, g0:g0 + CH].rearrange("k p t -> p k t"), b[:], r=[b], w=[self.hT_t[s][bi]])
                    it += 1
            kb.barrier()

    def load_w_bf16(self, dst, src2d, nk):
        kb = self.kb
        ncols = src2d.shape[1]
        src = src2d.rearrange("(k p) c -> p k c", p=128)
        step = 1024
        for c0 in range(0, ncols, step):
            c1 = min(ncols, c0 + step)
            kb.dma("pool", dst[:, :, c0:c1], src[:, :, c0:c1], w=[dst])

    def rmsnorm_fm(self, n, src, src_t, gcol, hsq, ps_ss, lnv, rstd, hn):
        nc, kb, c = self.nc, self.kb, self.c
        kb.op("act", lambda: nc.scalar.activation(out=hsq[:, :, :n], in_=src[:, :, :n], func=AF.Square), r=[src_t], w=[hsq])
        for k in range(8):
            kb.op("pe", lambda: nc.tensor.matmul(ps_ss[:, :n], lhsT=c["ones"][:], rhs=hsq[:, k, :n], start=(k == 0), stop=(k == 7)),
                  r=[hsq, c["ones"]], w=[ps_ss])
        kb.op("act", lambda: nc.scalar.activation(out=lnv[:, :n], in_=ps_ss[:, :n], func=AF.Ln, scale=1.0 / D, bias=c["eps"][:, 0:1]),
              r=[ps_ss, c["eps"]], w=[lnv])
        kb.op("act", lambda: nc.scalar.activation(out=rstd[:, :n], in_=lnv[:, :n], func=AF.Exp, scale=-0.5), r=[lnv], w=[rstd])
        for k in range(8):
            kb.op("dve", lambda: nc.vector.scalar_tensor_tensor(out=hn[:, k, :n], in0=src[:, k, :n], scalar=gcol[:, k:k + 1], in1=rstd[:, :n],
                                                                op0=ALU.mult, op1=ALU.mult), r=[src_t, gcol, rstd], w=[hn])

    def headnorm(self, P, n, gcol_ap, gt, dst_ap, dst_t, W):
        nc, kb, c = self.nc, self.kb, self.c
        sq, ps_ss, lnv, rs = W["sq"], W["ps_ss"], W["lnv"], W["rs"]
        kb.op("act", lambda: nc.scalar.activation(out=sq[:, :n], in_=P[:, :n], func=AF.Square), r=[P], w=[sq])
        kb.op("pe", lambda: nc.tensor.matmul(ps_ss[:, :n], lhsT=c["bones"][:], rhs=sq[:, :n], start=True, stop=True), r=[sq, c["bones"]], w=[ps_ss])
        kb.op("act", lambda: nc.scalar.activation(out=lnv[:, :n], in_=ps_ss[:, :n], func=AF.Ln, scale=1.0 / HD, bias=c["eps"][:, 0:1]),
              r=[ps_ss, c["eps"]], w=[lnv])
        kb.op("act", lambda: nc.scalar.activation(out=rs[:, :n], in_=lnv[:, :n], func=AF.Exp, scale=-0.5), r=[lnv], w=[rs])
        kb.op("dve", lambda: nc.vector.scalar_tensor_tensor(out=dst_ap, in0=P[:, :n], scalar=gcol_ap, in1=rs[:, :n], op0=ALU.mult, op1=ALU.mult),
              r=[P, gt, rs], w=[dst_t])

    def headnorm_g(self, P, n, gcol_ap, gt, dst_ap, dst_t, W):
        nc, kb, c = self.nc, self.kb, self.c
        sq, ps_ss, lnv, rs = W["sq"], W["ps_ss"], W["lnv"], W["rs"]
        yield
        kb.op("act", lambda: nc.scalar.activation(out=sq[:, :n], in_=P[:, :n], func=AF.Square), r=[P], w=[sq])
        yield
        kb.op("pe", lambda: nc.tensor.matmul(ps_ss[:, :n], lhsT=c["bones"][:], rhs=sq[:, :n], start=True, stop=True), r=[sq, c["bones"]], w=[ps_ss])
        yield
        kb.op("act", lambda: nc.scalar.activation(out=lnv[:, :n], in_=ps_ss[:, :n], func=AF.Ln, scale=1.0 / HD, bias=c["eps"][:, 0:1]),
              r=[ps_ss, c["eps"]], w=[lnv])
        kb.op("act", lambda: nc.scalar.activation(out=rs[:, :n], in_=lnv[:, :n], func=AF.Exp, scale=-0.5), r=[lnv], w=[rs])
        yield
        kb.op("dve", lambda: nc.vector.scalar_tensor_tensor(out=dst_ap, in0=P[:, :n], scalar=gcol_ap, in1=rs[:, :n], op0=ALU.mult, op1=ALU.mult),
              r=[P, gt, rs], w=[dst_t])
        yield

    def phase_m1(self, l):
        cfg, nc, kb, d, c = self.cfg, self.nc, self.kb, self.d, self.c
        L, NCH = cfg.L, cfg.NCH
        blocks = cfg.blocks()
        with ExitStack() as es:
            w_in = kb.sb(es, "w_in", [128, 8, D_IN], BF16)
            self.load_w_bf16(w_in, d["w_in"][l], 8)
            n1g = kb.sb(es, "n1g", [128, 8], F32)
            kb.dma("sp", n1g[:], d["n1g"][l], w=[n1g])
            hg = kb.sb(es, "hg", [128, 4], F32)
            kb.dma("sp", hg[:], d["hg"][l], w=[hg])
            hg8 = kb.sb(es, "hg8", [128, 1], F32)
            kb.op("dve", lambda: nc.vector.tensor_scalar(out=hg8[:], in0=hg[:, 0:1], scalar1=HD ** -0.5, scalar2=None, op0=ALU.mult), r=[hg], w=[hg8])
            KT = [kb.sb(es, "KT%d" % p, [128, L], BF16) for p in range(3)]
            KT_t = [[T(None) for _ in blocks] for p in range(3)]
            Vt = kb.sb(es, "Vtok", [128, NCH, SBW], BF16)
            Vt_t = [T(None) for _ in blocks]
            hT_sb = kb.sb(es, "hT_sb", [128, 8, 512], F32)
            hsq = kb.sb(es, "hsq", [128, 8, 512], BF16)
            hn = kb.sb(es, "hn", [128, 8, 512], BF16)
            lnv = kb.sb(es, "lnv", [128, 512], F32)
            rstd = kb.sb(es, "rstd", [128, 512], F32)
            W = {"sq": kb.sb(es, "hn_sq", [128, 512], BF16), "lnv": kb.sb(es, "hn_lnv", [128, 512], F32),
                 "rs": kb.sb(es, "hn_rs", [128, 512], F32)}
            QN = [kb.sb(es, "QN%d" % p, [128, 512], BF16) for p in range(3)]
            NQ = [kb.sb(es, "NQ%d" % p, [128, 512], BF16) for p in range(3)]
            ebf = [[kb.sb(es, "ebf%d_%d" % (hh, i), [128, 512], BF16) for i in range(2)] for hh in range(2)]
            xc = [[kb.sb(es, "xc%d_%d" % (hh, i), [128, 512], BF16) for i in range(2)] for hh in range(2)]
            sp = [[kb.sb(es, "sp%d_%d" % (hh, i), [128, 512], BF16) for i in range(3)] for hh in range(2)]
            wt = [[kb.sb(es, "wt%d_%d" % (hh, i), [128, 512], BF16) for i in range(2)] for hh in range(2)]
            mixblk = [kb.sb(es, "mixblk%d" % i, [128, 512], BF16) for i in range(2)]
            ublk = kb.sb(es, "ublk", [128, 2, 512], BF16)
            rog = kb.sb(es, "rog", [128, 3], F32)
            kb.dma("sp", rog[:], d["rog"][l], w=[rog])
            cosb = kb.sb(es, "cosb", [128, 512], F32)
            sinb = kb.sb(es, "sinb", [128, 512], F32)
            rn = kb.sb(es, "rn", [128, 512], BF16)
            tmp1 = kb.sb(es, "tmp1", [128, 512], F32)
            tmp2 = kb.sb(es, "tmp2", [128, 512], F32)
            kp = [kb.sb(es, "kp%d" % p, [128, 512], BF16) for p in range(3)]
            qp = [kb.sb(es, "qp%d" % p, [128, 2, 512], BF16) for p in range(3)]
            for p in range(3):
                kb.op("dve", lambda: nc.vector.memset(qp[p][:], 0.0), w=[qp[p]])
            kptok = [kb.sb(es, "kptok%d" % p, [128, 4, 128], BF16) for p in range(3)]
            rvt = kb.sb(es, "rvt", [128, 4, RETW], BF16)
            gate = [kb.sb(es, "gate%d" % p, [128, 512], F32) for p in range(3)]
            scm = kb.sb(es, "scm", [128, 2, 128], BF16)
            st32 = [kb.sb(es, "st32_%d" % p, [128, HD], F32) for p in range(3)]
            stbf = [kb.sb(es, "stbf_%d" % p, [128, HD], BF16) for p in range(3)]
            ps = [kb.ps(es, "m1ps%d" % i, [128, 512], F32) for i in range(8)]
            X0, X1, X2 = ps[5], ps[6], ps[7]
            W["ps_ss"] = X1
            QN2 = [QN, [kb.sb(es, "QNb%d" % p, [128, 512], BF16) for p in range(3)]]
            xh = kb.sb(es, "xh", [128, 512], F32) if (l == 0 and FUSE_IN) else None
            mixc = [0]

            def next_mb():
                mb = mixblk[mixc[0] % 2]
                mixc[0] += 1
                return mb

            def stage_p(s, bi, part=0):
                t0, n = blocks[bi]
                g0 = self.tok0(s, t0)
                nq = n // CH
                qc0 = t0 // CH
                QNc = QN2[bi % 2]
                def proj_fm(P, col0):
                    for k in range(8):
                        kb.op("pe", lambda: nc.tensor.matmul(P[:, :n], lhsT=w_in[:, k, col0:col0 + 128], rhs=hn[:, k, :n], start=(k == 0), stop=(k == 7)),
                              r=[w_in, hn], w=[P])

                if part in (0, 1):
                    if bi == 0:
                        for p in range(3):
                            kb.op("dve", lambda: nc.vector.memset(st32[p][:], 0.0), w=[st32[p]])
                            kb.op("dve", lambda: nc.vector.memset(stbf[p][:], 0.0), w=[stbf[p]])
                    if l == 0 and FUSE_IN:
                        for j in range(nq):
                            ch = qc0 + j
                            for half in range(2):
                                fs_ = slice(half * 512, (half + 1) * 512)
                                if ch == 0:
                                    kb.op("dve", lambda: nc.vector.memset(xh[:], 0.0), w=[xh])
                                    kb.dma("sp", xh[PAD_FRONT:128, :], d["meta"][:, fs_], w=[xh])
                                else:
                                    kb.dma("sp", xh[:], d["x"][s, (ch - 1) * CH:ch * CH, fs_], w=[xh])
                                for q in range(4):
                                    kb.op("pe", lambda: nc.tensor.transpose(out=X2[:, q * 128:(q + 1) * 128], in_=xh[:, q * 128:(q + 1) * 128], identity=c["ident_f"][:]),
                                          r=[xh, c["ident_f"]], w=[X2])
                                kb.op("act", lambda: nc.scalar.copy(out=hT_sb[:, 4 * half:4 * half + 4, j * CH:(j + 1) * CH],
                                                                    in_=X2[:, :].rearrange("p (q t) -> p q t", t=CH)), r=[X2], w=[hT_sb])
                                yield
                        kb.dma("sp", self.hT[:, :, g0:g0 + n].rearrange("k p t -> p k t"), hT_sb[:, :, :n], r=[hT_sb], w=[self.hT_t[s][bi]])
                    else:
                        kb.dma("sp", hT_sb[:, :, :n], self.hT[:, :, g0:g0 + n].rearrange("k p t -> p k t"), r=[self.hT_t[s][bi]], w=[hT_sb])
                    self.rmsnorm_fm(n, hT_sb, hT_sb, n1g, hsq, X1, lnv, rstd, hn)
                    yield
                for p in (range(3) if part in (0, 1) else []):
                    proj_fm(X0, C_SK + p * 128)
                    yield
                    proj_fm(X2, C_SQ + p * 128)
                    yield from self.headnorm_g(X0, n, hg[:, 1:2], hg, KT[p][:, t0:t0 + n], KT_t[p][bi], W)
                    yield from self.headnorm_g(X2, n, hg8[:, 0:1], hg8, QNc[p][:, :n], QNc[p], W)
                for j in (range(nq) if part in (0, 1) else []):
                    for k in range(8):
                        kb.op("pe", lambda: nc.tensor.matmul(X2[:, :SBW], lhsT=hn[:, k, j * CH:(j + 1) * CH], rhs=w_in[:, k, C_SV:C_SV + SBW],
                                                             start=(k == 0), stop=(k == 7)), r=[w_in, hn], w=[X2])
                    kb.op("dve", lambda: nc.vector.tensor_copy(out=Vt[:, qc0 + j, :], in_=X2[:, :SBW]), r=[X2], w=[Vt_t[bi]])
                    yield
                if part == 1:
                    return
                for uh in range(2):
                    proj_fm(X0, C_U + uh * 128)
                    kb.op("dve", lambda: nc.vector.tensor_copy(out=ublk[:, uh, :n], in_=X0[:, :n]), r=[X0], w=[ublk])
                    yield
                kb.dma("sp", self.uT[:, :, g0:g0 + n].rearrange("k p t -> p k t"), ublk[:, :, :n], r=[ublk], w=[self.u_t[s]])
                if not cfg.do_ret:
                    return
                kb.dma("sp", cosb[:, :n], d["c_cosT"][:, t0:t0 + n], w=[cosb])
                kb.dma("sp", sinb[:, :n], d["c_sinT"][:, t0:t0 + n], w=[sinb])

                def rot(dst, gi, dec, p, padded=False):
                    yield from self.headnorm_g(X0, n, hg[:, gi:gi + 1], hg, rn[:, :n], rn, W)
                    kb.op("pe", lambda: nc.tensor.matmul(X2[:, :n], lhsT=c["PT"][:], rhs=rn[:, :n], start=True, stop=True), r=[c["PT"], rn], w=[X2])
                    kb.op("dve", lambda: nc.vector.tensor_tensor(out=tmp1[:, :n], in0=rn[:, :n], in1=cosb[:, :n], op=ALU.mult), r=[rn, cosb], w=[tmp1])
                    yield
                    kb.op("dve", lambda: nc.vector.tensor_tensor(out=tmp2[:, :n], in0=X2[:, :n], in1=sinb[:, :n], op=ALU.mult), r=[X2, sinb], w=[tmp2])
                    kb.op("dve", lambda: nc.vector.tensor_tensor(out=tmp1[:, :n], in0=tmp1[:, :n], in1=tmp2[:, :n], op=ALU.add), r=[tmp1, tmp2], w=[tmp1])
                    if padded:
                        for hh in range(2):
                            rs_ = slice(64 * hh, 64 * hh + 64)
                            kb.op("dve", lambda: nc.vector.tensor_tensor(out=dst[rs_, hh, :n].rearrange("p (j i) -> p j i", i=CH),
                                                                          in0=tmp1[rs_, :n].rearrange("p (j i) -> p j i", i=CH),
                                                                          in1=dec[rs_, p, :].unsqueeze(1).broadcast_to([64, nq, CH]), op=ALU.mult),
                                  r=[tmp1, dec], w=[dst])
                    else:
                        kb.op("dve", lambda: nc.vector.tensor_tensor(out=dst[:, :n].rearrange("p (j i) -> p j i", i=CH),
                                                                      in0=tmp1[:, :n].rearrange("p (j i) -> p j i", i=CH),
                                                                      in1=dec[:, p, :].unsqueeze(1).broadcast_to([128, nq, CH]), op=ALU.mult),
                              r=[tmp1, dec], w=[dst])

                for p in range(3):
                    proj_fm(X0, C_RK + p * 128)
                    yield from rot(kp[p], 3, c["kdec"], p)
                    yield
                    proj_fm(X0, C_RQ + p * 128)
                    yield from rot(qp[p], 2, c["qdec"], p, padded=True)
                    yield
                    pst = X2[:].bitcast(BF16)
                    for j in range(nq):
                        kb.op("pe", lambda: nc.tensor.transpose(out=pst[:, j * CH:(j + 1) * CH], in_=kp[p][:, j * CH:(j + 1) * CH], identity=c["ident"][:]),
                              r=[kp[p], c["ident"]], w=[X2])
                    kb.op("dve", lambda: nc.vector.tensor_copy(out=kptok[p][:, :nq, :].rearrange("p j f -> p (j f)"), in_=pst[:, :n]), r=[X2], w=[kptok[p]])
                    yield
                    proj_fm(X0, C_RG + p * 128)
                    yield
                    kb.op("act", lambda: nc.scalar.activation(out=tmp1[:, :n], in_=X0[:, :n], func=AF.Exp, scale=-1.0), r=[X0], w=[tmp1])
                    yield
                    kb.op("dve", lambda: nc.vector.tensor_scalar(out=tmp1[:, :n], in0=tmp1[:, :n], scalar1=1.0, scalar2=None, op0=ALU.add), r=[tmp1], w=[tmp1])
                    kb.op("dve", lambda: nc.vector.reciprocal(out=tmp1[:, :n], in_=tmp1[:, :n]), r=[tmp1], w=[tmp1])
                    kb.op("dve", lambda: nc.vector.tensor_tensor(out=gate[p][:, :n], in0=X0[:, :n], in1=tmp1[:, :n], op=ALU.mult), r=[X0, tmp1], w=[gate[p]])
                    yield
                for j in range(nq):
                    for k in range(8):
                        kb.op("pe", lambda: nc.tensor.matmul(X2[:, :RETW], lhsT=hn[:, k, j * CH:(j + 1) * CH], rhs=w_in[:, k, C_RV:C_RV + RETW],
                                                             start=(k == 0), stop=(k == 7)), r=[w_in, hn], w=[X2])
                    kb.op("dve", lambda: nc.vector.tensor_copy(out=rvt[:, j, :], in_=X2[:, :RETW]), r=[X2], w=[rvt])
                    yield
                for p in range(3):
                    po = X2
                    for j in range(nq):
                        jc = slice(j * CH, (j + 1) * CH)
                        kb.op("pe", lambda: nc.tensor.matmul(X0[:, 0:2 * CH].rearrange("p (a i) -> p a i", i=CH), lhsT=kp[p][:, jc], rhs=qp[p][:, :, jc], start=True, stop=True),
                              r=[kp[p], qp[p]], w=[X0])
                        for hh in range(2):
                            kb.op("dve", lambda: nc.vector.tensor_tensor(out=scm[:, hh, :], in0=X0[:, hh * CH:(hh + 1) * CH], in1=c["mret"][:], op=ALU.mult),
                                  r=[X0, c["mret"]], w=[scm])
                        for hh in range(2):
                            h = 2 * p + hh
                            rs_ = slice(64 * hh, 64 * hh + 64)
                            kb.op("pe", lambda: nc.tensor.matmul(po[rs_, jc], lhsT=rvt[:, j, h * HD:(h + 1) * HD], rhs=scm[:, hh, :], start=True, stop=False),
                                  r=[rvt, scm], w=[po])
                            kb.op("pe", lambda: nc.tensor.matmul(po[rs_, jc], lhsT=stbf[p][:, :], rhs=qp[p][:, hh, jc], start=False, stop=True),
                                  r=[stbf[p], qp[p]], w=[po])
                        kb.op("pe", lambda: nc.tensor.matmul(X1[:, 0:CH], lhsT=kptok[p][:, j, :], rhs=rvt[:, j, p * CH:(p + 1) * CH], start=True, stop=True),
                              r=[kptok[p], rvt], w=[X1])
                        for hh in range(2):
                            rs_ = slice(64 * hh, 64 * hh + 64)
                            kb.op("dve", lambda: nc.vector.tensor_tensor(out=st32[p][rs_, :], in0=st32[p][rs_, :], in1=X1[rs_, hh * HD:(hh + 1) * HD], op=ALU.add),
                                  r=[st32[p], X1], w=[st32[p]])
                        kb.op("dve", lambda: nc.vector.tensor_scalar(out=st32[p][:], in0=st32[p][:], scalar1=c["cdec"][:, p:p + 1], scalar2=None, op0=ALU.mult),
                              r=[st32[p], c["cdec"]], w=[st32[p]])
                        kb.op("dve", lambda: nc.vector.tensor_copy(out=stbf[p][:], in_=st32[p][:]), r=[st32[p]], w=[stbf[p]])
                        yield
                    yield from self.headnorm_g(po, n, rog[:, p:p + 1], rog, tmp2[:, :n], tmp2, W)
                    mb = next_mb()
                    kb.op("dve", lambda: nc.vector.tensor_tensor(out=mb[:, :n], in0=tmp2[:, :n], in1=gate[p][:, :n], op=ALU.mult), r=[tmp2, gate[p]], w=[mb])
                    kb.dma("sp", self.mixT[3 + p, :, g0:g0 + n], mb[:, :n], r=[mb], w=[self.mix_t[s][bi]])
                    yield

            def stage_sb(s, bi):
                t0, n = blocks[bi]
                g0 = self.tok0(s, t0)
                nq = n // CH
                qc0 = t0 // CH
                QNc = QN2[bi % 2]
                kt_all = lambda p: [KT_t[p][i] for i in range(bi + 1)]
                vt_all = [Vt_t[i] for i in range(bi + 1)]
                kcs = list(range(qc0 + nq - 1, -1, -1))
                NST = len(kcs)
                ob = ps[4]

                def geo(kc):
                    col0 = max(0, kc - qc0) * CH
                    return col0, slice(col0, n), slice(kc * CH, (kc + 1) * CH), kc >= qc0

                for p in range(3):
                    def st_z(i):
                        col0, cols, kcols, diag = geo(kcs[i])
                        for hh in range(2):
                            rs_ = slice(64 * hh, 64 * hh + 64)
                            z = ps[hh]
                            kb.op("pe", lambda: nc.tensor.matmul(z[:, cols], lhsT=KT[p][rs_, kcols], rhs=QNc[p][rs_, cols], start=True, stop=True),
                                  r=kt_all(p) + [QNc[p]], w=[z])

                    def st_e(i):
                        kc = kcs[i]
                        col0, cols, kcols, diag = geo(kc)
                        dc = slice(col0, col0 + CH)
                        bias = c["padb"][:, 0:1] if kc == 0 else 0.0
                        for hh in range(2):
                            z = ps[hh]
                            eb = ebf[hh][i % 2]
                            kb.op("act", lambda: nc.scalar.activation(out=eb[:, cols], in_=z[:, cols], func=AF.Exp, bias=bias), r=[z, c["padb"]], w=[eb])
                        if diag:
                            for hh in range(2):
                                eb = ebf[hh][i % 2]
                                kb.op("dve", lambda: nc.vector.tensor_tensor(out=eb[:, dc], in0=eb[:, dc], in1=c["msb"][:], op=ALU.mult),
                                      r=[eb, c["msb"]], w=[eb])
                        for hh in range(2):
                            eb = ebf[hh][i % 2]
                            spc = sp[hh][i % 3]
                            kb.op("act", lambda: nc.scalar.activation(out=spc[:, cols], in_=eb[:, cols], func=AF.Ln, bias=1.0), r=[eb], w=[spc])

                    def st_acc(i):
                        kc = kcs[i]
                        col0, cols, kcols, diag = geo(kc)
                        for hh in range(2):
                            Bk = ps[2 + hh]
                            spc = sp[hh][i % 3]
                            if i > 0:
                                pcol0, pcols, pkcols, _ = geo(kcs[i - 1])
                                psp = sp[hh][(i - 1) % 3]
                                kb.op("pe", lambda: nc.tensor.matmul(Bk[:, pcols], lhsT=c["Ls"][:], rhs=psp[:, pcols], start=False, stop=False, skip_group_check=True),
                                      r=[c["Ls"], psp], w=[Bk])
                            kb.op("pe", lambda: nc.tensor.matmul(Bk[:, cols], lhsT=c["U"][:], rhs=spc[:, cols], start=(i == 0), stop=(kc == 0), skip_group_check=True),
                                  r=[c["U"], spc], w=[Bk])

                    def st_x(i):
                        col0, cols, kcols, diag = geo(kcs[i])
                        for hh in range(2):
                            Bk = ps[2 + hh]
                            x_ = xc[hh][i % 2]
                            kb.op("act", lambda: nc.scalar.activation(out=x_[:, cols], in_=Bk[:, cols], func=AF.Exp, scale=-1.0), r=[Bk], w=[x_])

                    def st_w(i):
                        col0, cols, kcols, diag = geo(kcs[i])
                        for hh in range(2):
                            wc, eb, x_ = wt[hh][i % 2], ebf[hh][i % 2], xc[hh][i % 2]
                            kb.op("dve", lambda: nc.vector.tensor_tensor(out=wc[:, cols], in0=eb[:, cols], in1=x_[:, cols], op=ALU.mult),
                                  r=[eb, x_], w=[wc])

                    def st_pv(i):
                        kc = kcs[i]
                        col0, cols, kcols, diag = geo(kc)
                        for hh in range(2):
                            h = 2 * p + hh
                            rs_ = slice(64 * hh, 64 * hh + 64)
                            wc = wt[hh][i % 2]
                            kb.op("pe", lambda: nc.tensor.matmul(ob[rs_, cols], lhsT=Vt[:, kc, h * HD:(h + 1) * HD], rhs=wc[:, cols],
                                                                 start=(i == 0), stop=(kc == 0), skip_group_check=True), r=vt_all + [wc], w=[ob])

                    st_z(0)
                    st_e(0)
                    if NST > 1:
                        st_z(1)
                    for tau in range(NST):
                        st_acc(tau)
                        if tau + 1 < NST:
                            st_e(tau + 1)
                        if tau + 2 < NST:
                            st_z(tau + 2)
                        st_x(tau)
                        st_w(tau)
                        if tau >= 1:
                            st_pv(tau - 1)
                        yield
                    st_pv(NST - 1)
                    mb = next_mb()
                    kb.op("dve", lambda: nc.vector.tensor_copy(out=mb[:, :n], in_=ob[:, :n]), r=[ob], w=[mb])
                    kb.dma("sp", self.mixT[p, :, g0:g0 + n], mb[:, :n], r=[mb], w=[self.mix_t[s][bi]])
                    yield

            def drain(g):
                for _ in g:
                    pass

            flat = [(s, bi) for s in range(cfg.NS) for bi in range(len(blocks))]

            def chain(*gens):
                for g in gens:
                    for _ in g:
                        yield

            NP_EST = 170 if cfg.do_ret else 50
            for idx, (s, bi) in enumerate(flat):
                if bi == 0:
                    drain(stage_p(s, bi, part=1))
                sbg = stage_sb(s, bi)
                gens = [stage_p(s, bi, part=2)]
                if bi + 1 < len(blocks):
                    gens.append(stage_p(s, bi + 1, part=1))
                pg = chain(*gens)
                if INTERLEAVE:
                    t0, n = blocks[bi]
                    n_sb = 3 * (t0 // CH + n // CH + 1)
                    per = max(1, -(-NP_EST // n_sb))
                    for _ in sbg:
                        for _k in range(per):
                            next(pg, None)
                    drain(pg)
                else:
                    drain(pg)
                    drain(sbg)
            kb.barrier()


def _col(v, nk):
    return np.ascontiguousarray(np.asarray(v, np.float32).reshape(nk, 128).T)


def make_in_map(cfg, cn, inp, x_shard):
    DEPTH = cfg.DEPTH
    m = {"x": np.ascontiguousarray(x_shard, dtype=np.float32), "meta": np.asarray(inp["meta_tokens"], np.float32)}
    m["n1g"] = np.stack([_col(inp["norm1_g"][l], 8) for l in range(DEPTH)])
    m["n2g"] = np.stack([_col(inp["norm2_g"][l], 8) for l in range(DEPTH)])
    hg = np.zeros((DEPTH, 128, 4), np.float32)
    for l in range(DEPTH):
        for j, nm in enumerate(["sb_q_g", "sb_k_g", "ret_q_g", "ret_k_g"]):
            hg[l, :, j] = np.tile(np.asarray(inp[nm][l], np.float32), 2)
    m["hg"] = hg
    m["rog"] = np.stack([_col(inp["ret_out_g"][l], 3) for l in range(DEPTH)])
    for nm in ["w_in", "w_out", "w_gate", "w_up", "w_down"]:
        m[nm] = np.ascontiguousarray(np.asarray(inp[nm], np.float32)[:DEPTH])
    G = 16
    def colq(a):
        return np.ascontiguousarray(np.asarray(a, np.float32).reshape(8, 2, 64).transpose(1, 2, 0).reshape(128, 8))
    lam_r, lam_i, ldt = [], [], []
    Bcr, Bci, Ccr, Cci, dcol = [], [], [], [], []
    for l in range(DEPTH):
        lam_r.append(colq(inp["s5_lam_re"][l]))
        lam_i.append(colq(inp["s5_lam_im"][l]))
        ldt.append(colq(np.repeat(np.asarray(inp["s5_log_dt"][l], np.float32)[:, None], 64, axis=1)))
        def padB(b):
            out = np.zeros((128, 8, 128), np.float32)
            for g in range(G):
                j, gl, gi = g // 2, g % 2, g % 8
                out[gl * 64:(gl + 1) * 64, j, gi * 16:(gi + 1) * 16] = b[g]
            return out
        def padC(cm):
            out = np.zeros((128, 8, 128), np.float32)
            for g in range(G):
                j, gl, gi = g // 2, g % 2, g % 8
                out[gl * 64:(gl + 1) * 64, j, gi * 16:(gi + 1) * 16] = cm[g].T
            return out
        Bcr.append(padB(np.asarray(inp["s5_b_re"][l], np.float32)))
        Bci.append(padB(np.asarray(inp["s5_b_im"][l], np.float32)))
        Ccr.append(padC(np.asarray(inp["s5_c_re"][l], np.float32)))
        Cci.append(padC(np.asarray(inp["s5_c_im"][l], np.float32)))
        dcol.append(_col(inp["s5_d"][l], 2))
    m["lam_r"], m["lam_i"], m["ldt"] = np.stack(lam_r), np.stack(lam_i), np.stack(ldt)
    m["Bcr"], m["Bci"], m["Ccr"], m["Cci"] = np.stack(Bcr), np.stack(Bci), np.stack(Ccr), np.stack(Cci)
    m["s5d"] = np.stack(dcol)
    m["w_glu"] = np.ascontiguousarray(np.asarray(inp["s5_w_glu"], np.float32)[:DEPTH])
    for k, v in cn.items():
        m["c_" + k] = v
    return m


TWO_PI_LO = 6.2831845
MAGIC = 12582912.0
GELU_C = math.sqrt(2.0 / math.pi)


def _phase_s5(self, l):
    cfg, nc, kb, d, c = self.cfg, self.nc, self.kb, self.d, self.c
    L = cfg.L
    NB = L // 8
    blocks = cfg.blocks()

    def tt(out, a, b, op, r, w, e="dve"):
        kb.op(e, lambda: self.eng_of(e).tensor_tensor(out=out, in0=a, in1=b, op=op), r=r, w=w)

    def ts(out, a, s1, op0, r, w, s2=None, op1=None):
        if op1 is None:
            kb.op("dve", lambda: nc.vector.tensor_scalar(out=out, in0=a, scalar1=s1, scalar2=None, op0=op0), r=r, w=w)
        else:
            kb.op("dve", lambda: nc.vector.tensor_scalar(out=out, in0=a, scalar1=s1, scalar2=s2, op0=op0, op1=op1), r=r, w=w)

    def act(out, a, func, r, w, **kw):
        kb.op("act", lambda: nc.scalar.activation(out=out, in_=a, func=func, **kw), r=r, w=w)

    with ExitStack() as es:
        BP = kb.sb(es, "BP", [128, 8, 2, 8, 128], BF16)
        CP = kb.sb(es, "CP", [128, 8, 2, 8, 128], BF16)
        Kt = kb.sb(es, "Ktap", [128, 8, 2, 128], BF16)
        cosn = kb.sb(es, "cosn", [128, 8, NB], F32)
        sinn = kb.sb(es, "sinn", [128, 8, NB], F32)
        m8 = kb.sb(es, "m8", [128, 8], F32)
        dcol = kb.sb(es, "s5dcol", [128, 2], F32)
        kb.dma("sp", dcol[:], d["s5d"][l], w=[dcol])
        wglu = kb.sb(es, "wglu", [128, 2, S5W], BF16)
        self.load_w_bf16(wglu, d["w_glu"][l], 2)
        ps = [kb.ps(es, "s5ps%d" % i, [128, 512], F32) for i in range(8)]
        with ExitStack() as pes:
            P8 = kb.sb(pes, "P8", [128, 24, 8], F32)
            pwr = kb.sb(pes, "pwr", [128, 9, 8], F32)
            pwi = kb.sb(pes, "pwi", [128, 9, 8], F32)
            iota = kb.sb(pes, "iota", [128, NB], F32)
            kb.dma("sp", iota[:], d["c_iota"], w=[iota])
            big = [kb.sb(pes, "s5big%d" % i, [128, 8, 128], F32) for i in range(11)]
            Bcr, Bci, Ccr, Cci, nCi, Bbr, Bbi, Er, Ei, T1, T2 = big
            for tl, nm in [(Bcr, "Bcr"), (Bci, "Bci"), (Ccr, "Ccr"), (Cci, "Cci")]:
                kb.dma("sp", tl[:], d[nm][l], w=[tl])
            nbt = [kb.sb(pes, "s5nb%d" % i, [128, NB], F32) for i in range(4)]
            V = lambda i: P8[:, i, :]
            LR, LI, LDT, DT, LRDT, MAG, R, SIN, COS, AR, AI, DEN, AM1, FR, FI, X1, X2, X3, F8 = range(19)
            kb.dma("sp", V(LR), d["lam_r"][l], w=[P8])
            kb.dma("sp", V(LI), d["lam_i"][l], w=[P8])
            kb.dma("sp", V(LDT), d["ldt"][l], w=[P8])
            p8 = [P8]
            act(V(DT), V(LDT), AF.Exp, p8, p8)
            tt(V(LRDT), V(LR), V(DT), ALU.mult, p8, p8)
            act(V(MAG), V(LRDT), AF.Exp, p8, p8)
            act(m8[:], V(LRDT), AF.Exp, p8, [m8], scale=8.0)
            tt(V(R), V(LI), V(DT), ALU.mult, p8, p8)
            ts(V(R), V(R), 1.0 / (2.0 * math.pi), ALU.mult, p8, p8)

            def red_sin(dst, dst_t, src, src_t, tmpa, tmpb, tmp_t, shift=0.0):
                if shift != 0.0:
                    ts(tmpb, src, shift, ALU.add, src_t + tmp_t, tmp_t)
                    src = tmpb
                ts(tmpa, src, MAGIC, ALU.add, src_t + tmp_t, tmp_t)
                ts(tmpa, tmpa, MAGIC, ALU.subtract, tmp_t, tmp_t)
                tt(tmpa, src, tmpa, ALU.subtract, src_t + tmp_t, tmp_t)
                act(dst, tmpa, AF.Sin, tmp_t, dst_t, scale=TWO_PI_LO)

            red_sin(V(SIN), p8, V(R), p8, V(X1), V(X2), p8)
            red_sin(V(COS), p8, V(R), p8, V(X1), V(X2), p8, shift=0.25)
            tt(V(AR), V(MAG), V(COS), ALU.mult, p8, p8)
            tt(V(AI), V(MAG), V(SIN), ALU.mult, p8, p8)
            tt(V(DEN), V(LR), V(LR), ALU.mult, p8, p8)
            tt(V(X1), V(LI), V(LI), ALU.mult, p8, p8)
            tt(V(DEN), V(DEN), V(X1), ALU.add, p8, p8)
            kb.op("dve", lambda: nc.vector.reciprocal(out=V(DEN), in_=V(DEN)), r=p8, w=p8)
            ts(V(AM1), V(AR), -1.0, ALU.add, p8, p8)
            tt(V(X1), V(AM1), V(LR), ALU.mult, p8, p8)
            tt(V(X2), V(AI), V(LI), ALU.mult, p8, p8)
            tt(V(X1), V(X1), V(X2), ALU.add, p8, p8)
            tt(V(FR), V(X1), V(DEN), ALU.mult, p8, p8)
            tt(V(X1), V(AI), V(LR), ALU.mult, p8, p8)
            tt(V(X2), V(AM1), V(LI), ALU.mult, p8, p8)
            tt(V(X1), V(X1), V(X2), ALU.subtract, p8, p8)
            tt(V(FI), V(X1), V(DEN), ALU.mult, p8, p8)
            pw = [pwr, pwi]
            kb.op("dve", lambda: nc.vector.memset(pwr[:, 0, :], 1.0), w=[pwr])
            kb.op("dve", lambda: nc.vector.memset(pwi[:, 0, :], 0.0), w=[pwi])
            kb.op("dve", lambda: nc.vector.tensor_copy(out=pwr[:, 1, :], in_=V(AR)), r=p8, w=[pwr])
            kb.op("dve", lambda: nc.vector.tensor_copy(out=pwi[:, 1, :], in_=V(AI)), r=p8, w=[pwi])
            for k in range(1, 8):
                tt(V(X1), pwr[:, k, :], V(AR), ALU.mult, p8 + pw, p8)
                tt(V(X2), pwi[:, k, :], V(AI), ALU.mult, p8 + pw, p8)
                tt(pwr[:, k + 1, :], V(X1), V(X2), ALU.subtract, p8, [pwr])
                tt(V(X1), pwr[:, k, :], V(AI), ALU.mult, p8 + pw, p8)
                tt(V(X2), pwi[:, k, :], V(AR), ALU.mult, p8 + pw, p8)
                tt(pwi[:, k + 1, :], V(X1), V(X2), ALU.add, p8, [pwi])
            ts(V(X3), V(R), 8.0, ALU.mult, p8, p8)
            ts(V(X1), V(X3), MAGIC, ALU.add, p8, p8)
            ts(V(X1), V(X1), MAGIC, ALU.subtract, p8, p8)
            tt(V(F8), V(X3), V(X1), ALU.subtract, p8, p8)
            for j in range(8):
                ts(nbt[0][:], iota[:], P8[:, F8, j:j + 1], ALU.mult, [iota, P8], [nbt[0]])
                red_sin(sinn[:, j, :], [sinn], nbt[0][:], [nbt[0]], nbt[1][:], nbt[2][:], [nbt[1], nbt[2]])
                red_sin(cosn[:, j, :], [cosn], nbt[0][:], [nbt[0]], nbt[1][:], nbt[2][:], [nbt[1], nbt[2]], shift=0.25)
            bc = lambda i: P8[:, i, :].unsqueeze(2).broadcast_to([128, 8, 128])
            pb = lambda t_, k: t_[:, k, :].unsqueeze(2).broadcast_to([128, 8, 128])
            ts(nCi[:], Cci[:], -1.0, ALU.mult, [Cci], [nCi])
            tt(T1[:], Bcr[:], bc(FR), ALU.mult, [Bcr, P8], [T1])
            tt(T2[:], Bci[:], bc(FI), ALU.mult, [Bci, P8], [T2])
            tt(Bbr[:], T1[:], T2[:], ALU.subtract, [T1, T2], [Bbr])
            tt(T1[:], Bci[:], bc(FR), ALU.mult, [Bci, P8], [T1])
            tt(T2[:], Bcr[:], bc(FI), ALU.mult, [Bcr, P8], [T2])
            tt(Bbi[:], T1[:], T2[:], ALU.add, [T1, T2], [Bbi])
            for tau in range(8):
                tt(T1[:], Bbr[:], pb(pwr, tau), ALU.mult, [Bbr, pwr], [T1])
                tt(T2[:], Bbi[:], pb(pwi, tau), ALU.mult, [Bbi, pwi], [T2])
                tt(Er[:], T1[:], T2[:], ALU.subtract, [T1, T2], [Er])
                tt(T1[:], Bbi[:], pb(pwr, tau), ALU.mult, [Bbi, pwr], [T1])
                tt(T2[:], Bbr[:], pb(pwi, tau), ALU.mult, [Bbr, pwi], [T2])
                tt(Ei[:], T1[:], T2[:], ALU.add, [T1, T2], [Ei])
                for h in range(2):
                    pk = ps[h]
                    for jj in range(4):
                        j = 4 * h + jj
                        kb.op("pe", lambda: nc.tensor.matmul(pk[:, 0:128], lhsT=Er[:, j, :], rhs=Ccr[:, j, :], start=(jj == 0), stop=False), r=[Er, Ccr], w=[pk])
                        kb.op("pe", lambda: nc.tensor.matmul(pk[:, 0:128], lhsT=Ei[:, j, :], rhs=nCi[:, j, :], start=False, stop=(jj == 3)), r=[Ei, nCi], w=[pk])
                    kb.op("act", lambda: nc.scalar.copy(out=Kt[:, tau, h, :], in_=pk[:, 0:128]), r=[pk], w=[Kt])
                sidx = 7 - tau
                for ri, E in enumerate([Er, Ei]):
                    for h in range(2):
                        pt = ps[2 + 2 * ri + h]
                        for jj in range(4):
                            j = 4 * h + jj
                            kb.op("pe", lambda: nc.tensor.transpose(out=pt[:, jj * 128:(jj + 1) * 128], in_=E[:, j, :], identity=c["ident_f"][:]),
                                  r=[E, c["ident_f"]], w=[pt])
                        kb.op("act", lambda: nc.scalar.copy(out=BP[:, sidx, ri, 4 * h:4 * h + 4, :].rearrange("p j f -> p (j f)"), in_=pt[:]), r=[pt], w=[BP])
            for t in range(8):
                tt(T1[:], Ccr[:], pb(pwr, t + 1), ALU.mult, [Ccr, pwr], [T1])
                tt(T2[:], Cci[:], pb(pwi, t + 1), ALU.mult, [Cci, pwi], [T2])
                tt(CP[:, t, 0, :, :], T1[:], T2[:], ALU.subtract, [T1, T2], [CP])
                tt(T1[:], nCi[:], pb(pwr, t + 1), ALU.mult, [nCi, pwr], [T1])
                tt(T2[:], Ccr[:], pb(pwi, t + 1), ALU.mult, [Ccr, pwi], [T2])
                tt(CP[:, t, 1, :, :], T1[:], T2[:], ALU.subtract, [T1, T2], [CP])
            kb.barrier()
        with ExitStack() as ses:
            u = kb.sb(ses, "s5u", [128, 2, L], BF16)
            W = kb.sb(ses, "s5W", [128, 2, 8, NB], F32)
            X0 = kb.sb(ses, "s5X0", [128, 2, 8, NB], BF16)
            tb = [kb.sb(ses, "s5t%d" % i, [128, max(NB, 512)], F32) for i in range(4)]
            yv = kb.sb(ses, "s5yv", [128, 2, 512], F32)
            gf = kb.sb(ses, "s5gf", [128, 2, 512], F32)
            gb = kb.sb(ses, "s5gb", [128, 2, 512], BF16)
            sob = [kb.sb(ses, "s5so%d" % i, [128, 512], BF16) for i in range(2)]
            nchk = (NB + 511) // 512
            csz = NB // nchk
            assert csz * nchk == NB
            for s in range(cfg.NS):
                kb.dma("sp", u[:], self.uT[:, :, s * L:(s + 1) * L].rearrange("k p t -> p k t"), r=[self.u_t[s]], w=[u])
                kb.op("dve", lambda: nc.vector.memset(X0[:], 0.0), w=[X0])
                for j in range(8):
                    h = j // 4
                    uv = u[:, h, :].rearrange("p (m s) -> p s m", s=8)
                    for ck in range(nchk):
                        cs = slice(ck * csz, (ck + 1) * csz)
                        for ri in range(2):
                            pS = ps[2 * ck + ri]
                            for sft in range(8):
                                kb.op("pe", lambda: nc.tensor.matmul(pS[:, :csz], lhsT=BP[:, sft, ri, j, :], rhs=uv[:, sft, cs], start=(sft == 0), stop=(sft == 7)),
                                      r=[BP, u], w=[pS])
                        pSr, pSi = ps[2 * ck], ps[2 * ck + 1]
                        tt(tb[0][:, :csz], pSr[:, :csz], cosn[:, j, cs], ALU.mult, [pSr, cosn], [tb[0]])
                        tt(tb[1][:, :csz], pSi[:, :csz], sinn[:, j, cs], ALU.mult, [pSi, sinn], [tb[1]])
                        tt(W[:, 0, j, cs], tb[0][:, :csz], tb[1][:, :csz], ALU.add, [tb[0], tb[1]], [W])
                        tt(tb[0][:, :csz], pSi[:, :csz], cosn[:, j, cs], ALU.mult, [pSi, cosn], [tb[0]])
                        tt(tb[1][:, :csz], pSr[:, :csz], sinn[:, j, cs], ALU.mult, [pSr, sinn], [tb[1]])
                        tt(W[:, 1, j, cs], tb[0][:, :csz], tb[1][:, :csz], ALU.subtract, [tb[0], tb[1]], [W])
                    for ri in range(2):
                        kb.op("dve", lambda: nc.vector.tensor_tensor_scan(out=tb[2 + ri][:, :NB], data0=m8[:, j:j + 1].broadcast_to([128, NB]), data1=W[:, ri, j, :],
                                                                          initial=0.0, op0=ALU.mult, op1=ALU.add), r=[m8, W], w=[tb[2 + ri]])
                    if NB > 1:
                        a_, b_ = slice(0, NB - 1), slice(1, NB)
                        tt(tb[0][:, a_], tb[2][:, a_], cosn[:, j, a_], ALU.mult, [tb[2], cosn], [tb[0]])
                        tt(tb[1][:, a_], tb[3][:, a_], sinn[:, j, a_], ALU.mult, [tb[3], sinn], [tb[1]])
                        tt(X0[:, 0, j, b_], tb[0][:, a_], tb[1][:, a_], ALU.subtract, [tb[0], tb[1]], [X0])
                        tt(tb[0][:, a_], tb[2][:, a_], sinn[:, j, a_], ALU.mult, [tb[2], sinn], [tb[0]])
                        tt(tb[1][:, a_], tb[3][:, a_], cosn[:, j, a_], ALU.mult, [tb[3], cosn], [tb[1]])
                        tt(X0[:, 1, j, b_], tb[0][:, a_], tb[1][:, a_], ALU.add, [tb[0], tb[1]], [X0])
                for bi, (t0, n) in enumerate(blocks):
                    g0 = self.tok0(s, t0)
                    nb, n0 = n // 8, t0 // 8
                    for h in range(2):
                        pY = ps[4 + h]
                        uv = u[:, h, t0:t0 + n].rearrange("p (m s) -> p s m", s=8)
                        for t in range(8):
                            oap = pY[:, :n].rearrange("p (m s) -> p s m", s=8)[:, t, :]
                            mm = []
                            for sft in range(t + 1):
                                mm.append((Kt[:, t - sft, h, :], uv[:, sft, :], [Kt, u]))
                            for jj in range(4):
                                j = 4 * h + jj
                                mm.append((CP[:, t, 0, j, :], X0[:, 0, j, n0:n0 + nb], [CP, X0]))
                                mm.append((CP[:, t, 1, j, :], X0[:, 1, j, n0:n0 + nb], [CP, X0]))
                            for i, (lh, rh, rr) in enumerate(mm):
                                kb.op("pe", lambda: nc.tensor.matmul(oap, lhsT=lh, rhs=rh, start=(i == 0), stop=(i == len(mm) - 1), skip_group_check=True), r=rr, w=[pY])
                        kb.op("dve", lambda: nc.vector.scalar_tensor_tensor(out=yv[:, h, :n], in0=u[:, h, t0:t0 + n], scalar=dcol[:, h:h + 1], in1=pY[:, :n],
                                                                            op0=ALU.mult, op1=ALU.add), r=[u, dcol, pY], w=[yv])
                        tt(tb[0][:, :n], yv[:, h, :n], yv[:, h, :n], ALU.mult, [yv], [tb[0]])
                        ts(tb[0][:, :n], tb[0][:, :n], 2.0 * GELU_C * 0.044715, ALU.mult, [tb[0]], [tb[0]], s2=2.0 * GELU_C, op1=ALU.add)
                        tt(tb[0][:, :n], tb[0][:, :n], yv[:, h, :n], ALU.mult, [tb[0], yv], [tb[0]])
                        act(tb[1][:, :n], tb[0][:, :n], AF.Exp, [tb[0]], [tb[1]], scale=-1.0)
                        ts(tb[1][:, :n], tb[1][:, :n], 1.0, ALU.add, [tb[1]], [tb[1]])
                        kb.op("dve", lambda: nc.vector.reciprocal(out=tb[1][:, :n], in_=tb[1][:, :n]), r=[tb[1]], w=[tb[1]])
                        tt(gf[:, h, :n], yv[:, h, :n], tb[1][:, :n], ALU.mult, [yv, tb[1]], [gf])
                        kb.op("dve", lambda: nc.vector.tensor_copy(out=gb[:, h, :n], in_=gf[:, h, :n]), r=[gf], w=[gb])
                    for ho in range(2):
                        pV = ps[6 + ho]
                        for hi in range(2):
                            kb.op("pe", lambda: nc.tensor.matmul(pV[:, :n], lhsT=wglu[:, hi, ho * 128:(ho + 1) * 128], rhs=gb[:, hi, :n], start=(hi == 0), stop=(hi == 1)),
                                  r=[wglu, gb], w=[pV])
                        act(tb[2][:, :n], pV[:, :n], AF.Exp, [pV], [tb[2]], scale=-1.0)
                        ts(tb[2][:, :n], tb[2][:, :n], 1.0, ALU.add, [tb[2]], [tb[2]])
                        kb.op("dve", lambda: nc.vector.reciprocal(out=tb[2][:, :n], in_=tb[2][:, :n]), r=[tb[2]], w=[tb[2]])
                        tt(sob[ho][:, :n], gf[:, ho, :n], tb[2][:, :n], ALU.mult, [gf, tb[2]], [sob[ho]])
                        kb.dma("sp", self.mixT[6 + ho, :, g0:g0 + n], sob[ho][:, :n], r=[sob[ho]], w=[self.mix_t[s][bi]])
            kb.barrier()


def _eng_of(self, e):
    return self.kb.eng[e]


Prog.phase_s5 = _phase_s5
Prog.eng_of = _eng_of


def _phase_m2(self, l):
    cfg, nc, kb, d, c = self.cfg, self.nc, self.kb, self.d, self.c
    blocks1 = cfg.blocks()
    TB = 256
    with ExitStack() as es:
        w_out = kb.sb(es, "w_out", [128, 8, D], BF16)
        w_gate = kb.sb(es, "w_gate", [128, 8, D_FF], BF16)
        w_up = kb.sb(es, "w_up", [128, 8, D_FF], BF16)
        w_down = kb.sb(es, "w_down", [128, NFF, D], BF16)
        self.load_w_bf16(w_out, d["w_out"][l], 8)
        self.load_w_bf16(w_gate, d["w_gate"][l], 8)
        self.load_w_bf16(w_up, d["w_up"][l], 8)
        self.load_w_bf16(w_down, d["w_down"][l], NFF)
        n2g = kb.sb(es, "n2g", [128, 8], F32)
        kb.dma("sp", n2g[:], d["n2g"][l], w=[n2g])
        hbs = [kb.sb(es, "m2_h%d" % i, [128, 8, TB], F32) for i in range(2)]
        mixbs = [kb.sb(es, "m2_mix%d" % i, [128, 8, TB], BF16) for i in range(2)]
        hns = [kb.sb(es, "m2_hn%d" % i, [128, 8, TB], BF16) for i in range(2)]
        hsq = kb.sb(es, "m2_hsq", [128, 8, TB], BF16)
        actT = kb.sb(es, "m2_act", [128, NFF, TB], BF16)
        lnv = kb.sb(es, "m2_lnv", [128, TB], F32)
        rstd = kb.sb(es, "m2_rstd", [128, TB], F32)
        ev = [kb.sb(es, "m2_e%d" % i, [128, TB], F32) for i in range(2)]
        ps = [kb.ps(es, "m2ps%d" % i, [128, 512], F32) for i in range(8)]
        last_layer = (l == cfg.DEPTH - 1) and FUSE_OUT
        otile = kb.sb(es, "m2_ot", [128, 512], F32) if last_layer else None
        work = []
        for s in range(cfg.NS):
            for bi, (tb0, nb_) in enumerate(blocks1):
                for t0 in range(tb0, tb0 + nb_, TB):
                    work.append((s, bi, t0, min(TB, tb0 + nb_ - t0)))

        def stage_a(i):
            s, bi, t0, n = work[i]
            hb, mixb, hn = hbs[i % 2], mixbs[i % 2], hns[i % 2]
            g0 = self.tok0(s, t0)
            hview = self.hT[:, :, g0:g0 + n].rearrange("k p t -> p k t")
            kb.dma("sp", mixb[:, :, :n], self.mixT[:, :, g0:g0 + n].rearrange("k p t -> p k t"), r=[self.mix_t[s][bi]], w=[mixb])
            kb.dma("sp", hb[:, :, :n], hview, r=[self.hT_t[s][bi]], w=[hb])
            yield
            for m in range(8):
                p = ps[m % 2]
                for k in range(8):
                    kb.op("pe", lambda: nc.tensor.matmul(p[:, :n], lhsT=w_out[:, k, m * 128:(m + 1) * 128], rhs=mixb[:, k, :n], start=(k == 0), stop=(k == 7)),
                          r=[w_out, mixb], w=[p])
                kb.op("dve", lambda: nc.vector.tensor_tensor(out=hb[:, m, :n], in0=p[:, :n], in1=hb[:, m, :n], op=ALU.add), r=[p, hb], w=[hb])
                yield
            kb.op("act", lambda: nc.scalar.activation(out=hsq[:, :, :n], in_=hb[:, :, :n], func=AF.Square), r=[hb], w=[hsq])
            for k in range(8):
                kb.op("pe", lambda: nc.tensor.matmul(ps[0][:, :n], lhsT=c["ones"][:], rhs=hsq[:, k, :n], start=(k == 0), stop=(k == 7)),
                      r=[hsq, c["ones"]], w=[ps[0]])
            yield
            kb.op("act", lambda: nc.scalar.activation(out=lnv[:, :n], in_=ps[0][:, :n], func=AF.Ln, scale=1.0 / D, bias=c["eps"][:, 0:1]),
                  r=[ps[0], c["eps"]], w=[lnv])
            kb.op("act", lambda: nc.scalar.activation(out=rstd[:, :n], in_=lnv[:, :n], func=AF.Exp, scale=-0.5), r=[lnv], w=[rstd])
            yield
            for k in range(8):
                kb.op("dve", lambda: nc.vector.scalar_tensor_tensor(out=hn[:, k, :n], in0=hb[:, k, :n], scalar=n2g[:, k:k + 1], in1=rstd[:, :n],
                                                                    op0=ALU.mult, op1=ALU.mult), r=[hb, n2g, rstd], w=[hn])
                if k % 2 == 1:
                    yield

        def stage_b(i):
            s, bi, t0, n = work[i]
            hb, hn = hbs[i % 2], hns[i % 2]
            g0 = self.tok0(s, t0)
            hview = self.hT[:, :, g0:g0 + n].rearrange("k p t -> p k t")
            for f in range(NFF):
                pg, pu, e = ps[2 + f % 2], ps[4 + f % 2], ev[f % 2]
                fs = slice(f * 128, (f + 1) * 128)
                for k in range(8):
                    kb.op("pe", lambda: nc.tensor.matmul(pg[:, :n], lhsT=w_gate[:, k, fs], rhs=hn[:, k, :n], start=(k == 0), stop=(k == 7)), r=[w_gate, hn], w=[pg])
                for k in range(8):
                    kb.op("pe", lambda: nc.tensor.matmul(pu[:, :n], lhsT=w_up[:, k, fs], rhs=hn[:, k, :n], start=(k == 0), stop=(k == 7)), r=[w_up, hn], w=[pu])
                kb.op("act", lambda: nc.scalar.activation(out=e[:, :n], in_=pg[:, :n], func=AF.Silu), r=[pg], w=[e])
                kb.op("dve", lambda: nc.vector.tensor_tensor(out=actT[:, f, :n], in0=pu[:, :n], in1=e[:, :n], op=ALU.mult), r=[pu, e], w=[actT])
                yield
            for m in range(8):
                p = ps[6 + m % 2]
                for f in range(NFF):
                    kb.op("pe", lambda: nc.tensor.matmul(p[:, :n], lhsT=w_down[:, f, m * 128:(m + 1) * 128], rhs=actT[:, f, :n], start=(f == 0), stop=(f == NFF - 1)),
                          r=[w_down, actT], w=[p])
                kb.op("dve", lambda: nc.vector.tensor_tensor(out=hb[:, m, :n], in0=p[:, :n], in1=hb[:, m, :n], op=ALU.add), r=[p, hb], w=[hb])
                yield
            if not last_layer:
                kb.dma("sp", hview, hb[:, :, :n], r=[hb], w=[self.hT_t[s][bi]])
                return
            for cc in range(n // CH):
                t = t0 + cc * CH
                if t < CH:
                    continue
                for half in range(2):
                    pT = ps[6 + half]
                    for q in range(4):
                        kb.op("pe", lambda: nc.tensor.transpose(out=pT[:, q * 128:(q + 1) * 128], in_=hb[:, 4 * half + q, cc * CH:(cc + 1) * CH], identity=c["ident_f"][:]),
                              r=[hb, c["ident_f"]], w=[pT])
                    kb.op("act", lambda: nc.scalar.copy(out=otile[:], in_=pT[:]), r=[pT], w=[otile])
                    kb.dma("sp", self.out[s, t - CH:t, half * 512:(half + 1) * 512], otile[:], r=[otile], w=[self.out_t])
                yield

        def drain(g):
            for _ in g:
                pass

        drain(stage_a(0))
        for i in range(len(work)):
            bg = stage_b(i)
            if i + 1 < len(work):
                ag = stage_a(i + 1)
                cnt = 0
                for _ in bg:
                    cnt += 1
                    if cnt % 2 == 0:
                        next(ag, None)
                drain(ag)
            else:
                drain(bg)
        kb.barrier()


def _phase_f(self):
    cfg, nc, kb, d, c = self.cfg, self.nc, self.kb, self.d, self.c
    blocks = cfg.blocks()
    with ExitStack() as es:
        hb = [kb.sb(es, "f_h%d" % i, [128, 8, 128], F32) for i in range(2)]
        ot = [kb.sb(es, "f_o%d" % i, [128, D], F32) for i in range(2)]
        ps = [kb.ps(es, "f_ps%d" % i, [128, 1024], F32) for i in range(2)]
        it = 0
        for s in range(cfg.NS):
            for ch in range(1, cfg.NCH):
                a, b, p = hb[it % 2], ot[it % 2], ps[it % 2]
                t = ch * CH
                bi = [i for i, (t0, n) in enumerate(blocks) if t0 <= t < t0 + n][0]
                g0 = self.tok0(s, t)
                kb.dma("sp", a[:], self.hT[:, :, g0:g0 + CH].rearrange("k p t -> p k t"), r=[self.hT_t[s][bi]], w=[a])
                for k in range(8):
                    kb.op("pe", lambda: nc.tensor.transpose(out=p[:, k * 128:(k + 1) * 128], in_=a[:, k, :], identity=c["ident_f"][:]), r=[a, c["ident_f"]], w=[p])
                kb.op("act", lambda: nc.scalar.copy(out=b[:], in_=p[:]), r=[p], w=[b])
                kb.dma("sp", self.out[s, (ch - 1) * CH:ch * CH, :], b[:], r=[b], w=[self.out_t])
                it += 1
        kb.barrier()


Prog.phase_m2 = _phase_m2
Prog.phase_f = _phase_f


_CACHE = {}


def kernel(**inputs):
    n_cores = 8
    x = np.asarray(inputs["x"], np.float32)
    B, S, _ = x.shape
    NS = B // n_cores
    NCH = S // CH + 1
    DEPTH = int(np.asarray(inputs["w_in"]).shape[0])
    key = (NS, NCH, DEPTH)
    if key not in _CACHE:
        cfg = Cfg(NS=NS, NCH=NCH, DEPTH=DEPTH)
        P = Prog(cfg)
        P.build()
        _CACHE[key] = (cfg, P)
    cfg, P = _CACHE[key]
    in_maps = [make_in_map(cfg, P.cn, inputs, x[NS * i:NS * (i + 1)]) for i in range(n_cores)]
    res = run_bass_kernel_spmd(P.nc, in_maps, core_ids=list(range(n_cores)))
    out = np.concatenate([np.asarray(res.results[i]["out"]).reshape(NS, S, D) for i in range(n_cores)], axis=0)
    return out.astype(np.float32)
```

```python
import math
import os
FUSE_IN = int(os.environ.get('FUSE_IN', '1'))
FUSE_OUT = int(os.environ.get('FUSE_OUT', '1'))
INTERLEAVE = int(os.environ.get('INTERLEAVE', '1'))
GN = int(os.environ.get('GN', '3'))
SKIPDVE = int(os.environ.get('SKIPDVE', '0'))
SKIPMM = int(os.environ.get('SKIPMM', '0'))
SCB = int(os.environ.get('SCB', '3'))
RET_STOP = int(os.environ.get('RET_STOP', '99'))
from contextlib import ExitStack

import numpy as np
import ml_dtypes

import concourse.bass as bass
import concourse.mybir as mybir
from concourse.bass_utils import run_bass_kernel_spmd

F32 = mybir.dt.float32
BF16 = mybir.dt.bfloat16
AF = mybir.ActivationFunctionType
ALU = mybir.AluOpType

D = 1024
HD = 64
SBW = 384
RETW = 384
S5W = 256
D_IN = 2944
D_FF = 2816
NFF = D_FF // 128
CH = 128
N_META = 16
PAD_FRONT = 112
EPS = 1e-6
C_SQ, C_SK, C_SV, C_RQ, C_RK, C_RV, C_RG, C_U = 0, 384, 768, 1152, 1536, 1920, 2304, 2688


class T:
    def __init__(self, h):
        self.h = h
        self.w = None
        self.r = {}

    def __getitem__(self, idx):
        return self.h[idx]


class KB:
    def __init__(self, nc, es):
        self.nc = nc
        self.es = es
        self.eng = {"pe": nc.tensor, "act": nc.scalar, "dve": nc.vector, "pool": nc.gpsimd, "sp": nc.sync}
        self.semh = {}
        self.cnt = {}
        for e in self.eng:
            self.semh[e] = es.enter_context(nc.semaphore("s_" + e))
            self.cnt[e] = 0
        self.waited = {}
        self.ndma = 24
        self.dma_tot = [0] * self.ndma
        for i in range(self.ndma):
            self.semh["d%d" % i] = es.enter_context(nc.semaphore("s_d%d" % i))
        self.dma_next = 0
        self.n_ins = 0
        self.sw_sems = []

    def _nm(self, name):
        self.n_names = getattr(self, "n_names", 0) + 1
        return "t%d_%s" % (self.n_names, name)

    def sb(self, es, name, shape, dt):
        return T(es.enter_context(self.nc.sbuf_tensor(self._nm(name), list(shape), dt)))

    def ps(self, es, name, shape, dt=F32):
        return T(es.enter_context(self.nc.psum_tensor(self._nm(name), list(shape), dt)))

    def _wait(self, e, k, v):
        if self.waited.get((e, k), 0) >= v:
            return
        self.eng[e].wait_ge(self.semh[k], v)
        self.waited[(e, k)] = v

    def _deps(self, e, r, w):
        deps = {}

        def add(kv):
            k, v = kv
            if deps.get(k, 0) < v:
                deps[k] = v

        for t in r:
            if t.w:
                add(t.w)
        for t in w:
            if t.w:
                add(t.w)
            for kv in t.r.items():
                add(kv)
        for k, v in deps.items():
            if k == e and e == "pe":
                continue
            self._wait(e, k, v)

    def op(self, e, fn, r=(), w=()):
        self._deps(e, r, w)
        ins = fn()
        self.cnt[e] += 1
        ins.then_inc(self.semh[e], 1)
        seq = self.cnt[e]
        for t in r:
            t.r[e] = max(t.r.get(e, 0), seq)
        for t in w:
            t.w = (e, seq)
            t.r = {}
        self.n_ins += 1
        return ins

    def dma(self, q, out, in_, r=(), w=(), **kw):
        if q == "pool":
            k = "sw%d" % len(self.sw_sems)
            self.semh[k] = self.es.enter_context(self.nc.semaphore("s_" + k))
            self.sw_sems.append(k)
            self._deps(q, r, w)
            ins = self.eng[q].dma_start(out=out, in_=in_, **kw)
            ins.then_inc(self.semh[k], 16)
            for t in r:
                t.r[k] = 16
            for t in w:
                t.w = (k, 16)
                t.r = {}
            self.n_ins += 1
            return ins
        i = self.dma_next
        self.dma_next = (i + 1) % self.ndma
        k = "d%d" % i
        self._wait(q, k, self.dma_tot[i])
        self._deps(q, r, w)
        ins = self.eng[q].dma_start(out=out, in_=in_, **kw)
        self.dma_tot[i] += 16
        ins.then_inc(self.semh[k], 16)
        v = self.dma_tot[i]
        for t in r:
            t.r[k] = max(t.r.get(k, 0), v)
        for t in w:
            t.w = (k, v)
            t.r = {}
        self.n_ins += 1
        return ins

    def barrier(self):
        for e in self.eng:
            for k in list(self.eng.keys()):
                if self.cnt[k] > 0:
                    self._wait(e, k, self.cnt[k])
            for i in range(self.ndma):
                if self.dma_tot[i] > 0:
                    self._wait(e, "d%d" % i, self.dma_tot[i])
            for k in self.sw_sems:
                self._wait(e, k, 16)


def _consts(L):
    c = {}
    idx = np.arange(128)
    c["ident"] = np.eye(128, dtype=np.float32)
    c["ones"] = np.ones((128, 128), np.float32)
    bo = np.zeros((128, 128), np.float32)
    bo[:64, :64] = 1.0
    bo[64:, 64:] = 1.0
    c["bones"] = bo
    c["U"] = (idx[:, None] >= idx[None, :]).astype(np.float32)
    c["Ls"] = (idx[:, None] < idx[None, :]).astype(np.float32)
    c["msb"] = (idx[:, None] < idx[None, :]).astype(np.float32)
    c["mret"] = (idx[:, None] <= idx[None, :]).astype(np.float32)
    pb = np.zeros((128, 1), np.float32)
    pb[:PAD_FRONT] = -100.0
    c["padb"] = pb
    PT = np.zeros((128, 128), np.float32)
    for fp in range(128):
        if (fp % 64) < 32:
            PT[fp + 32, fp] = -1.0
        else:
            PT[fp - 32, fp] = 1.0
    c["PT"] = PT
    half = 32
    inv = (10000.0 ** (-np.arange(half, dtype=np.float32) / half)).astype(np.float32)
    pos = (np.arange(L) - PAD_FRONT).astype(np.float32)
    ang = (pos[None, :] * inv[:, None]).astype(np.float32)
    cosv = np.cos(ang).astype(np.float32)
    sinv = np.sin(ang).astype(np.float32)
    c["cosT"] = np.tile(cosv, (4, 1))
    c["sinT"] = np.tile(sinv, (4, 1))
    lg = np.log1p(-np.exp2(-5.0 - np.arange(6, dtype=np.float32))).astype(np.float32)
    i = np.arange(128, dtype=np.float32)
    qd = np.zeros((128, 3, 128), np.float32)
    kd = np.zeros((128, 3, 128), np.float32)
    cd = np.zeros((128, 3), np.float32)
    for h in range(6):
        p, hh = h // 2, h % 2
        qd[64 * hh:64 * hh + 64, p, :] = np.exp(lg[h] * (i + 1.0))[None, :]
        kd[64 * hh:64 * hh + 64, p, :] = (np.exp(-lg[h] * (i + 1.0)) * (HD ** -0.5))[None, :]
        cd[64 * hh:64 * hh + 64, p] = np.exp(lg[h] * 128.0)
    c["iota"] = np.tile(np.arange(L // 8, dtype=np.float32)[None, :], (128, 1))
    c["qdec"] = qd
    c["kdec"] = kd
    c["cdec"] = cd
    return c


class Cfg:
    def __init__(self, NS=2, NCH=33, DEPTH=2, do_ret=True, do_s5=True, do_m2=True, dbg=False):
        self.NS, self.NCH, self.DEPTH = NS, NCH, DEPTH
        self.L = NCH * CH
        self.NT = NS * self.L
        self.do_ret, self.do_s5, self.do_m2, self.dbg = do_ret, do_s5, do_m2, dbg

    def blocks(self, bs=512):
        out = []
        t = 0
        rem = self.L % bs
        if rem:
            out.append((0, rem))
            t = rem
        while t < self.L:
            out.append((t, bs))
            t += bs
        return out


class Prog:
    def __init__(self, cfg):
        self.cfg = cfg
        self.nc = bass.Bass("TRN2", target_bir_lowering=False)
        self.cn = _consts(cfg.L)

    def dram_in(self, name, shape, dt=F32):
        return self.nc.dram_tensor(name, list(shape), dt, kind="ExternalInput").ap()

    def build(self):
        cfg, nc = self.cfg, self.nc
        NS, NCH, L, NT, DEPTH = cfg.NS, cfg.NCH, cfg.L, cfg.NT, cfg.DEPTH
        d = {}
        d["x"] = self.dram_in("x", [NS, (NCH - 1) * CH, D])
        d["meta"] = self.dram_in("meta", [N_META, D])
        d["n1g"] = self.dram_in("n1g", [DEPTH, 128, 8])
        d["n2g"] = self.dram_in("n2g", [DEPTH, 128, 8])
        d["hg"] = self.dram_in("hg", [DEPTH, 128, 4])
        d["rog"] = self.dram_in("rog", [DEPTH, 128, 3])
        d["w_in"] = self.dram_in("w_in", [DEPTH, D, D_IN])
        d["w_out"] = self.dram_in("w_out", [DEPTH, D, D])
        d["w_gate"] = self.dram_in("w_gate", [DEPTH, D, D_FF])
        d["w_up"] = self.dram_in("w_up", [DEPTH, D, D_FF])
        d["w_down"] = self.dram_in("w_down", [DEPTH, D_FF, D])
        for nm in ["lam_r", "lam_i", "ldt"]:
            d[nm] = self.dram_in(nm, [DEPTH, 128, 8])
        for nm in ["Bcr", "Bci", "Ccr", "Cci"]:
            d[nm] = self.dram_in(nm, [DEPTH, 128, 8, 128])
        d["s5d"] = self.dram_in("s5d", [DEPTH, 128, 2])
        d["w_glu"] = self.dram_in("w_glu", [DEPTH, S5W, S5W])
        for k, v in self.cn.items():
            d["c_" + k] = self.dram_in("c_" + k, v.shape)
        self.d = d
        self.out = nc.dram_tensor("out", [NS, (NCH - 1) * CH, D], F32, kind="ExternalOutput").ap()
        kind_dbg = "ExternalOutput" if cfg.dbg else "Internal"
        self.hT = nc.dram_tensor("hT", [8, 128, NT], F32, kind=kind_dbg).ap()
        self.mixT = nc.dram_tensor("mixT", [8, 128, NT], BF16, kind=kind_dbg).ap()
        self.uT = nc.dram_tensor("uT", [2, 128, NT], BF16, kind=kind_dbg).ap()
        self.hT_t = [[T(None) for _ in cfg.blocks()] for _ in range(NS)]
        self.mix_t = [[T(None) for _ in cfg.blocks()] for _ in range(NS)]
        self.u_t = [T(None) for _ in range(NS)]
        self.out_t = T(None)

        with ExitStack() as es:
            self.kb = kb = KB(nc, es)
            self.load_consts(es)
            if not FUSE_IN:
                self.phase0()
            for l in range(DEPTH):
                self.phase_m1(l)
                if cfg.do_s5:
                    self.phase_s5(l)
                if cfg.do_m2:
                    self.phase_m2(l)
            if not (FUSE_OUT and cfg.do_m2):
                self.phase_f()
            kb.barrier()
        return nc

    def load_consts(self, es):
        kb, nc, d = self.kb, self.nc, self.d
        c = {}
        names = ["ident", "ones", "bones", "U", "Ls", "msb", "mret", "PT"]
        c["ident_f"] = kb.sb(es, "k_ident_f", [128, 128], F32)
        for name in names:
            c[name] = kb.sb(es, "k_" + name, [128, 128], BF16)
        c["padb"] = kb.sb(es, "k_padb", [128, 1], F32)
        c["eps"] = kb.sb(es, "k_eps", [128, 1], F32)
        c["qdec"] = kb.sb(es, "k_qdec", [128, 3, 128], F32)
        c["kdec"] = kb.sb(es, "k_kdec", [128, 3, 128], F32)
        c["cdec"] = kb.sb(es, "k_cdec", [128, 3], F32)
        with ExitStack() as tmp:
            for name in names:
                st = kb.sb(tmp, "st_" + name, [128, 128], F32)
                kb.dma("sp", st[:], d["c_" + name], w=[st])
                if name == "ident":
                    kb.op("dve", lambda: nc.vector.tensor_copy(out=c["ident_f"][:], in_=st[:]), r=[st], w=[c["ident_f"]])
                kb.op("dve", lambda: nc.vector.tensor_copy(out=c[name][:], in_=st[:]), r=[st], w=[c[name]])
            kb.barrier()
        kb.dma("sp", c["padb"][:], d["c_padb"], w=[c["padb"]])
        kb.op("dve", lambda: nc.vector.memset(c["eps"][:], EPS), w=[c["eps"]])
        kb.dma("sp", c["qdec"][:], d["c_qdec"], w=[c["qdec"]])
        kb.dma("sp", c["kdec"][:], d["c_kdec"], w=[c["kdec"]])
        kb.dma("sp", c["cdec"][:], d["c_cdec"], w=[c["cdec"]])
        self.c = c

    def tok0(self, s, t):
        return s * self.cfg.L + t

    def phase0(self):
        cfg, nc, kb, d, c = self.cfg, self.nc, self.kb, self.d, self.c
        blocks = cfg.blocks()
        with ExitStack() as es:
            xt = [kb.sb(es, "p0_xt%d" % i, [128, D], F32) for i in range(2)]
            xT = [kb.sb(es, "p0_xT%d" % i, [128, 8, 128], F32) for i in range(2)]
            ps = [kb.ps(es, "p0_ps%d" % i, [128, 1024], F32) for i in range(2)]
            it = 0
            for s in range(cfg.NS):
                for ch in range(cfg.NCH):
                    a, b, p = xt[it % 2], xT[it % 2], ps[it % 2]
                    if ch == 0:
                        kb.op("dve", lambda: nc.vector.memset(a[:], 0.0), w=[a])
                        kb.dma("sp", a[PAD_FRONT:128, :], d["meta"], w=[a])
                    else:
                        kb.dma("sp", a[:], d["x"][s, (ch - 1) * CH:ch * CH, :], w=[a])
                    for k in range(8):
                        kb.op("pe", lambda: nc.tensor.transpose(out=p[:, k * 128:(k + 1) * 128], in_=a[:, k * 128:(k + 1) * 128],
                                                               identity=c["ident_f"][:]), r=[a, c["ident_f"]], w=[p])
                    kb.op("act", lambda: nc.scalar.copy(out=b[:].rearrange("p k t -> p (k t)"), in_=p[:]), r=[p], w=[b])
                    t = ch * CH
                    bi = [i for i, (t0, n) in enumerate(blocks) if t0 <= t < t0 + n][0]
                    g0 = self.tok0(s, t)
                    kb.dma("sp", self.hT[:, :, g0:g0 + CH].rearrange("k p t -> p k t"), b[:], r=[b], w=[self.hT_t[s][bi]])
                    it += 1
            kb.barrier()

    def load_w_bf16(self, dst, src2d, nk):
        kb = self.kb
        ncols = src2d.shape[1]
        src = src2d.rearrange("(k p) c -> p k c", p=128)
        step = 1024
        for c0 in range(0, ncols, step):
            c1 = min(ncols, c0 + step)
            kb.dma("pool", dst[:, :, c0:c1], src[:, :, c0:c1], w=[dst])

    def rmsnorm_fm(self, n, src, src_t, gcol, hsq, ps_ss, lnv, rstd, hn):
        nc, kb, c = self.nc, self.kb, self.c
        kb.op("act", lambda: nc.scalar.activation(out=hsq[:, :, :n], in_=src[:, :, :n], func=AF.Square), r=[src_t], w=[hsq])
        for k in range(8):
            kb.op("pe", lambda: nc.tensor.matmul(ps_ss[:, :n], lhsT=c["ones"][:], rhs=hsq[:, k, :n], start=(k == 0), stop=(k == 7)),
                  r=[hsq, c["ones"]], w=[ps_ss])
        kb.op("act", lambda: nc.scalar.activation(out=lnv[:, :n], in_=ps_ss[:, :n], func=AF.Ln, scale=1.0 / D, bias=c["eps"][:, 0:1]),
              r=[ps_ss, c["eps"]], w=[lnv])
        kb.op("act", lambda: nc.scalar.activation(out=rstd[:, :n], in_=lnv[:, :n], func=AF.Exp, scale=-0.5), r=[lnv], w=[rstd])
        for k in range(8):
            kb.op("dve", lambda: nc.vector.scalar_tensor_tensor(out=hn[:, k, :n], in0=src[:, k, :n], scalar=gcol[:, k:k + 1], in1=rstd[:, :n],
                                                                op0=ALU.mult, op1=ALU.mult), r=[src_t, gcol, rstd], w=[hn])

    def headnorm(self, P, n, gcol_ap, gt, dst_ap, dst_t, W):
        nc, kb, c = self.nc, self.kb, self.c
        sq, ps_ss, lnv, rs = W["sq"], W["ps_ss"], W["lnv"], W["rs"]
        kb.op("act", lambda: nc.scalar.activation(out=sq[:, :n], in_=P[:, :n], func=AF.Square), r=[P], w=[sq])
        kb.op("pe", lambda: nc.tensor.matmul(ps_ss[:, :n], lhsT=c["bones"][:], rhs=sq[:, :n], start=True, stop=True), r=[sq, c["bones"]], w=[ps_ss])
        kb.op("act", lambda: nc.scalar.activation(out=lnv[:, :n], in_=ps_ss[:, :n], func=AF.Ln, scale=1.0 / HD, bias=c["eps"][:, 0:1]),
              r=[ps_ss, c["eps"]], w=[lnv])
        kb.op("act", lambda: nc.scalar.activation(out=rs[:, :n], in_=lnv[:, :n], func=AF.Exp, scale=-0.5), r=[lnv], w=[rs])
        kb.op("dve", lambda: nc.vector.scalar_tensor_tensor(out=dst_ap, in0=P[:, :n], scalar=gcol_ap, in1=rs[:, :n], op0=ALU.mult, op1=ALU.mult),
              r=[P, gt, rs], w=[dst_t])

    def headnorm_g(self, P, n, gcol_ap, gt, dst_ap, dst_t, W):
        nc, kb, c = self.nc, self.kb, self.c
        sq, ps_ss, lnv, rs = W["sq"], W["ps_ss"], W["lnv"], W["rs"]
        yield
        kb.op("act", lambda: nc.scalar.activation(out=sq[:, :n], in_=P[:, :n], func=AF.Square), r=[P], w=[sq])
        yield
        kb.op("pe", lambda: nc.tensor.matmul(ps_ss[:, :n], lhsT=c["bones"][:], rhs=sq[:, :n], start=True, stop=True), r=[sq, c["bones"]], w=[ps_ss])
        yield
        kb.op("act", lambda: nc.scalar.activation(out=lnv[:, :n], in_=ps_ss[:, :n], func=AF.Ln, scale=1.0 / HD, bias=c["eps"][:, 0:1]),
              r=[ps_ss, c["eps"]], w=[lnv])
        kb.op("act", lambda: nc.scalar.activation(out=rs[:, :n], in_=lnv[:, :n], func=AF.Exp, scale=-0.5), r=[lnv], w=[rs])
        yield
        kb.op("dve", lambda: nc.vector.scalar_tensor_tensor(out=dst_ap, in0=P[:, :n], scalar=gcol_ap, in1=rs[:, :n], op0=ALU.mult, op1=ALU.mult),
              r=[P, gt, rs], w=[dst_t])
        yield

    def phase_m1(self, l):
        cfg, nc, kb, d, c = self.cfg, self.nc, self.kb, self.d, self.c
        L, NCH = cfg.L, cfg.NCH
        blocks = cfg.blocks()
        with ExitStack() as es:
            w_in = kb.sb(es, "w_in", [128, 8, D_IN], BF16)
            self.load_w_bf16(w_in, d["w_in"][l], 8)
            n1g = kb.sb(es, "n1g", [128, 8], F32)
            kb.dma("sp", n1g[:], d["n1g"][l], w=[n1g])
            hg = kb.sb(es, "hg", [128, 4], F32)
            kb.dma("sp", hg[:], d["hg"][l], w=[hg])
            hg8 = kb.sb(es, "hg8", [128, 1], F32)
            kb.op("dve", lambda: nc.vector.tensor_scalar(out=hg8[:], in0=hg[:, 0:1], scalar1=HD ** -0.5, scalar2=None, op0=ALU.mult), r=[hg], w=[hg8])
            KT = [kb.sb(es, "KT%d" % p, [128, L], BF16) for p in range(3)]
            KT_t = [[T(None) for _ in blocks] for p in range(3)]
            Vt = kb.sb(es, "Vtok", [128, NCH, SBW], BF16)
            Vt_t = [T(None) for _ in blocks]
            hT_sb = kb.sb(es, "hT_sb", [128, 8, 512], F32)
            hsq = kb.sb(es, "hsq", [128, 8, 512], BF16)
            hn = kb.sb(es, "hn", [128, 8, 512], BF16)
            lnv = kb.sb(es, "lnv", [128, 512], F32)
            rstd = kb.sb(es, "rstd", [128, 512], F32)
            W = {"sq": kb.sb(es, "hn_sq", [128, 512], BF16), "lnv": kb.sb(es, "hn_lnv", [128, 512], F32),
                 "rs": kb.sb(es, "hn_rs", [128, 512], F32)}
            QN = [kb.sb(es, "QN%d" % p, [128, 512], BF16) for p in range(3)]
            NQ = [kb.sb(es, "NQ%d" % p, [128, 512], BF16) for p in range(3)]
            ebf = [[kb.sb(es, "ebf%d_%d" % (hh, i), [128, 512], BF16) for i in range(2)] for hh in range(2)]
            xc = [[kb.sb(es, "xc%d_%d" % (hh, i), [128, 512], BF16) for i in range(2)] for hh in range(2)]
            sp = [[kb.sb(es, "sp%d_%d" % (hh, i), [128, 512], BF16) for i in range(3)] for hh in range(2)]
            wt = [[kb.sb(es, "wt%d_%d" % (hh, i), [128, 512], BF16) for i in range(2)] for hh in range(2)]
            mixblk = [kb.sb(es, "mixblk%d" % i, [128, 512], BF16) for i in range(2)]
            ublk = kb.sb(es, "ublk", [128, 2, 512], BF16)
            rog = kb.sb(es, "rog", [128, 3], F32)
            kb.dma("sp", rog[:], d["rog"][l], w=[rog])
            cosb = kb.sb(es, "cosb", [128, 512], F32)
            sinb = kb.sb(es, "sinb", [128, 512], F32)
            rn = kb.sb(es, "rn", [128, 512], BF16)
            tmp1 = kb.sb(es, "tmp1", [128, 512], F32)
            tmp2 = kb.sb(es, "tmp2", [128, 512], F32)
            kp = [kb.sb(es, "kp%d" % p, [128, 512], BF16) for p in range(3)]
            qp = [kb.sb(es, "qp%d" % p, [128, 2, 512], BF16) for p in range(3)]
            for p in range(3):
                kb.op("dve", lambda: nc.vector.memset(qp[p][:], 0.0), w=[qp[p]])
            kptok = [kb.sb(es, "kptok%d" % p, [128, 4, 128], BF16) for p in range(3)]
            rvt = kb.sb(es, "rvt", [128, 4, RETW], BF16)
            gate = [kb.sb(es, "gate%d" % p, [128, 512], F32) for p in range(3)]
            scm = kb.sb(es, "scm", [128, 2, 128], BF16)
            st32 = [kb.sb(es, "st32_%d" % p, [128, HD], F32) for p in range(3)]
            stbf = [kb.sb(es, "stbf_%d" % p, [128, HD], BF16) for p in range(3)]
            ps = [kb.ps(es, "m1ps%d" % i, [128, 512], F32) for i in range(8)]
            X0, X1, X2 = ps[5], ps[6], ps[7]
            W["ps_ss"] = X1
            QN2 = [QN, [kb.sb(es, "QNb%d" % p, [128, 512], BF16) for p in range(3)]]
            xh = kb.sb(es, "xh", [128, 512], F32) if (l == 0 and FUSE_IN) else None
            mixc = [0]

            def next_mb():
                mb = mixblk[mixc[0] % 2]
                mixc[0] += 1
                return mb

            def stage_p(s, bi, part=0):
                t0, n = blocks[bi]
                g0 = self.tok0(s, t0)
                nq = n // CH
                qc0 = t0 // CH
                QNc = QN2[bi % 2]
                def proj_fm(P, col0):
                    for k in range(8):
                        kb.op("pe", lambda: nc.tensor.matmul(P[:, :n], lhsT=w_in[:, k, col0:col0 + 128], rhs=hn[:, k, :n], start=(k == 0), stop=(k == 7)),
                              r=[w_in, hn], w=[P])

                if part in (0, 1):
                    if bi == 0:
                        for p in range(3):
                            kb.op("dve", lambda: nc.vector.memset(st32[p][:], 0.0), w=[st32[p]])
                            kb.op("dve", lambda: nc.vector.memset(stbf[p][:], 0.0), w=[stbf[p]])
                    if l == 0 and FUSE_IN:
                        for j in range(nq):
                            ch = qc0 + j
                            for half in range(2):
                                fs_ = slice(half * 512, (half + 1) * 512)
                                if ch == 0:
                                    kb.op("dve", lambda: nc.vector.memset(xh[:], 0.0), w=[xh])
                                    kb.dma("sp", xh[PAD_FRONT:128, :], d["meta"][:, fs_], w=[xh])
                                else:
                                    kb.dma("sp", xh[:], d["x"][s, (ch - 1) * CH:ch * CH, fs_], w=[xh])
                                for q in range(4):
                                    kb.op("pe", lambda: nc.tensor.transpose(out=X2[:, q * 128:(q + 1) * 128], in_=xh[:, q * 128:(q + 1) * 128], identity=c["ident_f"][:]),
                                          r=[xh, c["ident_f"]], w=[X2])
                                kb.op("act", lambda: nc.scalar.copy(out=hT_sb[:, 4 * half:4 * half + 4, j * CH:(j + 1) * CH],
                                                                    in_=X2[:, :].rearrange("p (q t) -> p q t", t=CH)), r=[X2], w=[hT_sb])
                                yield
                        kb.dma("sp", self.hT[:, :, g0:g0 + n].rearrange("k p t -> p k t"), hT_sb[:, :, :n], r=[hT_sb], w=[self.hT_t[s][bi]])
                    else:
                        kb.dma("sp", hT_sb[:, :, :n], self.hT[:, :, g0:g0 + n].rearrange("k p t -> p k t"), r=[self.hT_t[s][bi]], w=[hT_sb])
                    self.rmsnorm_fm(n, hT_sb, hT_sb, n1g, hsq, X1, lnv, rstd, hn)
                    yield
                for p in (range(3) if part in (0, 1) else []):
                    proj_fm(X0, C_SK + p * 128)
                    yield
                    proj_fm(X2, C_SQ + p * 128)
                    yield from self.headnorm_g(X0, n, hg[:, 1:2], hg, KT[p][:, t0:t0 + n], KT_t[p][bi], W)
                    yield from self.headnorm_g(X2, n, hg8[:, 0:1], hg8, QNc[p][:, :n], QNc[p], W)
                for j in (range(nq) if part in (0, 1) else []):
                    for k in range(8):
                        kb.op("pe", lambda: nc.tensor.matmul(X2[:, :SBW], lhsT=hn[:, k, j * CH:(j + 1) * CH], rhs=w_in[:, k, C_SV:C_SV + SBW],
                                                             start=(k == 0), stop=(k == 7)), r=[w_in, hn], w=[X2])
                    kb.op("act", lambda: nc.scalar.copy(out=Vt[:, qc0 + j, :], in_=X2[:, :SBW]), r=[X2], w=[Vt_t[bi]])
                    yield
                if part == 1:
                    return
                for uh in range(2):
                    proj_fm(X0, C_U + uh * 128)
                    kb.op("act", lambda: nc.scalar.copy(out=ublk[:, uh, :n], in_=X0[:, :n]), r=[X0], w=[ublk])
                    yield
                kb.dma("sp", self.uT[:, :, g0:g0 + n].rearrange("k p t -> p k t"), ublk[:, :, :n], r=[ublk], w=[self.u_t[s]])
                if not cfg.do_ret:
                    return
                kb.dma("sp", cosb[:, :n], d["c_cosT"][:, t0:t0 + n], w=[cosb])
                kb.dma("sp", sinb[:, :n], d["c_sinT"][:, t0:t0 + n], w=[sinb])

                def rot(dst, gi, dec, p, padded=False):
                    yield from self.headnorm_g(X0, n, hg[:, gi:gi + 1], hg, rn[:, :n], rn, W)
                    kb.op("pe", lambda: nc.tensor.matmul(X2[:, :n], lhsT=c["PT"][:], rhs=rn[:, :n], start=True, stop=True), r=[c["PT"], rn], w=[X2])
                    kb.op("dve", lambda: nc.vector.tensor_tensor(out=tmp1[:, :n], in0=rn[:, :n], in1=cosb[:, :n], op=ALU.mult), r=[rn, cosb], w=[tmp1])
                    yield
                    kb.op("dve", lambda: nc.vector.tensor_tensor(out=tmp2[:, :n], in0=X2[:, :n], in1=sinb[:, :n], op=ALU.mult), r=[X2, sinb], w=[tmp2])
                    kb.op("dve", lambda: nc.vector.tensor_tensor(out=tmp1[:, :n], in0=tmp1[:, :n], in1=tmp2[:, :n], op=ALU.add), r=[tmp1, tmp2], w=[tmp1])
                    if padded:
                        for hh in range(2):
                            rs_ = slice(64 * hh, 64 * hh + 64)
                            kb.op("dve", lambda: nc.vector.tensor_tensor(out=dst[rs_, hh, :n].rearrange("p (j i) -> p j i", i=CH),
                                                                          in0=tmp1[rs_, :n].rearrange("p (j i) -> p j i", i=CH),
                                                                          in1=dec[rs_, p, :].unsqueeze(1).broadcast_to([64, nq, CH]), op=ALU.mult),
                                  r=[tmp1, dec], w=[dst])
                    else:
                        kb.op("dve", lambda: nc.vector.tensor_tensor(out=dst[:, :n].rearrange("p (j i) -> p j i", i=CH),
                                                                      in0=tmp1[:, :n].rearrange("p (j i) -> p j i", i=CH),
                                                                      in1=dec[:, p, :].unsqueeze(1).broadcast_to([128, nq, CH]), op=ALU.mult),
                              r=[tmp1, dec], w=[dst])

                for p in range(3):
                    proj_fm(X0, C_RK + p * 128)
                    yield from rot(kp[p], 3, c["kdec"], p)
                    yield
                    proj_fm(X0, C_RQ + p * 128)
                    yield from rot(qp[p], 2, c["qdec"], p, padded=True)
                    yield
                    pst = X2[:].bitcast(BF16)
                    for j in range(nq):
                        kb.op("pe", lambda: nc.tensor.transpose(out=pst[:, j * CH:(j + 1) * CH], in_=kp[p][:, j * CH:(j + 1) * CH], identity=c["ident"][:]),
                              r=[kp[p], c["ident"]], w=[X2])
                    kb.op("act", lambda: nc.scalar.copy(out=kptok[p][:, :nq, :].rearrange("p j f -> p (j f)"), in_=pst[:, :n]), r=[X2], w=[kptok[p]])
                    yield
                    proj_fm(X0, C_RG + p * 128)
                    yield
                    kb.op("act", lambda: nc.scalar.activation(out=tmp1[:, :n], in_=X0[:, :n], func=AF.Exp, scale=-1.0), r=[X0], w=[tmp1])
                    yield
                    kb.op("dve", lambda: nc.vector.tensor_scalar(out=tmp1[:, :n], in0=tmp1[:, :n], scalar1=1.0, scalar2=None, op0=ALU.add), r=[tmp1], w=[tmp1])
                    kb.op("dve", lambda: nc.vector.reciprocal(out=tmp1[:, :n], in_=tmp1[:, :n]), r=[tmp1], w=[tmp1])
                    kb.op("dve", lambda: nc.vector.tensor_tensor(out=gate[p][:, :n], in0=X0[:, :n], in1=tmp1[:, :n], op=ALU.mult), r=[X0, tmp1], w=[gate[p]])
                    yield
                for j in range(nq):
                    for k in range(8):
                        kb.op("pe", lambda: nc.tensor.matmul(X2[:, :RETW], lhsT=hn[:, k, j * CH:(j + 1) * CH], rhs=w_in[:, k, C_RV:C_RV + RETW],
                                                             start=(k == 0), stop=(k == 7)), r=[w_in, hn], w=[X2])
                    kb.op("act", lambda: nc.scalar.copy(out=rvt[:, j, :], in_=X2[:, :RETW]), r=[X2], w=[rvt])
                    yield
                for p in range(3):
                    po = X2
                    for j in range(nq):
                        jc = slice(j * CH, (j + 1) * CH)
                        kb.op("pe", lambda: nc.tensor.matmul(X0[:, 0:2 * CH].rearrange("p (a i) -> p a i", i=CH), lhsT=kp[p][:, jc], rhs=qp[p][:, :, jc], start=True, stop=True),
                              r=[kp[p], qp[p]], w=[X0])
                        for hh in range(2):
                            kb.op("dve", lambda: nc.vector.tensor_tensor(out=scm[:, hh, :], in0=X0[:, hh * CH:(hh + 1) * CH], in1=c["mret"][:], op=ALU.mult),
                                  r=[X0, c["mret"]], w=[scm])
                        for hh in range(2):
                            h = 2 * p + hh
                            rs_ = slice(64 * hh, 64 * hh + 64)
                            kb.op("pe", lambda: nc.tensor.matmul(po[rs_, jc], lhsT=rvt[:, j, h * HD:(h + 1) * HD], rhs=scm[:, hh, :], start=True, stop=False),
                                  r=[rvt, scm], w=[po])
                            kb.op("pe", lambda: nc.tensor.matmul(po[rs_, jc], lhsT=stbf[p][:, :], rhs=qp[p][:, hh, jc], start=False, stop=True),
                                  r=[stbf[p], qp[p]], w=[po])
                        kb.op("pe", lambda: nc.tensor.matmul(X1[:, 0:CH], lhsT=kptok[p][:, j, :], rhs=rvt[:, j, p * CH:(p + 1) * CH], start=True, stop=True),
                              r=[kptok[p], rvt], w=[X1])
                        for hh in range(2):
                            rs_ = slice(64 * hh, 64 * hh + 64)
                            kb.op("dve", lambda: nc.vector.tensor_tensor(out=st32[p][rs_, :], in0=st32[p][rs_, :], in1=X1[rs_, hh * HD:(hh + 1) * HD], op=ALU.add),
                                  r=[st32[p], X1], w=[st32[p]])
                        kb.op("dve", lambda: nc.vector.tensor_scalar(out=st32[p][:], in0=st32[p][:], scalar1=c["cdec"][:, p:p + 1], scalar2=None, op0=ALU.mult),
                              r=[st32[p], c["cdec"]], w=[st32[p]])
                        kb.op("dve", lambda: nc.vector.tensor_copy(out=stbf[p][:], in_=st32[p][:]), r=[st32[p]], w=[stbf[p]])
                        yield
                    yield from self.headnorm_g(po, n, rog[:, p:p + 1], rog, tmp2[:, :n], tmp2, W)
                    mb = next_mb()
                    kb.op("dve", lambda: nc.vector.tensor_tensor(out=mb[:, :n], in0=tmp2[:, :n], in1=gate[p][:, :n], op=ALU.mult), r=[tmp2, gate[p]], w=[mb])
                    kb.dma("sp", self.mixT[3 + p, :, g0:g0 + n], mb[:, :n], r=[mb], w=[self.mix_t[s][bi]])
                    yield

            def stage_sb(s, bi):
                t0, n = blocks[bi]
                g0 = self.tok0(s, t0)
                nq = n // CH
                qc0 = t0 // CH
                QNc = QN2[bi % 2]
                kt_all = lambda p: [KT_t[p][i] for i in range(bi + 1)]
                vt_all = [Vt_t[i] for i in range(bi + 1)]
                kcs = list(range(qc0 + nq - 1, -1, -1))
                NST = len(kcs)
                ob = ps[4]

                def geo(kc):
                    col0 = max(0, kc - qc0) * CH
                    return col0, slice(col0, n), slice(kc * CH, (kc + 1) * CH), kc >= qc0

                for p in range(3):
                    def st_z(i):
                        col0, cols, kcols, diag = geo(kcs[i])
                        for hh in range(2):
                            rs_ = slice(64 * hh, 64 * hh + 64)
                            z = ps[hh]
                            kb.op("pe", lambda: nc.tensor.matmul(z[:, cols], lhsT=KT[p][rs_, kcols], rhs=QNc[p][rs_, cols], start=True, stop=True),
                                  r=kt_all(p) + [QNc[p]], w=[z])

                    def st_e(i):
                        kc = kcs[i]
                        col0, cols, kcols, diag = geo(kc)
                        dc = slice(col0, col0 + CH)
                        bias = c["padb"][:, 0:1] if kc == 0 else 0.0
                        for hh in range(2):
                            z = ps[hh]
                            eb = ebf[hh][i % 2]
                            kb.op("act", lambda: nc.scalar.activation(out=eb[:, cols], in_=z[:, cols], func=AF.Exp, bias=bias), r=[z, c["padb"]], w=[eb])
                        if diag:
                            for hh in range(2):
                                eb = ebf[hh][i % 2]
                                kb.op("dve", lambda: nc.vector.tensor_tensor(out=eb[:, dc], in0=eb[:, dc], in1=c["msb"][:], op=ALU.mult),
                                      r=[eb, c["msb"]], w=[eb])
                        for hh in range(2):
                            eb = ebf[hh][i % 2]
                            spc = sp[hh][i % 3]
                            kb.op("act", lambda: nc.scalar.activation(out=spc[:, cols], in_=eb[:, cols], func=AF.Ln, bias=1.0), r=[eb], w=[spc])

                    def st_acc(i):
                        kc = kcs[i]
                        col0, cols, kcols, diag = geo(kc)
                        for hh in range(2):
                            Bk = ps[2 + hh]
                            spc = sp[hh][i % 3]
                            if i > 0:
                                pcol0, pcols, pkcols, _ = geo(kcs[i - 1])
                                psp = sp[hh][(i - 1) % 3]
                                kb.op("pe", lambda: nc.tensor.matmul(Bk[:, pcols], lhsT=c["Ls"][:], rhs=psp[:, pcols], start=False, stop=False, skip_group_check=True),
                                      r=[c["Ls"], psp], w=[Bk])
                            kb.op("pe", lambda: nc.tensor.matmul(Bk[:, cols], lhsT=c["U"][:], rhs=spc[:, cols], start=(i == 0), stop=(kc == 0), skip_group_check=True),
                                  r=[c["U"], spc], w=[Bk])

                    def st_x(i):
                        col0, cols, kcols, diag = geo(kcs[i])
                        for hh in range(2):
                            Bk = ps[2 + hh]
                            x_ = xc[hh][i % 2]
                            kb.op("act", lambda: nc.scalar.activation(out=x_[:, cols], in_=Bk[:, cols], func=AF.Exp, scale=-1.0), r=[Bk], w=[x_])

                    def st_w(i):
                        col0, cols, kcols, diag = geo(kcs[i])
                        for hh in range(2):
                            wc, eb, x_ = wt[hh][i % 2], ebf[hh][i % 2], xc[hh][i % 2]
                            kb.op("dve", lambda: nc.vector.tensor_tensor(out=wc[:, cols], in0=eb[:, cols], in1=x_[:, cols], op=ALU.mult),
                                  r=[eb, x_], w=[wc])

                    def st_pv(i):
                        kc = kcs[i]
                        col0, cols, kcols, diag = geo(kc)
                        for hh in range(2):
                            h = 2 * p + hh
                            rs_ = slice(64 * hh, 64 * hh + 64)
                            wc = wt[hh][i % 2]
                            kb.op("pe", lambda: nc.tensor.matmul(ob[rs_, cols], lhsT=Vt[:, kc, h * HD:(h + 1) * HD], rhs=wc[:, cols],
                                                                 start=(i == 0), stop=(kc == 0), skip_group_check=True), r=vt_all + [wc], w=[ob])

                    st_z(0)
                    st_e(0)
                    if NST > 1:
                        st_z(1)
                    for tau in range(NST):
                        st_acc(tau)
                        if tau + 1 < NST:
                            st_e(tau + 1)
                        if tau + 2 < NST:
                            st_z(tau + 2)
                        st_x(tau)
                        st_w(tau)
                        if tau >= 1:
                            st_pv(tau - 1)
                        yield
                    st_pv(NST - 1)
                    mb = next_mb()
                    kb.op("act", lambda: nc.scalar.copy(out=mb[:, :n], in_=ob[:, :n]), r=[ob], w=[mb])
                    kb.dma("sp", self.mixT[p, :, g0:g0 + n], mb[:, :n], r=[mb], w=[self.mix_t[s][bi]])
                    yield

            def drain(g):
                for _ in g:
                    pass

            flat = [(s, bi) for s in range(cfg.NS) for bi in range(len(blocks))]

            def chain(*gens):
                for g in gens:
                    for _ in g:
                        yield

            NP_EST = 170 if cfg.do_ret else 50
            for idx, (s, bi) in enumerate(flat):
                if bi == 0:
                    drain(stage_p(s, bi, part=1))
                sbg = stage_sb(s, bi)
                gens = [stage_p(s, bi, part=2)]
                if bi + 1 < len(blocks):
                    gens.append(stage_p(s, bi + 1, part=1))
                pg = chain(*gens)
                if INTERLEAVE:
                    t0, n = blocks[bi]
                    n_sb = 3 * (t0 // CH + n // CH + 1)
                    per = max(1, -(-NP_EST // n_sb))
                    for _ in sbg:
                        for _k in range(per):
                            next(pg, None)
                    drain(pg)
                else:
                    drain(pg)
                    drain(sbg)
            kb.barrier()


def _col(v, nk):
    return np.ascontiguousarray(np.asarray(v, np.float32).reshape(nk, 128).T)


def make_in_map(cfg, cn, inp, x_shard):
    DEPTH = cfg.DEPTH
    m = {"x": np.ascontiguousarray(x_shard, dtype=np.float32), "meta": np.asarray(inp["meta_tokens"], np.float32)}
    m["n1g"] = np.stack([_col(inp["norm1_g"][l], 8) for l in range(DEPTH)])
    m["n2g"] = np.stack([_col(inp["norm2_g"][l], 8) for l in range(DEPTH)])
    hg = np.zeros((DEPTH, 128, 4), np.float32)
    for l in range(DEPTH):
        for j, nm in enumerate(["sb_q_g", "sb_k_g", "ret_q_g", "ret_k_g"]):
            hg[l, :, j] = np.tile(np.asarray(inp[nm][l], np.float32), 2)
    m["hg"] = hg
    m["rog"] = np.stack([_col(inp["ret_out_g"][l], 3) for l in range(DEPTH)])
    for nm in ["w_in", "w_out", "w_gate", "w_up", "w_down"]:
        m[nm] = np.ascontiguousarray(np.asarray(inp[nm], np.float32)[:DEPTH])
    G = 16
    def colq(a):
        return np.ascontiguousarray(np.asarray(a, np.float32).reshape(8, 2, 64).transpose(1, 2, 0).reshape(128, 8))
    lam_r, lam_i, ldt = [], [], []
    Bcr, Bci, Ccr, Cci, dcol = [], [], [], [], []
    for l in range(DEPTH):
        lam_r.append(colq(inp["s5_lam_re"][l]))
        lam_i.append(colq(inp["s5_lam_im"][l]))
        ldt.append(colq(np.repeat(np.asarray(inp["s5_log_dt"][l], np.float32)[:, None], 64, axis=1)))
        def padB(b):
            out = np.zeros((128, 8, 128), np.float32)
            for g in range(G):
                j, gl, gi = g // 2, g % 2, g % 8
                out[gl * 64:(gl + 1) * 64, j, gi * 16:(gi + 1) * 16] = b[g]
            return out
        def padC(cm):
            out = np.zeros((128, 8, 128), np.float32)
            for g in range(G):
                j, gl, gi = g // 2, g % 2, g % 8
                out[gl * 64:(gl + 1) * 64, j, gi * 16:(gi + 1) * 16] = cm[g].T
            return out
        Bcr.append(padB(np.asarray(inp["s5_b_re"][l], np.float32)))
        Bci.append(padB(np.asarray(inp["s5_b_im"][l], np.float32)))
        Ccr.append(padC(np.asarray(inp["s5_c_re"][l], np.float32)))
        Cci.append(padC(np.asarray(inp["s5_c_im"][l], np.float32)))
        dcol.append(_col(inp["s5_d"][l], 2))
    m["lam_r"], m["lam_i"], m["ldt"] = np.stack(lam_r), np.stack(lam_i), np.stack(ldt)
    m["Bcr"], m["Bci"], m["Ccr"], m["Cci"] = np.stack(Bcr), np.stack(Bci), np.stack(Ccr), np.stack(Cci)
    m["s5d"] = np.stack(dcol)
    m["w_glu"] = np.ascontiguousarray(np.asarray(inp["s5_w_glu"], np.float32)[:DEPTH])
    for k, v in cn.items():
        m["c_" + k] = v
    return m


TWO_PI_LO = 6.2831845
MAGIC = 12582912.0
GELU_C = math.sqrt(2.0 / math.pi)


def _phase_s5(self, l):
    cfg, nc, kb, d, c = self.cfg, self.nc, self.kb, self.d, self.c
    L = cfg.L
    NB = L // 8
    blocks = cfg.blocks()

    def tt(out, a, b, op, r, w, e="dve"):
        kb.op(e, lambda: self.eng_of(e).tensor_tensor(out=out, in0=a, in1=b, op=op), r=r, w=w)

    def ts(out, a, s1, op0, r, w, s2=None, op1=None):
        if op1 is None:
            kb.op("dve", lambda: nc.vector.tensor_scalar(out=out, in0=a, scalar1=s1, scalar2=None, op0=op0), r=r, w=w)
        else:
            kb.op("dve", lambda: nc.vector.tensor_scalar(out=out, in0=a, scalar1=s1, scalar2=s2, op0=op0, op1=op1), r=r, w=w)

    def act(out, a, func, r, w, **kw):
        kb.op("act", lambda: nc.scalar.activation(out=out, in_=a, func=func, **kw), r=r, w=w)

    with ExitStack() as es:
        BP = kb.sb(es, "BP", [128, 8, 2, 8, 128], BF16)
        CP = kb.sb(es, "CP", [128, 8, 2, 8, 128], BF16)
        Kt = kb.sb(es, "Ktap", [128, 8, 2, 128], BF16)
        cosn = kb.sb(es, "cosn", [128, 8, NB], F32)
        sinn = kb.sb(es, "sinn", [128, 8, NB], F32)
        m8 = kb.sb(es, "m8", [128, 8], F32)
        dcol = kb.sb(es, "s5dcol", [128, 2], F32)
        kb.dma("sp", dcol[:], d["s5d"][l], w=[dcol])
        wglu = kb.sb(es, "wglu", [128, 2, S5W], BF16)
        self.load_w_bf16(wglu, d["w_glu"][l], 2)
        ps = [kb.ps(es, "s5ps%d" % i, [128, 512], F32) for i in range(8)]
        with ExitStack() as pes:
            P8 = kb.sb(pes, "P8", [128, 24, 8], F32)
            pwr = kb.sb(pes, "pwr", [128, 9, 8], F32)
            pwi = kb.sb(pes, "pwi", [128, 9, 8], F32)
            iota = kb.sb(pes, "iota", [128, NB], F32)
            kb.dma("sp", iota[:], d["c_iota"], w=[iota])
            big = [kb.sb(pes, "s5big%d" % i, [128, 8, 128], F32) for i in range(11)]
            Bcr, Bci, Ccr, Cci, nCi, Bbr, Bbi, Er, Ei, T1, T2 = big
            for tl, nm in [(Bcr, "Bcr"), (Bci, "Bci"), (Ccr, "Ccr"), (Cci, "Cci")]:
                kb.dma("sp", tl[:], d[nm][l], w=[tl])
            nbt = [kb.sb(pes, "s5nb%d" % i, [128, NB], F32) for i in range(4)]
            V = lambda i: P8[:, i, :]
            LR, LI, LDT, DT, LRDT, MAG, R, SIN, COS, AR, AI, DEN, AM1, FR, FI, X1, X2, X3, F8 = range(19)
            kb.dma("sp", V(LR), d["lam_r"][l], w=[P8])
            kb.dma("sp", V(LI), d["lam_i"][l], w=[P8])
            kb.dma("sp", V(LDT), d["ldt"][l], w=[P8])
            p8 = [P8]
            act(V(DT), V(LDT), AF.Exp, p8, p8)
            tt(V(LRDT), V(LR), V(DT), ALU.mult, p8, p8)
            act(V(MAG), V(LRDT), AF.Exp, p8, p8)
            act(m8[:], V(LRDT), AF.Exp, p8, [m8], scale=8.0)
            tt(V(R), V(LI), V(DT), ALU.mult, p8, p8)
            ts(V(R), V(R), 1.0 / (2.0 * math.pi), ALU.mult, p8, p8)

            def red_sin(dst, dst_t, src, src_t, tmpa, tmpb, tmp_t, shift=0.0):
                if shift != 0.0:
                    ts(tmpb, src, shift, ALU.add, src_t + tmp_t, tmp_t)
                    src = tmpb
                ts(tmpa, src, MAGIC, ALU.add, src_t + tmp_t, tmp_t)
                ts(tmpa, tmpa, MAGIC, ALU.subtract, tmp_t, tmp_t)
                tt(tmpa, src, tmpa, ALU.subtract, src_t + tmp_t, tmp_t)
                act(dst, tmpa, AF.Sin, tmp_t, dst_t, scale=TWO_PI_LO)

            red_sin(V(SIN), p8, V(R), p8, V(X1), V(X2), p8)
            red_sin(V(COS), p8, V(R), p8, V(X1), V(X2), p8, shift=0.25)
            tt(V(AR), V(MAG), V(COS), ALU.mult, p8, p8)
            tt(V(AI), V(MAG), V(SIN), ALU.mult, p8, p8)
            tt(V(DEN), V(LR), V(LR), ALU.mult, p8, p8)
            tt(V(X1), V(LI), V(LI), ALU.mult, p8, p8)
            tt(V(DEN), V(DEN), V(X1), ALU.add, p8, p8)
            kb.op("dve", lambda: nc.vector.reciprocal(out=V(DEN), in_=V(DEN)), r=p8, w=p8)
            ts(V(AM1), V(AR), -1.0, ALU.add, p8, p8)
            tt(V(X1), V(AM1), V(LR), ALU.mult, p8, p8)
            tt(V(X2), V(AI), V(LI), ALU.mult, p8, p8)
            tt(V(X1), V(X1), V(X2), ALU.add, p8, p8)
            tt(V(FR), V(X1), V(DEN), ALU.mult, p8, p8)
            tt(V(X1), V(AI), V(LR), ALU.mult, p8, p8)
            tt(V(X2), V(AM1), V(LI), ALU.mult, p8, p8)
            tt(V(X1), V(X1), V(X2), ALU.subtract, p8, p8)
            tt(V(FI), V(X1), V(DEN), ALU.mult, p8, p8)
            pw = [pwr, pwi]
            kb.op("dve", lambda: nc.vector.memset(pwr[:, 0, :], 1.0), w=[pwr])
            kb.op("dve", lambda: nc.vector.memset(pwi[:, 0, :], 0.0), w=[pwi])
            kb.op("dve", lambda: nc.vector.tensor_copy(out=pwr[:, 1, :], in_=V(AR)), r=p8, w=[pwr])
            kb.op("dve", lambda: nc.vector.tensor_copy(out=pwi[:, 1, :], in_=V(AI)), r=p8, w=[pwi])
            for k in range(1, 8):
                tt(V(X1), pwr[:, k, :], V(AR), ALU.mult, p8 + pw, p8)
                tt(V(X2), pwi[:, k, :], V(AI), ALU.mult, p8 + pw, p8)
                tt(pwr[:, k + 1, :], V(X1), V(X2), ALU.subtract, p8, [pwr])
                tt(V(X1), pwr[:, k, :], V(AI), ALU.mult, p8 + pw, p8)
                tt(V(X2), pwi[:, k, :], V(AR), ALU.mult, p8 + pw, p8)
                tt(pwi[:, k + 1, :], V(X1), V(X2), ALU.add, p8, [pwi])
            ts(V(X3), V(R), 8.0, ALU.mult, p8, p8)
            ts(V(X1), V(X3), MAGIC, ALU.add, p8, p8)
            ts(V(X1), V(X1), MAGIC, ALU.subtract, p8, p8)
            tt(V(F8), V(X3), V(X1), ALU.subtract, p8, p8)
            for j in range(8):
                ts(nbt[0][:], iota[:], P8[:, F8, j:j + 1], ALU.mult, [iota, P8], [nbt[0]])
                red_sin(sinn[:, j, :], [sinn], nbt[0][:], [nbt[0]], nbt[1][:], nbt[2][:], [nbt[1], nbt[2]])
                red_sin(cosn[:, j, :], [cosn], nbt[0][:], [nbt[0]], nbt[1][:], nbt[2][:], [nbt[1], nbt[2]], shift=0.25)
            bc = lambda i: P8[:, i, :].unsqueeze(2).broadcast_to([128, 8, 128])
            pb = lambda t_, k: t_[:, k, :].unsqueeze(2).broadcast_to([128, 8, 128])
            ts(nCi[:], Cci[:], -1.0, ALU.mult, [Cci], [nCi])
            tt(T1[:], Bcr[:], bc(FR), ALU.mult, [Bcr, P8], [T1])
            tt(T2[:], Bci[:], bc(FI), ALU.mult, [Bci, P8], [T2])
            tt(Bbr[:], T1[:], T2[:], ALU.subtract, [T1, T2], [Bbr])
            tt(T1[:], Bci[:], bc(FR), ALU.mult, [Bci, P8], [T1])
            tt(T2[:], Bcr[:], bc(FI), ALU.mult, [Bcr, P8], [T2])
            tt(Bbi[:], T1[:], T2[:], ALU.add, [T1, T2], [Bbi])
            for tau in range(8):
                tt(T1[:], Bbr[:], pb(pwr, tau), ALU.mult, [Bbr, pwr], [T1])
                tt(T2[:], Bbi[:], pb(pwi, tau), ALU.mult, [Bbi, pwi], [T2], e="pool")
                tt(Er[:], T1[:], T2[:], ALU.subtract, [T1, T2], [Er])
                tt(T1[:], Bbi[:], pb(pwr, tau), ALU.mult, [Bbi, pwr], [T1])
                tt(T2[:], Bbr[:], pb(pwi, tau), ALU.mult, [Bbr, pwi], [T2], e="pool")
                tt(Ei[:], T1[:], T2[:], ALU.add, [T1, T2], [Ei])
                for h in range(2):
                    pk = ps[h]
                    for jj in range(4):
                        j = 4 * h + jj
                        kb.op("pe", lambda: nc.tensor.matmul(pk[:, 0:128], lhsT=Er[:, j, :], rhs=Ccr[:, j, :], start=(jj == 0), stop=False), r=[Er, Ccr], w=[pk])
                        kb.op("pe", lambda: nc.tensor.matmul(pk[:, 0:128], lhsT=Ei[:, j, :], rhs=nCi[:, j, :], start=False, stop=(jj == 3)), r=[Ei, nCi], w=[pk])
                    kb.op("act", lambda: nc.scalar.copy(out=Kt[:, tau, h, :], in_=pk[:, 0:128]), r=[pk], w=[Kt])
                sidx = 7 - tau
                for ri, E in enumerate([Er, Ei]):
                    for h in range(2):
                        pt = ps[2 + 2 * ri + h]
                        for jj in range(4):
                            j = 4 * h + jj
                            kb.op("pe", lambda: nc.tensor.transpose(out=pt[:, jj * 128:(jj + 1) * 128], in_=E[:, j, :], identity=c["ident_f"][:]),
                                  r=[E, c["ident_f"]], w=[pt])
                        kb.op("act", lambda: nc.scalar.copy(out=BP[:, sidx, ri, 4 * h:4 * h + 4, :].rearrange("p j f -> p (j f)"), in_=pt[:]), r=[pt], w=[BP])
            for t in range(8):
                tt(T1[:], Ccr[:], pb(pwr, t + 1), ALU.mult, [Ccr, pwr], [T1])
                tt(T2[:], Cci[:], pb(pwi, t + 1), ALU.mult, [Cci, pwi], [T2], e="pool")
                tt(CP[:, t, 0, :, :], T1[:], T2[:], ALU.subtract, [T1, T2], [CP])
                tt(T1[:], nCi[:], pb(pwr, t + 1), ALU.mult, [nCi, pwr], [T1])
                tt(T2[:], Ccr[:], pb(pwi, t + 1), ALU.mult, [Ccr, pwi], [T2], e="pool")
                tt(CP[:, t, 1, :, :], T1[:], T2[:], ALU.subtract, [T1, T2], [CP])
            kb.barrier()
        with ExitStack() as ses:
            u = kb.sb(ses, "s5u", [128, 2, L], BF16)
            W = kb.sb(ses, "s5W", [128, 2, 8, NB], F32)
            X0 = kb.sb(ses, "s5X0", [128, 2, 8, NB], BF16)
            tb = [kb.sb(ses, "s5t%d" % i, [128, max(NB, 512)], F32) for i in range(4)]
            yv = kb.sb(ses, "s5yv", [128, 2, 512], F32)
            gf = kb.sb(ses, "s5gf", [128, 2, 512], F32)
            gb = kb.sb(ses, "s5gb", [128, 2, 512], BF16)
            sob = [kb.sb(ses, "s5so%d" % i, [128, 512], BF16) for i in range(2)]
            nchk = (NB + 511) // 512
            csz = NB // nchk
            assert csz * nchk == NB
            for s in range(cfg.NS):
                kb.dma("sp", u[:], self.uT[:, :, s * L:(s + 1) * L].rearrange("k p t -> p k t"), r=[self.u_t[s]], w=[u])
                kb.op("dve", lambda: nc.vector.memset(X0[:], 0.0), w=[X0])
                for j in range(8):
                    h = j // 4
                    uv = u[:, h, :].rearrange("p (m s) -> p s m", s=8)
                    for ck in range(nchk):
                        cs = slice(ck * csz, (ck + 1) * csz)
                        for ri in range(2):
                            pS = ps[2 * ck + ri]
                            for sft in range(8):
                                kb.op("pe", lambda: nc.tensor.matmul(pS[:, :csz], lhsT=BP[:, sft, ri, j, :], rhs=uv[:, sft, cs], start=(sft == 0), stop=(sft == 7)),
                                      r=[BP, u], w=[pS])
                        pSr, pSi = ps[2 * ck], ps[2 * ck + 1]
                        tt(tb[0][:, :csz], pSr[:, :csz], cosn[:, j, cs], ALU.mult, [pSr, cosn], [tb[0]])
                        tt(tb[1][:, :csz], pSi[:, :csz], sinn[:, j, cs], ALU.mult, [pSi, sinn], [tb[1]])
                        tt(W[:, 0, j, cs], tb[0][:, :csz], tb[1][:, :csz], ALU.add, [tb[0], tb[1]], [W])
                        tt(tb[0][:, :csz], pSi[:, :csz], cosn[:, j, cs], ALU.mult, [pSi, cosn], [tb[0]])
                        tt(tb[1][:, :csz], pSr[:, :csz], sinn[:, j, cs], ALU.mult, [pSr, sinn], [tb[1]])
                        tt(W[:, 1, j, cs], tb[0][:, :csz], tb[1][:, :csz], ALU.subtract, [tb[0], tb[1]], [W])
                    for ri in range(2):
                        kb.op("dve", lambda: nc.vector.tensor_tensor_scan(out=tb[2 + ri][:, :NB], data0=m8[:, j:j + 1].broadcast_to([128, NB]), data1=W[:, ri, j, :],
                                                                          initial=0.0, op0=ALU.mult, op1=ALU.add), r=[m8, W], w=[tb[2 + ri]])
                    if NB > 1:
                        a_, b_ = slice(0, NB - 1), slice(1, NB)
                        tt(tb[0][:, a_], tb[2][:, a_], cosn[:, j, a_], ALU.mult, [tb[2], cosn], [tb[0]])
                        tt(tb[1][:, a_], tb[3][:, a_], sinn[:, j, a_], ALU.mult, [tb[3], sinn], [tb[1]])
                        tt(X0[:, 0, j, b_], tb[0][:, a_], tb[1][:, a_], ALU.subtract, [tb[0], tb[1]], [X0])
                        tt(tb[0][:, a_], tb[2][:, a_], sinn[:, j, a_], ALU.mult, [tb[2], sinn], [tb[0]])
                        tt(tb[1][:, a_], tb[3][:, a_], cosn[:, j, a_], ALU.mult, [tb[3], cosn], [tb[1]])
                        tt(X0[:, 1, j, b_], tb[0][:, a_], tb[1][:, a_], ALU.add, [tb[0], tb[1]], [X0])
                for bi, (t0, n) in enumerate(blocks):
                    g0 = self.tok0(s, t0)
                    nb, n0 = n // 8, t0 // 8
                    for h in range(2):
                        pY = ps[4 + h]
                        uv = u[:, h, t0:t0 + n].rearrange("p (m s) -> p s m", s=8)
                        for t in range(8):
                            oap = pY[:, :n].rearrange("p (m s) -> p s m", s=8)[:, t, :]
                            mm = []
                            for sft in range(t + 1):
                                mm.append((Kt[:, t - sft, h, :], uv[:, sft, :], [Kt, u]))
                            for jj in range(4):
                                j = 4 * h + jj
                                mm.append((CP[:, t, 0, j, :], X0[:, 0, j, n0:n0 + nb], [CP, X0]))
                                mm.append((CP[:, t, 1, j, :], X0[:, 1, j, n0:n0 + nb], [CP, X0]))
                            for i, (lh, rh, rr) in enumerate(mm):
                                kb.op("pe", lambda: nc.tensor.matmul(oap, lhsT=lh, rhs=rh, start=(i == 0), stop=(i == len(mm) - 1), skip_group_check=True), r=rr, w=[pY])
                        kb.op("dve", lambda: nc.vector.scalar_tensor_tensor(out=yv[:, h, :n], in0=u[:, h, t0:t0 + n], scalar=dcol[:, h:h + 1], in1=pY[:, :n],
                                                                            op0=ALU.mult, op1=ALU.add), r=[u, dcol, pY], w=[yv])
                        tt(tb[0][:, :n], yv[:, h, :n], yv[:, h, :n], ALU.mult, [yv], [tb[0]])
                        ts(tb[0][:, :n], tb[0][:, :n], 2.0 * GELU_C * 0.044715, ALU.mult, [tb[0]], [tb[0]], s2=2.0 * GELU_C, op1=ALU.add)
                        tt(tb[0][:, :n], tb[0][:, :n], yv[:, h, :n], ALU.mult, [tb[0], yv], [tb[0]])
                        act(tb[1][:, :n], tb[0][:, :n], AF.Exp, [tb[0]], [tb[1]], scale=-1.0)
                        ts(tb[1][:, :n], tb[1][:, :n], 1.0, ALU.add, [tb[1]], [tb[1]])
                        kb.op("dve", lambda: nc.vector.reciprocal(out=tb[1][:, :n], in_=tb[1][:, :n]), r=[tb[1]], w=[tb[1]])
                        tt(gf[:, h, :n], yv[:, h, :n], tb[1][:, :n], ALU.mult, [yv, tb[1]], [gf])
                        kb.op("dve", lambda: nc.vector.tensor_copy(out=gb[:, h, :n], in_=gf[:, h, :n]), r=[gf], w=[gb])
                    for ho in range(2):
                        pV = ps[6 + ho]
                        for hi in range(2):
                            kb.op("pe", lambda: nc.tensor.matmul(pV[:, :n], lhsT=wglu[:, hi, ho * 128:(ho + 1) * 128], rhs=gb[:, hi, :n], start=(hi == 0), stop=(hi == 1)),
                                  r=[wglu, gb], w=[pV])
                        act(tb[2][:, :n], pV[:, :n], AF.Exp, [pV], [tb[2]], scale=-1.0)
                        ts(tb[2][:, :n], tb[2][:, :n], 1.0, ALU.add, [tb[2]], [tb[2]])
                        kb.op("dve", lambda: nc.vector.reciprocal(out=tb[2][:, :n], in_=tb[2][:, :n]), r=[tb[2]], w=[tb[2]])
                        tt(sob[ho][:, :n], gf[:, ho, :n], tb[2][:, :n], ALU.mult, [gf, tb[2]], [sob[ho]])
                        kb.dma("sp", self.mixT[6 + ho, :, g0:g0 + n], sob[ho][:, :n], r=[sob[ho]], w=[self.mix_t[s][bi]])
            kb.barrier()


def _eng_of(self, e):
    return self.kb.eng[e]


Prog.phase_s5 = _phase_s5
Prog.eng_of = _eng_of


def _phase_m2(self, l):
    cfg, nc, kb, d, c = self.cfg, self.nc, self.kb, self.d, self.c
    blocks1 = cfg.blocks()
    TB = 256
    with ExitStack() as es:
        w_out = kb.sb(es, "w_out", [128, 8, D], BF16)
        w_gate = kb.sb(es, "w_gate", [128, 8, D_FF], BF16)
        w_up = kb.sb(es, "w_up", [128, 8, D_FF], BF16)
        w_down = kb.sb(es, "w_down", [128, NFF, D], BF16)
        self.load_w_bf16(w_out, d["w_out"][l], 8)
        self.load_w_bf16(w_gate, d["w_gate"][l], 8)
        self.load_w_bf16(w_up, d["w_up"][l], 8)
        self.load_w_bf16(w_down, d["w_down"][l], NFF)
        n2g = kb.sb(es, "n2g", [128, 8], F32)
        kb.dma("sp", n2g[:], d["n2g"][l], w=[n2g])
        hbs = [kb.sb(es, "m2_h%d" % i, [128, 8, TB], F32) for i in range(2)]
        mixbs = [kb.sb(es, "m2_mix%d" % i, [128, 8, TB], BF16) for i in range(2)]
        hns = [kb.sb(es, "m2_hn%d" % i, [128, 8, TB], BF16) for i in range(2)]
        hsq = kb.sb(es, "m2_hsq", [128, 8, TB], BF16)
        actT = kb.sb(es, "m2_act", [128, NFF, TB], BF16)
        lnv = kb.sb(es, "m2_lnv", [128, TB], F32)
        rstd = kb.sb(es, "m2_rstd", [128, TB], F32)
        ev = [kb.sb(es, "m2_e%d" % i, [128, TB], F32) for i in range(2)]
        ps = [kb.ps(es, "m2ps%d" % i, [128, 512], F32) for i in range(8)]
        last_layer = (l == cfg.DEPTH - 1) and FUSE_OUT
        otile = kb.sb(es, "m2_ot", [128, 512], F32) if last_layer else None
        work = []
        for s in range(cfg.NS):
            for bi, (tb0, nb_) in enumerate(blocks1):
                for t0 in range(tb0, tb0 + nb_, TB):
                    work.append((s, bi, t0, min(TB, tb0 + nb_ - t0)))

        def stage_a(i):
            s, bi, t0, n = work[i]
            hb, mixb, hn = hbs[i % 2], mixbs[i % 2], hns[i % 2]
            g0 = self.tok0(s, t0)
            hview = self.hT[:, :, g0:g0 + n].rearrange("k p t -> p k t")
            kb.dma("sp", mixb[:, :, :n], self.mixT[:, :, g0:g0 + n].rearrange("k p t -> p k t"), r=[self.mix_t[s][bi]], w=[mixb])
            kb.dma("sp", hb[:, :, :n], hview, r=[self.hT_t[s][bi]], w=[hb])
            yield
            for m in range(8):
                p = ps[m % 2]
                for k in range(8):
                    kb.op("pe", lambda: nc.tensor.matmul(p[:, :n], lhsT=w_out[:, k, m * 128:(m + 1) * 128], rhs=mixb[:, k, :n], start=(k == 0), stop=(k == 7)),
                          r=[w_out, mixb], w=[p])
                kb.op("dve", lambda: nc.vector.tensor_tensor(out=hb[:, m, :n], in0=p[:, :n], in1=hb[:, m, :n], op=ALU.add), r=[p, hb], w=[hb])
                yield
            kb.op("act", lambda: nc.scalar.activation(out=hsq[:, :, :n], in_=hb[:, :, :n], func=AF.Square), r=[hb], w=[hsq])
            for k in range(8):
                kb.op("pe", lambda: nc.tensor.matmul(ps[0][:, :n], lhsT=c["ones"][:], rhs=hsq[:, k, :n], start=(k == 0), stop=(k == 7)),
                      r=[hsq, c["ones"]], w=[ps[0]])
            yield
            kb.op("act", lambda: nc.scalar.activation(out=lnv[:, :n], in_=ps[0][:, :n], func=AF.Ln, scale=1.0 / D, bias=c["eps"][:, 0:1]),
                  r=[ps[0], c["eps"]], w=[lnv])
            kb.op("act", lambda: nc.scalar.activation(out=rstd[:, :n], in_=lnv[:, :n], func=AF.Exp, scale=-0.5), r=[lnv], w=[rstd])
            yield
            for k in range(8):
                kb.op("dve", lambda: nc.vector.scalar_tensor_tensor(out=hn[:, k, :n], in0=hb[:, k, :n], scalar=n2g[:, k:k + 1], in1=rstd[:, :n],
                                                                    op0=ALU.mult, op1=ALU.mult), r=[hb, n2g, rstd], w=[hn])
                if k % 2 == 1:
                    yield

        def stage_b(i):
            s, bi, t0, n = work[i]
            hb, hn = hbs[i % 2], hns[i % 2]
            g0 = self.tok0(s, t0)
            hview = self.hT[:, :, g0:g0 + n].rearrange("k p t -> p k t")
            for f in range(NFF):
                pg, pu, e = ps[2 + f % 2], ps[4 + f % 2], ev[f % 2]
                fs = slice(f * 128, (f + 1) * 128)
                for k in range(8):
                    kb.op("pe", lambda: nc.tensor.matmul(pg[:, :n], lhsT=w_gate[:, k, fs], rhs=hn[:, k, :n], start=(k == 0), stop=(k == 7)), r=[w_gate, hn], w=[pg])
                for k in range(8):
                    kb.op("pe", lambda: nc.tensor.matmul(pu[:, :n], lhsT=w_up[:, k, fs], rhs=hn[:, k, :n], start=(k == 0), stop=(k == 7)), r=[w_up, hn], w=[pu])
                kb.op("act", lambda: nc.scalar.activation(out=e[:, :n], in_=pg[:, :n], func=AF.Silu), r=[pg], w=[e])
                kb.op("dve", lambda: nc.vector.tensor_tensor(out=actT[:, f, :n], in0=pu[:, :n], in1=e[:, :n], op=ALU.mult), r=[pu, e], w=[actT])
                yield
            for m in range(8):
                p = ps[6 + m % 2]
                for f in range(NFF):
                    kb.op("pe", lambda: nc.tensor.matmul(p[:, :n], lhsT=w_down[:, f, m * 128:(m + 1) * 128], rhs=actT[:, f, :n], start=(f == 0), stop=(f == NFF - 1)),
                          r=[w_down, actT], w=[p])
                kb.op("dve", lambda: nc.vector.tensor_tensor(out=hb[:, m, :n], in0=p[:, :n], in1=hb[:, m, :n], op=ALU.add), r=[p, hb], w=[hb])
                yield
            if not last_layer:
                kb.dma("sp", hview, hb[:, :, :n], r=[hb], w=[self.hT_t[s][bi]])
                return
            for cc in range(n // CH):
                t = t0 + cc * CH
                if t < CH:
                    continue
                for half in range(2):
                    pT = ps[6 + half]
                    for q in range(4):
                        kb.op("pe", lambda: nc.tensor.transpose(out=pT[:, q * 128:(q + 1) * 128], in_=hb[:, 4 * half + q, cc * CH:(cc + 1) * CH], identity=c["ident_f"][:]),
                              r=[hb, c["ident_f"]], w=[pT])
                    kb.op("act", lambda: nc.scalar.copy(out=otile[:], in_=pT[:]), r=[pT], w=[otile])
                    kb.dma("sp", self.out[s, t - CH:t, half * 512:(half + 1) * 512], otile[:], r=[otile], w=[self.out_t])
                yield

        def drain(g):
            for _ in g:
                pass

        drain(stage_a(0))
        for i in range(len(work)):
            bg = stage_b(i)
            if i + 1 < len(work):
                ag = stage_a(i + 1)
                cnt = 0
                for _ in bg:
                    cnt += 1
                    if cnt % 2 == 0:
                        next(ag, None)
                drain(ag)
            else:
                drain(bg)
        kb.barrier()


def _phase_f(self):
    cfg, nc, kb, d, c = self.cfg, self.nc, self.kb, self.d, self.c
    blocks = cfg.blocks()
    with ExitStack() as es:
        hb = [kb.sb(es, "f_h%d" % i, [128, 8, 128], F32) for i in range(2)]
        ot = [kb.sb(es, "f_o%d" % i, [128, D], F32) for i in range(2)]
        ps = [kb.ps(es, "f_ps%d" % i, [128, 1024], F32) for i in range(2)]
        it = 0
        for s in range(cfg.NS):
            for ch in range(1, cfg.NCH):
                a, b, p = hb[it % 2], ot[it % 2], ps[it % 2]
                t = ch * CH
                bi = [i for i, (t0, n) in enumerate(blocks) if t0 <= t < t0 + n][0]
                g0 = self.tok0(s, t)
                kb.dma("sp", a[:], self.hT[:, :, g0:g0 + CH].rearrange("k p t -> p k t"), r=[self.hT_t[s][bi]], w=[a])
                for k in range(8):
                    kb.op("pe", lambda: nc.tensor.transpose(out=p[:, k * 128:(k + 1) * 128], in_=a[:, k, :], identity=c["ident_f"][:]), r=[a, c["ident_f"]], w=[p])
                kb.op("act", lambda: nc.scalar.copy(out=b[:], in_=p[:]), r=[p], w=[b])
                kb.dma("sp", self.out[s, (ch - 1) * CH:ch * CH, :], b[:], r=[b], w=[self.out_t])
                it += 1
        kb.barrier()


Prog.phase_m2 = _phase_m2
Prog.phase_f = _phase_f


_CACHE = {}


def kernel(**inputs):
    n_cores = 8
    x = np.asarray(inputs["x"], np.float32)
    B, S, _ = x.shape
    NS = B // n_cores
    NCH = S // CH + 1
    DEPTH = int(np.asarray(inputs["w_in"]).shape[0])
    key = (NS, NCH, DEPTH)
    if key not in _CACHE:
        cfg = Cfg(NS=NS, NCH=NCH, DEPTH=DEPTH)
        P = Prog(cfg)
        P.build()
        _CACHE[key] = (cfg, P)
    cfg, P = _CACHE[key]
    in_maps = [make_in_map(cfg, P.cn, inputs, x[NS * i:NS * (i + 1)]) for i in range(n_cores)]
    res = run_bass_kernel_spmd(P.nc, in_maps, core_ids=list(range(n_cores)))
    out = np.concatenate([np.asarray(res.results[i]["out"]).reshape(NS, S, D) for i in range(n_cores)], axis=0)
    return out.astype(np.float32)
```
